# Optimizing a Trainium2 kernel written in Bass

```python
import math
import jax, jax.numpy as jnp
from jax import lax
import numpy as np

D_MODEL = 2048
BATCH = 4
SEQ = 2048
DEPTH = 1

CHUNK = 64
Q_BLOCK = 128
SB_HEADS = 8
SB_HEAD_DIM = 128
D_SB = SB_HEADS * SB_HEAD_DIM
D_CONV = D_MODEL // 2
CONV_WIDTH = 31
D_FF = 4 * D_MODEL
EPS = 1e-6
IN_SPLITS = (D_SB, D_SB, D_SB, D_CONV, D_CONV, D_MODEL, D_MODEL)
D_IN = sum(IN_SPLITS)

kernel_name = "stickbreak_conformer_gated_hybrid"


def rmsnorm(x, g):
    xf = x.astype(jnp.float32)
    y = xf * lax.rsqrt(jnp.mean(xf * xf, axis=-1, keepdims=True) + EPS)
    return (y * g.astype(jnp.float32)).astype(x.dtype)


def layernorm(x, g, b):
    xf = x.astype(jnp.float32)
    mu = jnp.mean(xf, axis=-1, keepdims=True)
    var = jnp.mean(jnp.square(xf - mu), axis=-1, keepdims=True)
    y = (xf - mu) * lax.rsqrt(var + EPS)
    return (y * g.astype(jnp.float32) + b.astype(jnp.float32)).astype(x.dtype)


def stick_breaking_attention(q, k, v):
    b, s, h, dh = q.shape
    scale = 1.0 / math.sqrt(dh)
    outs = []
    for i in range(s // Q_BLOCK):
        t0, t1 = i * Q_BLOCK, (i + 1) * Q_BLOCK
        qb = q[:, t0:t1]
        kb = k[:, :t1]
        vb = v[:, :t1]
        z = jnp.einsum('bqhd,bkhd->bhqk', qb, kb).astype(jnp.float32) * scale
        tq = t0 + jnp.arange(Q_BLOCK)[:, None]
        sk = jnp.arange(t1)[None, :]
        mask = sk < tq
        log_1m = jnp.where(mask, jax.nn.log_sigmoid(-z), 0.0)
        suffix = lax.cumsum(log_1m, axis=3, reverse=True) - log_1m
        a = jnp.where(mask, jnp.exp(jax.nn.log_sigmoid(z) + suffix), 0.0)
        outs.append(jnp.einsum('bhqk,bkhd->bqhd', a.astype(vb.dtype), vb))
    return jnp.concatenate(outs, axis=1)


def causal_depthwise_conv(u, w, bias):
    c = u.shape[-1]
    y = lax.conv_general_dilated(
        u, w.reshape(CONV_WIDTH, 1, c).astype(u.dtype),
        window_strides=(1,), padding=[(CONV_WIDTH - 1, 0)],
        dimension_numbers=('NWC', 'WIO', 'NWC'), feature_group_count=c)
    return y + bias


def setup_inputs(seed: int = 0) -> dict:
    key = jax.random.key(seed)
    ks = jax.random.split(key, 17)
    L = DEPTH
    nrm = lambda k, shape, fan_in: jax.random.normal(k, shape, jnp.float32) * (fan_in ** -0.5)
    gain = lambda k, shape: 1.0 + 0.02 * jax.random.normal(k, shape, jnp.float32)
    small = lambda k, shape: 0.02 * jax.random.normal(k, shape, jnp.float32)
    return {
        "x": jax.random.normal(ks[0], (BATCH, SEQ, D_MODEL), jnp.float32),
        "g_pre_mix": gain(ks[1], (L, D_MODEL)),
        "w_in": nrm(ks[2], (L, D_MODEL, D_IN), D_MODEL),
        "b_in": small(ks[3], (L, D_IN)),
        "w_dw": nrm(ks[4], (L, CONV_WIDTH, D_CONV), CONV_WIDTH),
        "b_dw": small(ks[5], (L, D_CONV)),
        "g_conv_ln": gain(ks[6], (L, D_CONV)),
        "b_conv_ln": small(ks[7], (L, D_CONV)),
        "w_sb_out": nrm(ks[8], (L, D_SB, D_MODEL), D_SB),
        "w_conv_out": nrm(ks[9], (L, D_CONV, D_MODEL), D_CONV),
        "w_o": nrm(ks[10], (L, D_MODEL, D_MODEL), D_MODEL),
        "g_post_mix": gain(ks[11], (L, D_MODEL)),
        "g_pre_mlp": gain(ks[12], (L, D_MODEL)),
        "w_up": nrm(ks[13], (L, D_MODEL, D_FF), D_MODEL),
        "w_down": nrm(ks[14], (L, D_FF, D_MODEL), D_FF),
        "g_post_mlp": gain(ks[15], (L, D_MODEL)),
    }


def reference(x, g_pre_mix, w_in, b_in, w_dw, b_dw, g_conv_ln, b_conv_ln,
              w_sb_out, w_conv_out, w_o, g_post_mix, g_pre_mlp, w_up, w_down,
              g_post_mlp):
    b, s, _ = x.shape
    offs = np.cumsum(IN_SPLITS)[:-1].tolist()
    for l in range(DEPTH):
        h = rmsnorm(x, g_pre_mix[l])
        proj = jnp.einsum('bsd,de->bse', h, w_in[l]) + b_in[l]
        q, k, v, glu_a, glu_b, gate_sb, gate_cv = jnp.split(proj, offs, axis=-1)
        hd = (b, s, SB_HEADS, SB_HEAD_DIM)
        o_sb = stick_breaking_attention(q.reshape(hd), k.reshape(hd), v.reshape(hd))
        o_sb = jnp.einsum('bse,ed->bsd', o_sb.reshape(b, s, D_SB), w_sb_out[l])
        u = glu_a * jax.nn.sigmoid(glu_b)
        u = causal_depthwise_conv(u, w_dw[l], b_dw[l])
        u = jax.nn.silu(layernorm(u, g_conv_ln[l], b_conv_ln[l]))
        o_cv = jnp.einsum('bsc,cd->bsd', u, w_conv_out[l])
        merged = jax.nn.sigmoid(gate_sb) * o_sb + jax.nn.sigmoid(gate_cv) * o_cv
        y = jnp.einsum('bsd,de->bse', merged, w_o[l])
        x = x + rmsnorm(y, g_post_mix[l])
        h = rmsnorm(x, g_pre_mlp[l])
        f = jnp.square(jax.nn.relu(jnp.einsum('bsd,df->bsf', h, w_up[l])))
        f = jnp.einsum('bsf,fd->bsd', f, w_down[l])
        x = x + rmsnorm(f, g_post_mlp[l])
    return x
```

```python
from contextlib import ExitStack
import math
import numpy as np
import concourse.bass as bass
import concourse.mybir as mybir
from concourse.bass_utils import run_bass_kernel_spmd

F32 = mybir.dt.float32
BF16 = mybir.dt.bfloat16
AF = mybir.ActivationFunctionType
ALU = mybir.AluOpType

D = 2048
NT = 1024
DIN = 9216
DFF = 8192
EPS = 1e-6
NSLOT = 5
SCALE = 1.0 / math.sqrt(128.0)
KIB = 256


class Tile:
    __slots__ = ("name", "w", "r", "excl")

    def __init__(self, name, excl=False):
        self.name = name
        self.w = None
        self.r = []
        self.excl = excl


class Eng:
    def __init__(self, name, handle, sem):
        self.name = name
        self.h = handle
        self.sem = sem
        self.count = 0
        self.known = {}


class FW:
    def __init__(self, nc, stack):
        self.nc = nc
        self.stack = stack
        self.sems = {}
        self.engs = {}
        for name, h in (("pe", nc.tensor), ("act", nc.scalar), ("dve", nc.vector),
                        ("pool", nc.gpsimd), ("sp", nc.sync)):
            sem = stack.enter_context(nc.semaphore("sem_" + name))
            self.sems["e:" + name] = sem
            self.engs[name] = Eng(name, h, sem)
        self.dma_vals = {}
        self.n_wait = 0
        self.n_inst = 0
        self.n_by = {}
        self.marks = []

    def _need(self, eng, tickets):
        need = {}
        for t in tickets:
            if t is None:
                continue
            k, v, _ = t
            if eng.known.get(k, 0) >= v:
                continue
            if need.get(k, 0) < v:
                need[k] = v
        for k, v in need.items():
            eng.h.wait_ge(self.sems[k], v)
            eng.known[k] = v
            self.n_wait += 1

    def op(self, engname, fn, reads=(), writes=(), inc=True):
        eng = self.engs[engname]
        tickets = []
        for t in reads:
            if t.w is not None:
                tickets.append(t.w)
            if t.excl:
                for r in t.r:
                    if r[2] != engname:
                        tickets.append(r)
        same_ok = engname == "pe"
        for t in writes:
            if t.w is not None and not (same_ok and t.w[2] == engname):
                tickets.append(t.w)
            for r in t.r:
                if not (same_ok and r[2] == engname):
                    tickets.append(r)
        self._need(eng, tickets)
        ins = fn(eng.h)
        self.n_inst += 1
        self.n_by[engname] = self.n_by.get(engname, 0) + 1
        key = "e:" + engname
        if inc:
            eng.count += 1
            ins.then_inc(eng.sem, 1)
            ticket = (key, eng.count, engname)
        else:
            ticket = (key, eng.count + 1, engname)
        for t in reads:
            t.r.append(ticket)
        for t in writes:
            t.w = ticket
            t.r = []
        return ins

    def dma(self, qname, semkey, out, in_, reads=(), writes=(), **kw):
        eng = self.engs[qname]
        k = "d:" + semkey
        if k not in self.sems:
            self.sems[k] = self.stack.enter_context(self.nc.semaphore("dsem_" + semkey))
            self.dma_vals[k] = 0
        tickets = []
        for t in reads:
            if t.w is not None:
                tickets.append(t.w)
        for t in writes:
            if t.w is not None:
                tickets.append(t.w)
            tickets.extend(t.r)
        self._need(eng, tickets)
        ins = eng.h.dma_start(out=out, in_=in_, **kw)
        self.dma_vals[k] += 16
        ins.then_inc(self.sems[k], 16)
        ticket = (k, self.dma_vals[k], "dma")
        for t in reads:
            t.r.append(ticket)
        for t in writes:
            t.w = ticket
            t.r = []
        self.n_inst += 1
        return ins

    def fence(self, src_tiles, dst_tiles):
        for d in dst_tiles:
            for t in src_tiles:
                if t.w is not None:
                    d.r.append(t.w)
                d.r.extend(t.r)

    def wait_tiles(self, engname, tiles):
        eng = self.engs[engname]
        tickets = []
        for t in tiles:
            if t.w is not None:
                tickets.append(t.w)
            tickets.extend(t.r)
        self._need(eng, tickets)

    def barrier(self, extra_tiles=()):
        tickets = []
        for n in ("pe", "act", "dve", "pool"):
            e = self.engs[n]
            if e.count > 0:
                tickets.append(("e:" + n, e.count, n))
        for t in extra_tiles:
            if t.w is not None:
                tickets.append(t.w)
            tickets.extend(t.r)
        for n in ("pe", "act", "dve", "sp"):
            self._need(self.engs[n], [t for t in tickets if t[2] != n])


class _Stop(Exception):
    pass


class _Suppress:
    def __enter__(self):
        return self

    def __exit__(self, et, ev, tb):
        return et is _Stop


def build(dbg=False, stop_after=None):
    nc = bass.Bass("TRN2", target_bir_lowering=False)
    dram = {}

    def din(name, shape):
        dram[name] = nc.dram_tensor(name, list(shape), F32, kind="ExternalInput").ap()
        return dram[name]

    xin = din("xin", [2 * NT, D])
    consts_d = din("consts", [128, 384])
    gvec_d = din("gvec", [4, 128, D])
    bv_d = din("bv", [128, 1024])
    w_in = din("w_in", [D, DIN])
    w_sbo = din("w_sbo", [1024, D])
    w_cvo = din("w_cvo", [1024, D])
    w_o = din("w_o", [D, D])
    w_up = din("w_up", [D, DFF])
    w_dn = din("w_dn", [DFF, D])
    out_d = nc.dram_tensor("out", [NT, D], F32, kind="ExternalOutput").ap()
    dbg_out = {}

    with _Suppress(), ExitStack() as st:
        fw = FW(nc, st)
        arena = nc.alloc_sbuf_tensor("arena", [128, 52992], F32)
        ps = [nc.alloc_psum_tensor(f"ps{i}", [128, 512], F32) for i in range(8)]
        t_ps = [Tile(f"ps{i}", excl=True) for i in range(8)]

        def regf(off_kib, size_kib):
            a = int(round(off_kib * KIB))
            b = int(round((off_kib + size_kib) * KIB))
            return arena[:, a:b]

        def regb(off_kib, size_kib):
            return regf(off_kib, size_kib).bitcast(BF16)

        def dump(name, ap_sb, shape, tiles, dt=F32):
            if not dbg:
                return
            d = nc.dram_tensor("dbg_" + name, list(shape), dt, kind="ExternalOutput").ap()
            dbg_out[name] = d
            fw.dma("sp", "dbg", d, ap_sb, reads=tiles)
            fw.barrier(extra_tiles=tiles)

        def checkpoint(name):
            fw.marks.append((name, dict(fw.n_by)))
            if stop_after == name:
                fw.barrier()
                t_fin = Tile("fin")
                fw.dma("sp", "fin", out_d[0:128, :], regf(60, 8), reads=[t_fin])
                fw.wait_tiles("sp", [t_fin])
                raise _Stop()

        consts = regf(0, 1.5)
        t_consts = Tile("consts")
        ident = regb(1.5, 0.25)
        negU = regb(1.75, 0.25)
        negones = regb(2.0, 0.25)
        onesf = regf(2.25, 0.5)
        t_cmat = Tile("cmat")
        stats = regf(3, 1)
        epsc = stats[:, 255:256]
        hTc_last = regb(4, 4).rearrange("p (a b) -> p a b", a=16)
        t_hTc_last = Tile("hTc_last")
        masks = regb(8, 4).rearrange("p (a b) -> p a b", a=4)
        t_masks = Tile("masks")
        bvbc = regf(8, 4)
        t_bvbc = Tile("bvbc")
        bvf = None

        C_BIN, C_WDW, C_BDW, C_GLN, C_BLN, C_FLAG, C_BQS = 0, 72, 320, 328, 336, 344, 345
        flag = consts[:, C_FLAG:C_FLAG + 1]

        slot_ap = [regb(12 + 8 * i, 8) for i in range(NSLOT)]
        t_slot = [[Tile(f"slot{i}a"), Tile(f"slot{i}b")] for i in range(NSLOT)]
        wcount = [0]

        def wload(wd, r0, kc, c0, ncols):
            assert kc * ncols == 4096
            s = wcount[0] % NSLOT
            wcount[0] += 1
            view = slot_ap[s].rearrange("p (a b) -> p a b", a=kc)
            src = wd[r0:r0 + kc * 128, c0:c0 + ncols].rearrange("(a p) e -> p a e", p=128)
            fw.dma("pool", f"w{s}", view, src, writes=t_slot[s])
            return view, t_slot[s]

        def wload2(wdA, wdB, c0):
            s = wcount[0] % NSLOT
            wcount[0] += 1
            view = slot_ap[s].rearrange("p (h a b) -> p h a b", h=2, a=8)
            for hh, wd in enumerate((wdA, wdB)):
                src = wd[0:1024, c0:c0 + 256].rearrange("(a p) e -> p a e", p=128)
                fw.dma("pool", f"w{s}" + "ab"[hh], view[:, hh], src, writes=[t_slot[s][hh]])
            return view, t_slot[s]

        scol = [0]

        def stat(n=1):
            c = scol[0]
            scol[0] += n
            assert scol[0] <= 255
            return stats[:, c:c + n], Tile(f"stat{c}")

        fw.dma("sp", "consts", consts, consts_d[:, :], writes=[t_consts])
        scr = regf(188, 2)
        t_scr = Tile("scr")
        fw.op("pool", lambda e: e.memset(scr[:, 0:128], 1.0), writes=[t_scr])
        fw.op("pool", lambda e: e.affine_select(out=scr[:, 0:128], in_=scr[:, 0:128], pattern=[[-1, 128]],
                                                compare_op=ALU.is_equal, fill=0.0, base=0, channel_multiplier=1),
              reads=[t_scr], writes=[t_scr])
        fw.op("dve", lambda e: e.tensor_copy(ident, scr[:, 0:128]), reads=[t_scr], writes=[t_cmat])
        fw.op("pool", lambda e: e.memset(scr[:, 128:256], -1.0), writes=[t_scr])
        fw.op("pool", lambda e: e.affine_select(out=scr[:, 128:256], in_=scr[:, 128:256], pattern=[[-1, 128]],
                                                compare_op=ALU.is_ge, fill=0.0, base=0, channel_multiplier=1),
              reads=[t_scr], writes=[t_scr])
        fw.op("dve", lambda e: e.tensor_copy(negU, scr[:, 128:256]), reads=[t_scr], writes=[t_cmat])
        fw.op("pool", lambda e: e.memset(scr[:, 256:384], -1.0), writes=[t_scr])
        fw.op("dve", lambda e: e.tensor_copy(negones, scr[:, 256:384]), reads=[t_scr], writes=[t_cmat])
        fw.op("pool", lambda e: e.memset(onesf, 1.0 / 1024.0), writes=[t_cmat])
        fw.op("pool", lambda e: e.memset(epsc, EPS), writes=[t_cmat])
        fw.op("dve", lambda e: e.tensor_scalar(consts[:, C_BQS:C_BQS + 8], consts[:, C_BIN:C_BIN + 8], SCALE, None,
                                               ALU.mult), reads=[t_consts], writes=[t_consts])

        fw.barrier()

        checkpoint("const")
        hT = regb(60, 32).rearrange("p (a b) -> p a b", a=16)
        t_hT = [Tile(f"hT{i}") for i in range(8)]
        KT = regb(92, 32).rearrange("p (a b) -> p a b", a=8)
        t_KT = [[Tile(f"KT{h}_{q}") for q in range(4)] for h in range(8)]
        Vt = regb(124, 32).rearrange("p (a b) -> p a b", a=16)
        t_Vt = [Tile(f"Vt{i}") for i in range(16)]
        QT = regb(156, 16).rearrange("p (a b) -> p a b", a=8)
        t_QT = [[Tile(f"QT{h}_{q}") for q in range(2)] for h in range(8)]
        xbuf = [regf(172, 8), regf(180, 8)]
        t_xbuf = [Tile("xbuf0"), Tile("xbuf1")]
        hbf = [regb(188, 4), regb(192, 4)]
        t_hbf = [Tile("hbf0"), Tile("hbf1")]
        gslot = regf(196, 8)
        t_gslot = Tile("gslotA")

        bank_rr = [0]

        def next_bank():
            b = bank_rr[0] % 8
            bank_rr[0] += 1
            return b

        def rstd_from_ss(ss_ap, t_ss, n_inv, out_ap, t_out):
            fw.op("act", lambda e: e.activation(out_ap, ss_ap, AF.Sqrt, bias=epsc, scale=n_inv),
                  reads=[t_ss, t_cmat], writes=[t_out])
            fw.op("dve", lambda e: e.reciprocal(out_ap, out_ap), reads=[t_out], writes=[t_out])

        def prenorm(src_ap, t_src, hb, t_hb, g_ap, t_g):
            ss, t_ss = stat()
            rs, t_rs = stat()
            fw.op("act", lambda e: e.activation(hb, src_ap, AF.Square, accum_out=ss),
                  reads=[t_src], writes=[t_hb, t_ss])
            rstd_from_ss(ss, t_ss, 1.0 / D, rs, t_rs)
            fw.op("dve", lambda e: e.scalar_tensor_tensor(hb, src_ap, rs, g_ap, ALU.mult, ALU.mult),
                  reads=[t_src, t_rs, t_g], writes=[t_hb])

        def transpose_block(hb, t_hb, dstT, t_dst, col0, banks=None):
            for half in range(2):
                b = next_bank() if banks is None else banks[half]
                pT = ps[b][:].bitcast(BF16).rearrange("p (a b) -> p a b", a=8)
                for j in range(8):
                    kc = half * 8 + j
                    fw.op("pe", lambda e: e.transpose(pT[:, j, :], hb[:, kc * 128:(kc + 1) * 128], ident),
                          reads=[t_hb, t_cmat], writes=[t_ps[b]], inc=(j == 7))
                dst = dstT[:, half * 8:(half + 1) * 8, col0:col0 + 128]
                if half == 0:
                    fw.op("act", lambda e: e.activation(dst, pT, AF.Copy), reads=[t_ps[b]], writes=[t_dst])
                else:
                    fw.op("dve", lambda e: e.tensor_copy(dst, pT), reads=[t_ps[b]], writes=[t_dst])

        def prenorm_transpose(src_ap, t_src, i, g_ap, t_g, dstT, t_dst, col0):
            prenorm(src_ap, t_src, hbf[i], t_hbf[i], g_ap, t_g)
            transpose_block(hbf[i], t_hbf[i], dstT, t_dst, col0)

        def load_g(idx, slot, t_slot_):
            fw.dma("sp", "g" + t_slot_.name, slot, gvec_d[idx], writes=[t_slot_])

        load_g(0, gslot, t_gslot)
        fw.dma("sp", "bv", bvbc, bv_d[:, :], writes=[t_bvbc])
        bvf_ap = None

        def phaseA(row0):
            for blk in range(8):
                i = blk % 2
                fw.dma("sp", f"xb{i}", xbuf[i], xin[row0 + blk * 128: row0 + (blk + 1) * 128, :],
                       writes=[t_xbuf[i]])
                prenorm_transpose(xbuf[i], t_xbuf[i], i, gslot, t_gslot, hT, t_hT[blk], blk * 128)

        def proj_fm(wd, c0_list, dst_fn, nblk_tok, bias_col_fn, scale, func=AF.Identity):
            for ti, c0 in enumerate(c0_list):
                wv, t_w = wload(wd, 0, 16, c0, 256)
                for ec in range(2):
                    banks = [next_bank() for _ in range(nblk_tok)]
                    for kc in range(16):
                        for th in range(nblk_tok):
                            b = banks[th]
                            fw.op("pe", lambda e: e.matmul(ps[b][:], wv[:, kc, ec * 128:(ec + 1) * 128],
                                                           hT[:, kc, th * 512:(th + 1) * 512],
                                                           start=(kc == 0), stop=(kc == 15)),
                                  reads=t_w + t_hT[th * 4:(th + 1) * 4], writes=[t_ps[b]], inc=(kc == 15))
                    for th in range(nblk_tok):
                        b = banks[th]
                        dst, t_dst = dst_fn(ti * 2 + ec, th)
                        bias = bias_col_fn(ti * 2 + ec)
                        fw.op("act", lambda e: e.activation(dst, ps[b][:], func, bias=bias, scale=scale),
                              reads=[t_ps[b], t_consts], writes=[t_dst])

        def projV(blk0, is_ctx):
            for cg in range(2):
                wts = [wload(w_in, kt * 1024, 8, 2048 + cg * 512, 512) for kt in range(2)]
                for tbh in range(2):
                    banks = [next_bank() for _ in range(4)]
                    for kt in range(2):
                        wv, t_w = wts[kt]
                        for tbl in range(4):
                            tb = tbh * 4 + tbl
                            b = banks[tbl]
                            for kc in range(8):
                                fw.op("pe", lambda e: e.matmul(ps[b][:], hT[:, kt * 8 + kc, tb * 128:(tb + 1) * 128],
                                                               wv[:, kc, :], start=(kt == 0 and kc == 0),
                                                               stop=(kt == 1 and kc == 7), skip_group_check=True),
                                      reads=t_w + [t_hT[tb]], writes=[t_ps[b]], inc=(kt == 1 and kc == 7))
                    for tbl in range(4):
                        tb = tbh * 4 + tbl
                        b = banks[tbl]
                        dst = Vt[:, blk0 + tb, cg * 512:(cg + 1) * 512]
                        if is_ctx:
                            fw.op("dve", lambda e: e.scalar_tensor_tensor(dst, ps[b][:], flag,
                                                                          bvf_ap[:, cg * 512:(cg + 1) * 512],
                                                                          ALU.mult, ALU.add),
                                  reads=[t_ps[b], t_consts, t_bvf], writes=[t_Vt[blk0 + tb]])
                        else:
                            fw.op("dve", lambda e: e.tensor_tensor(dst, ps[b][:],
                                                                   bvbc[:, cg * 512:(cg + 1) * 512], ALU.add),
                                  reads=[t_ps[b], t_bvbc], writes=[t_Vt[blk0 + tb]])

        bvf_ap = regf(156, 4)
        t_bvf = Tile("bvf")
        fw.op("dve", lambda e: e.tensor_scalar(bvf_ap, bvbc, flag, None, ALU.mult),
              reads=[t_bvbc, t_consts], writes=[t_bvf])

        phaseA(0)
        fw.op("dve", lambda e: e.tensor_copy(hTc_last, hT[:, :, 896:1024]), reads=[t_hT[7]], writes=[t_hTc_last])
        proj_fm(w_in, [1024 + 256 * i for i in range(4)],
                lambda ch, th: (KT[:, ch, th * 512:(th + 1) * 512], t_KT[ch][th]), 2,
                lambda ch: consts[:, C_BIN + 8 + ch:C_BIN + 9 + ch], 1.0)
        projV(0, True)
        fw.barrier()
        checkpoint("Bctx")
        phaseA(NT)
        proj_fm(w_in, [1024 + 256 * i for i in range(4)],
                lambda ch, th: (KT[:, ch, 1024 + th * 512:1024 + (th + 1) * 512], t_KT[ch][2 + th]), 2,
                lambda ch: consts[:, C_BIN + 8 + ch:C_BIN + 9 + ch], 1.0)
        projV(8, False)
        proj_fm(w_in, [256 * i for i in range(4)],
                lambda ch, th: (QT[:, ch, th * 512:(th + 1) * 512], t_QT[ch][th]), 2,
                lambda ch: consts[:, C_BQS + ch:C_BQS + ch + 1], SCALE)
        fw.barrier()
        dump("hT", hT, [128, 16, 1024], t_hT, BF16)
        dump("KT", KT, [128, 8, 2048], [t for r in t_KT for t in r], BF16)
        dump("Vt", Vt, [128, 16, 1024], t_Vt, BF16)
        dump("QT", QT, [128, 8, 1024], [t for r in t_QT for t in r], BF16)
        checkpoint("B")

        scr2 = regf(204, 2)
        t_scr2 = Tile("scr2")
        for j in range(4):
            fw.op("pool", lambda e: e.memset(scr2, 1.0), writes=[t_scr2])
            fw.op("pool", lambda e: e.affine_select(out=scr2, in_=scr2, pattern=[[1, 512]], compare_op=ALU.is_gt,
                                                    fill=0.0, base=-128 * j, channel_multiplier=-1),
                  reads=[t_scr2], writes=[t_scr2])
            fw.op("dve", lambda e: e.tensor_copy(masks[:, j, :], scr2), reads=[t_scr2], writes=[t_masks])
        fw.barrier()

        oT = regb(172, 16).rearrange("p (a b) -> p a b", a=8)
        t_oT = [[Tile(f"oT{h}_{q}") for q in range(2)] for h in range(8)]

        class Chain:
            pass

        NCH = 4
        chains = []
        for ci in range(NCH):
            c = Chain()
            base = 188 + 5 * ci if ci < 3 else 52
            c.sp = regb(base, 1); c.t_sp = Tile(f"c{ci}sp")
            c.S = regf(base + 1, 2); c.t_S = Tile(f"c{ci}S")
            c.Sb = regb(base + 3, 1); c.t_Sb = Tile(f"c{ci}Sb")
            c.a = regb(base + 4, 1); c.t_a = Tile(f"c{ci}a")
            c.bA = ci * 2
            c.bO = ci * 2 + 1
            chains.append(c)
        ebuf = [regf(203, 2), regf(205, 2)]
        t_ebuf = [Tile("e0"), Tile("e1")]

        def attn_group(g, heads):
            nsteps = 8 + 4 * g + 4
            qsl = slice(g * 512, (g + 1) * 512)

            def QK(ci, kb):
                c, hd = chains[ci], heads[ci]
                fw.op("pe", lambda e: e.matmul(ps[c.bA][:], KT[:, hd, kb * 128:(kb + 1) * 128], QT[:, hd, qsl],
                                               start=True, stop=True),
                      reads=[t_KT[hd][kb // 4], t_QT[hd][g]], writes=[t_ps[c.bA]])

            for ci in range(NCH):
                QK(ci, nsteps - 1)
            for s in range(nsteps):
                kb = nsteps - 1 - s
                first, last = s == 0, s == nsteps - 1
                j = kb - (8 + 4 * g)
                diag = j >= 0
                for ci in range(NCH):
                    c = chains[ci]
                    ei = ci % 2
                    fw.op("act", lambda e: e.activation(ebuf[ei], ps[c.bA][:], AF.Exp),
                          reads=[t_ps[c.bA]], writes=[t_ebuf[ei]])
                    fw.op("act", lambda e: e.activation(c.sp, ebuf[ei], AF.Ln, bias=1.0),
                          reads=[t_ebuf[ei]], writes=[c.t_sp])
                    if diag:
                        fw.op("dve", lambda e: e.tensor_tensor(c.sp, c.sp, masks[:, j, :], ALU.mult),
                              reads=[c.t_sp, t_masks], writes=[c.t_sp])
                for ci in range(NCH):
                    c = chains[ci]
                    fw.op("pe", lambda e: e.matmul(ps[c.bA][:], negU, c.sp, start=False, stop=first,
                                                   skip_group_check=True),
                          reads=[c.t_sp, t_cmat], writes=[t_ps[c.bA]], inc=first)
                    if not first:
                        fw.op("pe", lambda e: e.matmul(ps[c.bA][:], negones, c.Sb, start=False, stop=True,
                                                       skip_group_check=True),
                              reads=[c.t_Sb, t_cmat], writes=[t_ps[c.bA]])
                for ci in range(NCH):
                    c = chains[ci]
                    fw.op("act", lambda e: e.activation(c.a, ps[c.bA][:], AF.Exp), reads=[t_ps[c.bA]], writes=[c.t_a])
                    if diag:
                        fw.op("dve", lambda e: e.tensor_tensor(c.a, c.a, masks[:, j, :], ALU.mult),
                              reads=[c.t_a, t_masks], writes=[c.t_a])
                for ci in range(NCH):
                    c, hd = chains[ci], heads[ci]
                    if not last:
                        QK(ci, kb - 1)
                    fw.op("pe", lambda e: e.matmul(ps[c.bO][:], Vt[:, kb, hd * 128:(hd + 1) * 128], c.a,
                                                   start=first, stop=last),
                          reads=[t_Vt[kb], c.t_a], writes=[t_ps[c.bO]], inc=last)
                for ci in range(NCH):
                    c, hd = chains[ci], heads[ci]
                    if not last:
                        if first:
                            fw.op("dve", lambda e: e.tensor_copy(c.S, c.sp), reads=[c.t_sp], writes=[c.t_S])
                        else:
                            fw.op("dve", lambda e: e.tensor_tensor(c.S, c.S, c.sp, ALU.add),
                                  reads=[c.t_S, c.t_sp], writes=[c.t_S])
                        fw.op("dve", lambda e: e.tensor_copy(c.Sb, c.S), reads=[c.t_S], writes=[c.t_Sb])
                    else:
                        fw.op("dve", lambda e: e.tensor_copy(oT[:, hd, qsl], ps[c.bO][:]),
                              reads=[t_ps[c.bO]], writes=[t_oT[hd][g]])

        for g in range(2):
            for hg in range(2):
                attn_group(g, [hg * 4 + i for i in range(4)])
        fw.barrier()
        dump("oT", oT, [128, 8, 1024], [t for r in t_oT for t in r], BF16)
        checkpoint("C")

        ycv = regf(92, 32).rearrange("p (a b) -> p a b", a=8)
        t_ycv = [[Tile(f"ycv{c}_{h}") for h in range(2)] for c in range(8)]
        swT = regb(124, 16).rearrange("p (a b) -> p a b", a=8)
        t_swT = [Tile(f"swT{c}") for c in range(8)]
        ubuf = [regb(140, 2.25)[:, 0:1054], regb(142.25, 2.25)[:, 0:1054]]
        t_u = [[Tile(f"u{i}_{h}") for h in range(3)] for i in range(2)]
        dg = regb(52, 7.75).rearrange("p (a b) -> p a b", a=31)
        t_dg = Tile("dg")
        ysq = [regf(149, 4), regf(153, 4)]
        t_ysq = [Tile("ysq0"), Tile("ysq1")]
        mean_sb = regf(157, 4); t_mean = Tile("mean")
        rstd_sb = regf(161, 4); t_rstd = Tile("rstd")
        var_sb = regf(165, 4); t_var = Tile("var")
        t1 = [regf(188, 4), regf(192, 4)]
        t_t1 = [Tile("t1_0"), Tile("t1_1")]
        acc1, acc2 = t1
        t_acc1, t_acc2 = t_t1
        sg = [regf(196, 2), regf(198, 2), regf(200, 2)]
        t_sg = [Tile("sg0"), Tile("sg1"), Tile("sg2")]
        ident3 = ident.unsqueeze(1).broadcast_to([128, 31, 128])

        def conv_dg(c):
            w3 = consts[:, C_WDW + c * 31: C_WDW + (c + 1) * 31].unsqueeze(2).broadcast_to([128, 31, 128])
            fw.op("dve", lambda e: e.tensor_tensor(dg, ident3, w3, ALU.mult),
                  reads=[t_cmat, t_consts], writes=[t_dg])

        def conv_chunk(c):
            ui = c % 2
            u = ubuf[ui]
            cb = [4, 5] if c % 2 == 0 else [6, 7]
            for th in range(2):
                rd = [t_u[ui][0], t_u[ui][1]] if th == 0 else [t_u[ui][1], t_u[ui][2]]
                for jtap in range(31):
                    fw.op("pe", lambda e: e.matmul(ps[cb[th]][:], dg[:, jtap, :],
                                                   u[:, jtap + th * 512: jtap + th * 512 + 512],
                                                   start=(jtap == 0), stop=(jtap == 30)),
                          reads=rd + [t_dg], writes=[t_ps[cb[th]]], inc=(jtap == 30))
                fw.op("act", lambda e: e.activation(ycv[:, c, th * 512:(th + 1) * 512], ps[cb[th]][:], AF.Identity,
                                                    bias=consts[:, C_BDW + c:C_BDW + c + 1]),
                      reads=[t_ps[cb[th]], t_consts], writes=[t_ycv[c][th]])
            qi = c % 2
            fw.op("act", lambda e: e.activation(ysq[qi], ycv[:, c, :], AF.Square),
                  reads=t_ycv[c], writes=[t_ysq[qi]])
            if c == 0:
                fw.op("dve", lambda e: e.tensor_copy(acc1, ycv[:, c, :]), reads=t_ycv[c], writes=[t_acc1])
                fw.op("dve", lambda e: e.tensor_copy(acc2, ysq[qi]), reads=[t_ysq[qi]], writes=[t_acc2])
            else:
                fw.op("dve", lambda e: e.tensor_tensor(acc1, acc1, ycv[:, c, :], ALU.add),
                      reads=t_ycv[c] + [t_acc1], writes=[t_acc1])
                fw.op("dve", lambda e: e.tensor_tensor(acc2, acc2, ysq[qi], ALU.add),
                      reads=[t_ysq[qi], t_acc2], writes=[t_acc2])

        for tpair in range(4):
            wa, t_wa = wload(w_in, 0, 16, 3072 + tpair * 256, 256)
            wb_, t_wb = wload(w_in, 0, 16, 4096 + tpair * 256, 256)
            for ec in range(2):
                c = tpair * 2 + ec
                ui = c % 2
                u = ubuf[ui]
                if c > 0:
                    conv_dg(c - 1)
                bh = 0
                for (wv_, t_wv, cs) in ((wa, t_wa, 0), (wb_, t_wb, 128)):
                    for kc in range(16):
                        fw.op("pe", lambda e: e.matmul(ps[bh][:, cs:cs + 128], wv_[:, kc, ec * 128:(ec + 1) * 128],
                                                       hTc_last[:, kc, :], start=(kc == 0), stop=(kc == 15)),
                              reads=t_wv + [t_hTc_last], writes=[t_ps[bh]], inc=(kc == 15))
                ba_col = consts[:, C_BIN + 24 + c:C_BIN + 25 + c]
                bb_col = consts[:, C_BIN + 32 + c:C_BIN + 33 + c]
                fw.op("act", lambda e: e.activation(sg[2][:, 0:30], ps[bh][:, 128 + 98:256], AF.Sigmoid, bias=bb_col),
                      reads=[t_ps[bh], t_consts], writes=[t_sg[2]])
                fw.op("dve", lambda e: e.scalar_tensor_tensor(u[:, 0:30], ps[bh][:, 98:128], ba_col, sg[2][:, 0:30],
                                                              ALU.add, ALU.mult),
                      reads=[t_ps[bh], t_consts, t_sg[2]], writes=[t_u[ui][0]])
                fw.op("dve", lambda e: e.tensor_scalar(u[:, 0:30], u[:, 0:30], flag, None, ALU.mult),
                      reads=[t_u[ui][0], t_consts], writes=[t_u[ui][0]])
                for th in range(2):
                    bA_, bB_ = (1, 2) if th == 0 else (3, 0)
                    for (wv_, t_wv, bb) in ((wa, t_wa, bA_), (wb_, t_wb, bB_)):
                        for kc in range(16):
                            fw.op("pe", lambda e: e.matmul(ps[bb][:], wv_[:, kc, ec * 128:(ec + 1) * 128],
                                                           hT[:, kc, th * 512:(th + 1) * 512],
                                                           start=(kc == 0), stop=(kc == 15)),
                                  reads=t_wv + t_hT[th * 4:(th + 1) * 4], writes=[t_ps[bb]], inc=(kc == 15))
                    fw.op("act", lambda e: e.activation(sg[th], ps[bB_][:], AF.Sigmoid, bias=bb_col),
                          reads=[t_ps[bB_], t_consts], writes=[t_sg[th]])
                    fw.op("dve", lambda e: e.scalar_tensor_tensor(u[:, 30 + th * 512: 30 + (th + 1) * 512], ps[bA_][:],
                                                                  ba_col, sg[th], ALU.add, ALU.mult),
                          reads=[t_ps[bA_], t_consts, t_sg[th]], writes=[t_u[ui][1 + th]])
                if c > 0:
                    conv_chunk(c - 1)
        conv_dg(7)
        conv_chunk(7)
        B_MEAN = [0, 1]
        B_MSQ = [2, 3]
        for th in range(2):
            sl = slice(th * 512, (th + 1) * 512)
            fw.op("pe", lambda e: e.matmul(ps[B_MEAN[th]][:], onesf, acc1[:, sl], start=True, stop=True),
                  reads=[t_cmat, t_acc1], writes=[t_ps[B_MEAN[th]]])
            fw.op("pe", lambda e: e.matmul(ps[B_MSQ[th]][:], onesf, acc2[:, sl], start=True, stop=True),
                  reads=[t_cmat, t_acc2], writes=[t_ps[B_MSQ[th]]])
        for th in range(2):
            sl = slice(th * 512, (th + 1) * 512)
            fw.op("act", lambda e: e.activation(mean_sb[:, sl], ps[B_MEAN[th]][:], AF.Copy),
                  reads=[t_ps[B_MEAN[th]]], writes=[t_mean])
            fw.op("dve", lambda e: e.tensor_tensor(var_sb[:, sl], mean_sb[:, sl], mean_sb[:, sl], ALU.mult),
                  reads=[t_mean], writes=[t_var])
            fw.op("dve", lambda e: e.tensor_tensor(var_sb[:, sl], ps[B_MSQ[th]][:], var_sb[:, sl], ALU.subtract),
                  reads=[t_ps[B_MSQ[th]], t_var], writes=[t_var])
            fw.op("act", lambda e: e.activation(rstd_sb[:, sl], var_sb[:, sl], AF.Sqrt, bias=epsc),
                  reads=[t_var, t_cmat], writes=[t_rstd])
            fw.op("dve", lambda e: e.reciprocal(rstd_sb[:, sl], rstd_sb[:, sl]), reads=[t_rstd], writes=[t_rstd])
        for c in range(8):
            qi = c % 2
            fw.op("dve", lambda e: e.tensor_tensor(t1[qi], ycv[:, c, :], mean_sb, ALU.subtract),
                  reads=t_ycv[c] + [t_mean], writes=[t_t1[qi]])
            fw.op("dve", lambda e: e.tensor_tensor(t1[qi], t1[qi], rstd_sb, ALU.mult),
                  reads=[t_t1[qi], t_rstd], writes=[t_t1[qi]])
            fw.op("act", lambda e: e.activation(swT[:, c, :], t1[qi], AF.Silu,
                                                bias=consts[:, C_BLN + c:C_BLN + c + 1],
                                                scale=consts[:, C_GLN + c:C_GLN + c + 1]),
                  reads=[t_t1[qi], t_consts], writes=[t_swT[c]])
        fw.barrier()
        dump("ycv", regf(92, 32), [128, 8192], [t for r in t_ycv for t in r])
        dump("swT", swT, [128, 8, 1024], t_swT, BF16)
        checkpoint("D")

        mergedT = regb(92, 32).rearrange("p (a b) -> p a b", a=16)
        t_mT = [Tile(f"mT{i}") for i in range(8)]
        sbcv_ap = [regb(188 + 8 * i, 8).rearrange("p (h a b) -> p h a b", h=2, a=8) for i in range(2)]
        t_sbcv = [[Tile(f"sbcv{i}a"), Tile(f"sbcv{i}b")] for i in range(2)]
        fw.fence(t_t1 + t_sg, t_sbcv[0] + t_sbcv[1])
        etmp = [[regf(140 + 8 * bi + 2 * k, 2) for k in range(4)] for bi in range(2)]
        t_etmp = [[Tile(f"et{bi}_{k}") for k in range(4)] for bi in range(2)]
        it = 0
        for eg2 in range(4):
            for sub in range(2):
                eg = eg2 * 2 + sub
                wg1, t_wg1 = wload(w_in, 0, 16, 5120 + eg * 256, 256)
                wg2, t_wg2 = wload(w_in, 0, 16, 7168 + eg * 256, 256)
                pv = eg % 2
                wsc = sbcv_ap[pv]
                t_wsc = t_sbcv[pv]
                for hh, wd in enumerate((w_sbo, w_cvo)):
                    fw.dma("pool", f"sbcv{pv}" + "ab"[hh], wsc[:, hh],
                           wd[0:1024, eg * 256:(eg + 1) * 256].rearrange("(a p) e -> p a e", p=128),
                           writes=[t_sbcv[pv][hh]])
                wsb, wcv = wsc[:, 0], wsc[:, 1]
                for ec in range(2):
                    ech = eg * 2 + ec
                    wcol = ec * 128
                    for th in range(2):
                        bi = it % 2
                        it += 1
                        bG1, bG2, bP1, bP2 = [bi * 4 + k for k in range(4)]
                        tsl = slice(th * 512, (th + 1) * 512)
                        for kc in range(16):
                            fw.op("pe", lambda e: e.matmul(ps[bG1][:], wg1[:, kc, ec * 128:(ec + 1) * 128], hT[:, kc, tsl],
                                                           start=(kc == 0), stop=(kc == 15)),
                                  reads=t_wg1 + t_hT[th * 4:(th + 1) * 4], writes=[t_ps[bG1]], inc=(kc == 15))
                        for kc in range(16):
                            fw.op("pe", lambda e: e.matmul(ps[bG2][:], wg2[:, kc, ec * 128:(ec + 1) * 128], hT[:, kc, tsl],
                                                           start=(kc == 0), stop=(kc == 15)),
                                  reads=t_wg2 + t_hT[th * 4:(th + 1) * 4], writes=[t_ps[bG2]], inc=(kc == 15))
                        for kc in range(8):
                            fw.op("pe", lambda e: e.matmul(ps[bP1][:], wsb[:, kc, wcol:wcol + 128], oT[:, kc, tsl],
                                                           start=(kc == 0), stop=(kc == 7)),
                                  reads=t_wsc + [t_oT[kc][th]], writes=[t_ps[bP1]], inc=(kc == 7))
                        for kc in range(8):
                            fw.op("pe", lambda e: e.matmul(ps[bP2][:], wcv[:, kc, wcol:wcol + 128], swT[:, kc, tsl],
                                                           start=(kc == 0), stop=(kc == 7)),
                                  reads=t_wsc + [t_swT[kc]], writes=[t_ps[bP2]], inc=(kc == 7))
                        s1, s2, m1, m2 = etmp[bi]
                        ts1, ts2, tm1, tm2 = t_etmp[bi]
                        fw.op("act", lambda e: e.activation(s1, ps[bG1][:], AF.Sigmoid,
                                                            bias=consts[:, C_BIN + 40 + ech:C_BIN + 41 + ech]),
                              reads=[t_ps[bG1], t_consts], writes=[ts1])
                        fw.op("act", lambda e: e.activation(s2, ps[bG2][:], AF.Sigmoid,
                                                            bias=consts[:, C_BIN + 56 + ech:C_BIN + 57 + ech]),
                              reads=[t_ps[bG2], t_consts], writes=[ts2])
                        fw.op("dve", lambda e: e.tensor_tensor(m1, ps[bP1][:], s1, ALU.mult),
                              reads=[t_ps[bP1], ts1], writes=[tm1])
                        fw.op("dve", lambda e: e.tensor_tensor(m2, ps[bP2][:], s2, ALU.mult),
                              reads=[t_ps[bP2], ts2], writes=[tm2])
                        fw.op("dve", lambda e: e.tensor_tensor(mergedT[:, ech, tsl], m1, m2, ALU.add),
                              reads=[tm1, tm2], writes=t_mT[th * 4:(th + 1) * 4])
        fw.barrier()
        dump("mT", mergedT, [128, 16, 1024], t_mT, BF16)
        checkpoint("E")

        x1 = regf(143, 64).rearrange("p (a b) -> p a b", a=8)
        t_x1 = [Tile(f"x1_{i}") for i in range(8)]
        xbF = [regf(60, 8), regf(68, 8)]
        t_xbF = [Tile("xbF0"), Tile("xbF1")]
        gslotF = regf(76, 8); t_gslotF = Tile("gslotF")
        junk = regb(84, 1); t_junk = Tile("junk")
        hbfA = regb(85, 4); t_hbfA = Tile("hbfA")
        hbfB = regb(108, 4); t_hbfB = Tile("hbfB")
        gslotG = regf(52, 8); t_gslotG = Tile("gslotG")
        gslotH = regf(100, 8); t_gslotH = Tile("gslotH")
        h2T = regb(124, 16).rearrange("p (a b) -> p a b", a=16)
        t_h2T = [Tile(f"h2T{i}") for i in range(4)]
        rtmp = [regf(8, 2), regf(10, 2)]
        t_rtmp = [Tile("rtmp0"), Tile("rtmp1")]
        ssqF, t_ssqF = stat(64)

        load_g(1, gslotF, t_gslotF)
        load_g(2, gslotG, t_gslotG)

        def F_cg(half, cg):
            wts = [wload(w_o, kt * 1024, 8, cg * 512, 512) for kt in range(2)]
            banks = [(cg * 4 + tbl) % 6 for tbl in range(4)]
            for kt in range(2):
                wv, t_w = wts[kt]
                for tbl in range(4):
                    tb = half * 4 + tbl
                    b = banks[tbl]
                    for kc in range(8):
                        fw.op("pe", lambda e: e.matmul(ps[b][:], mergedT[:, kt * 8 + kc, tb * 128:(tb + 1) * 128],
                                                       wv[:, kc, :], start=(kt == 0 and kc == 0),
                                                       stop=(kt == 1 and kc == 7), skip_group_check=True),
                              reads=t_w + [t_mT[tb]], writes=[t_ps[b]], inc=(kt == 1 and kc == 7))
            for tbl in range(4):
                tb = half * 4 + tbl
                b = banks[tbl]
                fw.op("dve", lambda e: e.tensor_copy(x1[:, tb, cg * 512:(cg + 1) * 512], ps[b][:]),
                      reads=[t_ps[b]], writes=[t_x1[tb]])
                fw.op("act", lambda e: e.activation(junk, x1[:, tb, cg * 512:(cg + 1) * 512], AF.Square,
                                                    accum_out=ssqF[:, tb * 8 + cg: tb * 8 + cg + 1]),
                      reads=[t_x1[tb]], writes=[t_junk, t_ssqF])

        def F_post(half):
            for tbl in range(4):
                tb = half * 4 + tbl
                i = tb % 2
                fw.dma("sp", f"xbF{i}", xbF[i], xin[NT + tb * 128: NT + (tb + 1) * 128, :], writes=[t_xbF[i]])
                ss, t_ss = stat()
                rs, t_rs = stat()
                fw.op("dve", lambda e: e.tensor_reduce(ss, ssqF[:, tb * 8:tb * 8 + 4], mybir.AxisListType.X, ALU.add),
                      reads=[t_ssqF], writes=[t_ss])
                rstd_from_ss(ss, t_ss, 1.0 / D, rs, t_rs)
                fw.op("dve", lambda e: e.scalar_tensor_tensor(x1[:, tb, :], x1[:, tb, :], rs, gslotF, ALU.mult, ALU.mult),
                      reads=[t_x1[tb], t_rs, t_gslotF], writes=[t_x1[tb]])
                fw.op("dve", lambda e: e.tensor_tensor(x1[:, tb, :], x1[:, tb, :], xbF[i], ALU.add),
                      reads=[t_x1[tb], t_xbF[i]], writes=[t_x1[tb]])

        for cg in range(4):
            F_cg(0, cg)
        F_post(0)
        for cg in range(4):
            F_cg(1, cg)
            if True:
                tbl = cg
                prenorm(x1[:, tbl, :], t_x1[tbl], hbfA, t_hbfA, gslotG, t_gslotG)
                transpose_block(hbfA, t_hbfA, h2T, t_h2T[tbl], tbl * 128, banks=(6, 7))
        F_post(1)
        h2Tb = regb(108, 16).rearrange("p (a b) -> p a b", a=16)
        t_h2Tb = [Tile(f"h2Tb{i}") for i in range(4)]
        fT = regb(92, 16).rearrange("p (a b) -> p a b", a=8)
        t_fT = [[Tile(f"fT{c}_{h}") for h in range(2)] for c in range(8)]
        fw.fence(t_mT, t_h2Tb + [t for r in t_fT for t in r])
        for tbl in range(4):
            prenorm(x1[:, 4 + tbl, :], t_x1[4 + tbl], hbfA, t_hbfA, gslotG, t_gslotG)
            transpose_block(hbfA, t_hbfA, h2Tb, t_h2Tb[tbl], tbl * 128, banks=(6, 7))
        h2Th = [h2T, h2Tb]
        t_h2Th = [t_h2T, t_h2Tb]
        t_out = [Tile(f"out{i}") for i in range(8)]
        for tb in range(8):
            fw.dma("sp", f"outA{tb}", out_d[tb * 128:(tb + 1) * 128, :], x1[:, tb, :],
                   reads=[t_x1[tb]], writes=[t_out[tb]])
        dump("x1", regf(143, 64), [128, 8 * 2048], t_x1)
        dump("h2T", h2T, [128, 16, 512], t_h2T, BF16)
        checkpoint("F")

        y2, t_y2 = x1, t_x1
        gslotH = regf(60, 8); t_gslotH = Tile("gslotH")
        junk2 = regb(68, 4); t_junk2 = Tile("junk2")
        fw.fence(t_xbF + [t_gslotF, t_junk, t_hbfA], [t_gslotH, t_junk2])
        load_g(3, gslotH, t_gslotH)
        for e8 in range(8):
            for ti in range(4):
                wv, t_w = wload(w_up, 0, 16, e8 * 1024 + ti * 256, 256)
                for ec in range(2):
                    ch = ti * 2 + ec
                    banks = [4 + (ch % 2) * 2, 5 + (ch % 2) * 2]
                    for kc in range(16):
                        for th in range(2):
                            b = banks[th]
                            fw.op("pe", lambda e: e.matmul(ps[b][:], wv[:, kc, ec * 128:(ec + 1) * 128],
                                                           h2Th[th][:, kc, :], start=(kc == 0), stop=(kc == 15)),
                                  reads=t_w + t_h2Th[th], writes=[t_ps[b]], inc=(kc == 15))
                    for th in range(2):
                        b = banks[th]
                        ri = th
                        fw.op("act", lambda e: e.activation(rtmp[ri], ps[b][:], AF.Relu),
                              reads=[t_ps[b]], writes=[t_rtmp[ri]])
                        fw.op("dve", lambda e: e.tensor_tensor(fT[:, ch, th * 512:(th + 1) * 512], rtmp[ri], rtmp[ri],
                                                               ALU.mult),
                              reads=[t_rtmp[ri]], writes=[t_fT[ch][th]])
            for cg in range(4):
                wv, t_w = wload(w_dn, e8 * 1024, 8, cg * 512, 512)
                for tbh in range(2):
                    for tbl in range(4):
                        tb = tbh * 4 + tbl
                        b = tbl
                        for kc in range(8):
                            fw.op("pe", lambda e: e.matmul(ps[b][:], fT[:, kc, tb * 128:(tb + 1) * 128], wv[:, kc, :],
                                                           start=(kc == 0), stop=(kc == 7)),
                                  reads=t_w + [t_fT[kc][tbh]], writes=[t_ps[b]], inc=(kc == 7))
                    for tbl in range(4):
                        tb = tbh * 4 + tbl
                        dst = y2[:, tb, cg * 512:(cg + 1) * 512]
                        if e8 == 0:
                            fw.op("act", lambda e: e.activation(dst, ps[tbl][:], AF.Copy),
                                  reads=[t_ps[tbl]], writes=[t_y2[tb]])
                        else:
                            fw.op("dve", lambda e: e.tensor_tensor(dst, ps[tbl][:], dst, ALU.add),
                                  reads=[t_ps[tbl], t_y2[tb]], writes=[t_y2[tb]])
        for tb in range(8):
            ss, t_ss = stat()
            rs, t_rs = stat()
            fw.op("act", lambda e: e.activation(junk2, y2[:, tb, :], AF.Square, accum_out=ss),
                  reads=[t_y2[tb]], writes=[t_junk2, t_ss])
            rstd_from_ss(ss, t_ss, 1.0 / D, rs, t_rs)
            fw.op("dve", lambda e: e.scalar_tensor_tensor(y2[:, tb, :], y2[:, tb, :], rs, gslotH, ALU.mult, ALU.mult),
                  reads=[t_y2[tb], t_rs, t_gslotH], writes=[t_y2[tb]])
            fw.dma("pool", f"outB{tb}", out_d[tb * 128:(tb + 1) * 128, :], y2[:, tb, :],
                   reads=[t_y2[tb]], writes=[t_out[tb]], accum_op=ALU.add)
        t_y2 = t_out
        fw.wait_tiles("sp", t_y2)
        print("[marks]", fw.marks)
        print(f"[build] insts={fw.n_inst} waits={fw.n_wait} wtiles={wcount[0]} "
              f"counts={ {k: e.count for k, e in fw.engs.items()} }")
    return nc, dbg_out


def make_in_maps(x, g_pre_mix, w_in, b_in, w_dw, b_dw, g_conv_ln, b_conv_ln, w_sb_out, w_conv_out, w_o,
                 g_post_mix, g_pre_mlp, w_up, w_down, g_post_mlp):
    f = lambda a: np.ascontiguousarray(np.asarray(a, dtype=np.float32))
    x = f(x)
    b_in0 = f(b_in)[0]
    consts = np.zeros((128, 384), np.float32)
    consts[:, 0:72] = b_in0.reshape(72, 128).T
    wdw = f(w_dw)[0]
    consts[:, 72:320] = wdw.reshape(31, 8, 128).transpose(2, 1, 0).reshape(128, 248)
    consts[:, 320:328] = f(b_dw)[0].reshape(8, 128).T
    consts[:, 328:336] = f(g_conv_ln)[0].reshape(8, 128).T
    consts[:, 336:344] = f(b_conv_ln)[0].reshape(8, 128).T
    gv = np.stack([f(g_pre_mix)[0], f(g_post_mix)[0], f(g_pre_mlp)[0], f(g_post_mlp)[0]])
    gvec = np.ascontiguousarray(np.broadcast_to(gv[:, None, :], (4, 128, D)))
    bv = np.ascontiguousarray(np.broadcast_to(b_in0[None, 2048:3072], (128, 1024)))
    shared = {
        "gvec": gvec, "bv": bv, "w_in": f(w_in)[0], "w_sbo": f(w_sb_out)[0], "w_cvo": f(w_conv_out)[0],
        "w_o": f(w_o)[0], "w_up": f(w_up)[0], "w_dn": f(w_down)[0],
    }
    in_maps = []
    for c in range(8):
        b, half = c // 2, c % 2
        xi = np.zeros((2 * NT, D), np.float32)
        if half == 1:
            xi[:NT] = x[b, :NT]
        xi[NT:] = x[b, half * NT:(half + 1) * NT]
        cc = consts.copy()
        cc[:, 344] = float(half)
        m = dict(shared)
        m["xin"] = xi
        m["consts"] = cc
        in_maps.append(m)
    return in_maps


_NC_CACHE = {}


def kernel(**inputs):
    if "nc" not in _NC_CACHE:
        _NC_CACHE["nc"] = build(False)[0]
    nc = _NC_CACHE["nc"]
    in_maps = make_in_maps(**inputs)
    res = run_bass_kernel_spmd(nc, in_maps, core_ids=list(range(8)))
    out = np.zeros((4, 2048, D), np.float32)
    for c in range(8):
        b, half = c // 2, c % 2
        out[b, half * NT:(half + 1) * NT] = res.results[c]["out"]
    return out
```

```python
from contextlib import ExitStack
import math
import numpy as np
import concourse.bass as bass
import concourse.mybir as mybir
from concourse.bass_utils import run_bass_kernel_spmd

F32 = mybir.dt.float32
BF16 = mybir.dt.bfloat16
AF = mybir.ActivationFunctionType
ALU = mybir.AluOpType

D = 2048
NT = 1024
DIN = 9216
DFF = 8192
EPS = 1e-6
NSLOT = 5
SCALE = 1.0 / math.sqrt(128.0)
KIB = 256


class Tile:
    __slots__ = ("name", "w", "r", "excl")

    def __init__(self, name, excl=False):
        self.name = name
        self.w = None
        self.r = []
        self.excl = excl


class Eng:
    def __init__(self, name, handle, sem):
        self.name = name
        self.h = handle
        self.sem = sem
        self.count = 0
        self.known = {}


class FW:
    def __init__(self, nc, stack):
        self.nc = nc
        self.stack = stack
        self.sems = {}
        self.engs = {}
        for name, h in (("pe", nc.tensor), ("act", nc.scalar), ("dve", nc.vector),
                        ("pool", nc.gpsimd), ("sp", nc.sync)):
            sem = stack.enter_context(nc.semaphore("sem_" + name))
            self.sems["e:" + name] = sem
            self.engs[name] = Eng(name, h, sem)
        self.dma_vals = {}
        self.n_wait = 0
        self.n_inst = 0
        self.n_by = {}
        self.marks = []

    def _need(self, eng, tickets):
        need = {}
        for t in tickets:
            if t is None:
                continue
            k, v, _ = t
            if eng.known.get(k, 0) >= v:
                continue
            if need.get(k, 0) < v:
                need[k] = v
        for k, v in need.items():
            eng.h.wait_ge(self.sems[k], v)
            eng.known[k] = v
            self.n_wait += 1

    def op(self, engname, fn, reads=(), writes=(), inc=True):
        eng = self.engs[engname]
        tickets = []
        for t in reads:
            if t.w is not None:
                tickets.append(t.w)
            if t.excl:
                for r in t.r:
                    if r[2] != engname:
                        tickets.append(r)
        same_ok = engname == "pe"
        for t in writes:
            if t.w is not None and not (same_ok and t.w[2] == engname):
                tickets.append(t.w)
            for r in t.r:
                if not (same_ok and r[2] == engname):
                    tickets.append(r)
        self._need(eng, tickets)
        ins = fn(eng.h)
        self.n_inst += 1
        self.n_by[engname] = self.n_by.get(engname, 0) + 1
        key = "e:" + engname
        if inc:
            eng.count += 1
            ins.then_inc(eng.sem, 1)
            ticket = (key, eng.count, engname)
        else:
            ticket = (key, eng.count + 1, engname)
        for t in reads:
            t.r.append(ticket)
        for t in writes:
            t.w = ticket
            t.r = []
        return ins

    def dma(self, qname, semkey, out, in_, reads=(), writes=(), **kw):
        eng = self.engs[qname]
        k = "d:" + semkey
        if k not in self.sems:
            self.sems[k] = self.stack.enter_context(self.nc.semaphore("dsem_" + semkey))
            self.dma_vals[k] = 0
        tickets = []
        for t in reads:
            if t.w is not None:
                tickets.append(t.w)
        for t in writes:
            if t.w is not None:
                tickets.append(t.w)
            tickets.extend(t.r)
        self._need(eng, tickets)
        ins = eng.h.dma_start(out=out, in_=in_, **kw)
        self.dma_vals[k] += 16
        ins.then_inc(self.sems[k], 16)
        ticket = (k, self.dma_vals[k], "dma")
        for t in reads:
            t.r.append(ticket)
        for t in writes:
            t.w = ticket
            t.r = []
        self.n_inst += 1
        return ins

    def fence(self, src_tiles, dst_tiles):
        for d in dst_tiles:
            for t in src_tiles:
                if t.w is not None:
                    d.r.append(t.w)
                d.r.extend(t.r)

    def wait_tiles(self, engname, tiles):
        eng = self.engs[engname]
        tickets = []
        for t in tiles:
            if t.w is not None:
                tickets.append(t.w)
            tickets.extend(t.r)
        self._need(eng, tickets)

    def barrier(self, extra_tiles=()):
        tickets = []
        for n in ("pe", "act", "dve", "pool"):
            e = self.engs[n]
            if e.count > 0:
                tickets.append(("e:" + n, e.count, n))
        for t in extra_tiles:
            if t.w is not None:
                tickets.append(t.w)
            tickets.extend(t.r)
        for n in ("pe", "act", "dve", "sp"):
            self._need(self.engs[n], [t for t in tickets if t[2] != n])


class _Stop(Exception):
    pass


class _Suppress:
    def __enter__(self):
        return self

    def __exit__(self, et, ev, tb):
        return et is _Stop


def build(dbg=False, stop_after=None):
    nc = bass.Bass("TRN2", target_bir_lowering=False)
    dram = {}

    def din(name, shape):
        dram[name] = nc.dram_tensor(name, list(shape), F32, kind="ExternalInput").ap()
        return dram[name]

    xin = din("xin", [2 * NT, D])
    consts_d = din("consts", [128, 384])
    gvec_d = din("gvec", [4, 128, D])
    bv_d = din("bv", [128, 1024])
    w_in = din("w_in", [D, DIN])
    w_sbo = din("w_sbo", [1024, D])
    w_cvo = din("w_cvo", [1024, D])
    w_o = din("w_o", [D, D])
    w_up = din("w_up", [D, DFF])
    w_dn = din("w_dn", [DFF, D])
    out_d = nc.dram_tensor("out", [NT, D], F32, kind="ExternalOutput").ap()
    dbg_out = {}

    with _Suppress(), ExitStack() as st:
        fw = FW(nc, st)
        arena = nc.alloc_sbuf_tensor("arena", [128, 52992], F32)
        ps = [nc.alloc_psum_tensor(f"ps{i}", [128, 512], F32) for i in range(8)]
        t_ps = [Tile(f"ps{i}", excl=True) for i in range(8)]

        def regf(off_kib, size_kib):
            a = int(round(off_kib * KIB))
            b = int(round((off_kib + size_kib) * KIB))
            return arena[:, a:b]

        def regb(off_kib, size_kib):
            return regf(off_kib, size_kib).bitcast(BF16)

        def dump(name, ap_sb, shape, tiles, dt=F32):
            if not dbg:
                return
            d = nc.dram_tensor("dbg_" + name, list(shape), dt, kind="ExternalOutput").ap()
            dbg_out[name] = d
            fw.dma("sp", "dbg", d, ap_sb, reads=tiles)
            fw.barrier(extra_tiles=tiles)

        def checkpoint(name):
            fw.marks.append((name, dict(fw.n_by)))
            if stop_after == name:
                fw.barrier()
                t_fin = Tile("fin")
                fw.dma("sp", "fin", out_d[0:128, :], regf(60, 8), reads=[t_fin])
                fw.wait_tiles("sp", [t_fin])
                raise _Stop()

        consts = regf(0, 1.5)
        t_consts = Tile("consts")
        ident = regb(1.5, 0.25)
        negU = regb(1.75, 0.25)
        negones = regb(2.0, 0.25)
        onesf = regf(2.25, 0.5)
        t_cmat = Tile("cmat")
        stats = regf(3, 1)
        epsc = stats[:, 255:256]
        hTc_last = regb(4, 4).rearrange("p (a b) -> p a b", a=16)
        t_hTc_last = Tile("hTc_last")
        masks = regb(8, 4).rearrange("p (a b) -> p a b", a=4)
        t_masks = Tile("masks")
        bvbc = regf(8, 4)
        t_bvbc = Tile("bvbc")
        bvf = None

        C_BIN, C_WDW, C_BDW, C_GLN, C_BLN, C_FLAG, C_BQS = 0, 72, 320, 328, 336, 344, 345
        flag = consts[:, C_FLAG:C_FLAG + 1]

        slot_ap = [regb(12 + 8 * i, 8) for i in range(NSLOT)]
        t_slot = [[Tile(f"slot{i}a"), Tile(f"slot{i}b")] for i in range(NSLOT)]
        wcount = [0]

        def wload(wd, r0, kc, c0, ncols):
            assert kc * ncols == 4096
            s = wcount[0] % NSLOT
            wcount[0] += 1
            view = slot_ap[s].rearrange("p (a b) -> p a b", a=kc)
            src = wd[r0:r0 + kc * 128, c0:c0 + ncols].rearrange("(a p) e -> p a e", p=128)
            fw.dma("pool", f"w{s}", view, src, writes=t_slot[s])
            return view, t_slot[s]

        def wload2(wdA, wdB, c0):
            s = wcount[0] % NSLOT
            wcount[0] += 1
            view = slot_ap[s].rearrange("p (h a b) -> p h a b", h=2, a=8)
            for hh, wd in enumerate((wdA, wdB)):
                src = wd[0:1024, c0:c0 + 256].rearrange("(a p) e -> p a e", p=128)
                fw.dma("pool", f"w{s}" + "ab"[hh], view[:, hh], src, writes=[t_slot[s][hh]])
            return view, t_slot[s]

        scol = [0]

        def stat(n=1):
            c = scol[0]
            scol[0] += n
            assert scol[0] <= 255
            return stats[:, c:c + n], Tile(f"stat{c}")

        fw.dma("sp", "consts", consts, consts_d[:, :], writes=[t_consts])
        scr = regf(188, 2)
        t_scr = Tile("scr")
        fw.op("pool", lambda e: e.memset(scr[:, 0:128], 1.0), writes=[t_scr])
        fw.op("pool", lambda e: e.affine_select(out=scr[:, 0:128], in_=scr[:, 0:128], pattern=[[-1, 128]],
                                                compare_op=ALU.is_equal, fill=0.0, base=0, channel_multiplier=1),
              reads=[t_scr], writes=[t_scr])
        fw.op("dve", lambda e: e.tensor_copy(ident, scr[:, 0:128]), reads=[t_scr], writes=[t_cmat])
        fw.op("pool", lambda e: e.memset(scr[:, 128:256], -1.0), writes=[t_scr])
        fw.op("pool", lambda e: e.affine_select(out=scr[:, 128:256], in_=scr[:, 128:256], pattern=[[-1, 128]],
                                                compare_op=ALU.is_ge, fill=0.0, base=0, channel_multiplier=1),
              reads=[t_scr], writes=[t_scr])
        fw.op("dve", lambda e: e.tensor_copy(negU, scr[:, 128:256]), reads=[t_scr], writes=[t_cmat])
        fw.op("pool", lambda e: e.memset(scr[:, 256:384], -1.0), writes=[t_scr])
        fw.op("dve", lambda e: e.tensor_copy(negones, scr[:, 256:384]), reads=[t_scr], writes=[t_cmat])
        fw.op("pool", lambda e: e.memset(onesf, 1.0 / 1024.0), writes=[t_cmat])
        fw.op("pool", lambda e: e.memset(epsc, EPS), writes=[t_cmat])
        fw.op("dve", lambda e: e.tensor_scalar(consts[:, C_BQS:C_BQS + 8], consts[:, C_BIN:C_BIN + 8], SCALE, None,
                                               ALU.mult), reads=[t_consts], writes=[t_consts])

        fw.barrier()

        checkpoint("const")
        hT = regb(60, 32).rearrange("p (a b) -> p a b", a=16)
        t_hT = [Tile(f"hT{i}") for i in range(8)]
        KT = regb(92, 32).rearrange("p (a b) -> p a b", a=8)
        t_KT = [[Tile(f"KT{h}_{q}") for q in range(4)] for h in range(8)]
        Vt = regb(124, 32).rearrange("p (a b) -> p a b", a=16)
        t_Vt = [Tile(f"Vt{i}") for i in range(16)]
        QT = regb(156, 16).rearrange("p (a b) -> p a b", a=8)
        t_QT = [[Tile(f"QT{h}_{q}") for q in range(2)] for h in range(8)]
        xbuf = [regf(172, 8), regf(180, 8)]
        t_xbuf = [Tile("xbuf0"), Tile("xbuf1")]
        hbf = [regb(188, 4), regb(192, 4)]
        t_hbf = [Tile("hbf0"), Tile("hbf1")]
        gslot = regf(196, 8)
        t_gslot = Tile("gslotA")

        bank_rr = [0]

        def next_bank():
            b = bank_rr[0] % 8
            bank_rr[0] += 1
            return b

        def rstd_from_ss(ss_ap, t_ss, n_inv, out_ap, t_out):
            fw.op("act", lambda e: e.activation(out_ap, ss_ap, AF.Sqrt, bias=epsc, scale=n_inv),
                  reads=[t_ss, t_cmat], writes=[t_out])
            fw.op("dve", lambda e: e.reciprocal(out_ap, out_ap), reads=[t_out], writes=[t_out])

        def prenorm(src_ap, t_src, hb, t_hb, g_ap, t_g):
            ss, t_ss = stat()
            rs, t_rs = stat()
            fw.op("act", lambda e: e.activation(hb, src_ap, AF.Square, accum_out=ss),
                  reads=[t_src], writes=[t_hb, t_ss])
            rstd_from_ss(ss, t_ss, 1.0 / D, rs, t_rs)
            fw.op("dve", lambda e: e.scalar_tensor_tensor(hb, src_ap, rs, g_ap, ALU.mult, ALU.mult),
                  reads=[t_src, t_rs, t_g], writes=[t_hb])

        def transpose_block(hb, t_hb, dstT, t_dst, col0, banks=None):
            for half in range(2):
                b = next_bank() if banks is None else banks[half]
                pT = ps[b][:].bitcast(BF16).rearrange("p (a b) -> p a b", a=8)
                for j in range(8):
                    kc = half * 8 + j
                    fw.op("pe", lambda e: e.transpose(pT[:, j, :], hb[:, kc * 128:(kc + 1) * 128], ident),
                          reads=[t_hb, t_cmat], writes=[t_ps[b]], inc=(j == 7))
                dst = dstT[:, half * 8:(half + 1) * 8, col0:col0 + 128]
                if half == 0:
                    fw.op("act", lambda e: e.activation(dst, pT, AF.Copy), reads=[t_ps[b]], writes=[t_dst])
                else:
                    fw.op("dve", lambda e: e.tensor_copy(dst, pT), reads=[t_ps[b]], writes=[t_dst])

        def prenorm_transpose(src_ap, t_src, i, g_ap, t_g, dstT, t_dst, col0):
            prenorm(src_ap, t_src, hbf[i], t_hbf[i], g_ap, t_g)
            transpose_block(hbf[i], t_hbf[i], dstT, t_dst, col0)

        def load_g(idx, slot, t_slot_):
            fw.dma("sp", "g" + t_slot_.name, slot, gvec_d[idx], writes=[t_slot_])

        load_g(0, gslot, t_gslot)
        fw.dma("sp", "bv", bvbc, bv_d[:, :], writes=[t_bvbc])
        bvf_ap = None

        def phaseA(row0):
            for blk in range(8):
                i = blk % 2
                fw.dma("sp", f"xb{i}", xbuf[i], xin[row0 + blk * 128: row0 + (blk + 1) * 128, :],
                       writes=[t_xbuf[i]])
                prenorm_transpose(xbuf[i], t_xbuf[i], i, gslot, t_gslot, hT, t_hT[blk], blk * 128)

        def proj_fm(wd, c0_list, dst_fn, nblk_tok, bias_col_fn, scale, func=AF.Identity):
            for ti, c0 in enumerate(c0_list):
                wv, t_w = wload(wd, 0, 16, c0, 256)
                for ec in range(2):
                    banks = [next_bank() for _ in range(nblk_tok)]
                    for kc in range(16):
                        for th in range(nblk_tok):
                            b = banks[th]
                            fw.op("pe", lambda e: e.matmul(ps[b][:], wv[:, kc, ec * 128:(ec + 1) * 128],
                                                           hT[:, kc, th * 512:(th + 1) * 512],
                                                           start=(kc == 0), stop=(kc == 15)),
                                  reads=t_w + t_hT[th * 4:(th + 1) * 4], writes=[t_ps[b]], inc=(kc == 15))
                    for th in range(nblk_tok):
                        b = banks[th]
                        dst, t_dst = dst_fn(ti * 2 + ec, th)
                        bias = bias_col_fn(ti * 2 + ec)
                        fw.op("act", lambda e: e.activation(dst, ps[b][:], func, bias=bias, scale=scale),
                              reads=[t_ps[b], t_consts], writes=[t_dst])

        def projV(blk0, is_ctx):
            for cg in range(2):
                wts = [wload(w_in, kt * 1024, 8, 2048 + cg * 512, 512) for kt in range(2)]
                for tbh in range(2):
                    banks = [next_bank() for _ in range(4)]
                    for kt in range(2):
                        wv, t_w = wts[kt]
                        for tbl in range(4):
                            tb = tbh * 4 + tbl
                            b = banks[tbl]
                            for kc in range(8):
                                fw.op("pe", lambda e: e.matmul(ps[b][:], hT[:, kt * 8 + kc, tb * 128:(tb + 1) * 128],
                                                               wv[:, kc, :], start=(kt == 0 and kc == 0),
                                                               stop=(kt == 1 and kc == 7), skip_group_check=True),
                                      reads=t_w + [t_hT[tb]], writes=[t_ps[b]], inc=(kt == 1 and kc == 7))
                    for tbl in range(4):
                        tb = tbh * 4 + tbl
                        b = banks[tbl]
                        dst = Vt[:, blk0 + tb, cg * 512:(cg + 1) * 512]
                        if is_ctx:
                            fw.op("dve", lambda e: e.scalar_tensor_tensor(dst, ps[b][:], flag,
                                                                          bvf_ap[:, cg * 512:(cg + 1) * 512],
                                                                          ALU.mult, ALU.add),
                                  reads=[t_ps[b], t_consts, t_bvf], writes=[t_Vt[blk0 + tb]])
                        else:
                            fw.op("dve", lambda e: e.tensor_tensor(dst, ps[b][:],
                                                                   bvbc[:, cg * 512:(cg + 1) * 512], ALU.add),
                                  reads=[t_ps[b], t_bvbc], writes=[t_Vt[blk0 + tb]])

        bvf_ap = regf(156, 4)
        t_bvf = Tile("bvf")
        fw.op("dve", lambda e: e.tensor_scalar(bvf_ap, bvbc, flag, None, ALU.mult),
              reads=[t_bvbc, t_consts], writes=[t_bvf])

        phaseA(0)
        fw.op("dve", lambda e: e.tensor_copy(hTc_last, hT[:, :, 896:1024]), reads=[t_hT[7]], writes=[t_hTc_last])
        proj_fm(w_in, [1024 + 256 * i for i in range(4)],
                lambda ch, th: (KT[:, ch, th * 512:(th + 1) * 512], t_KT[ch][th]), 2,
                lambda ch: consts[:, C_BIN + 8 + ch:C_BIN + 9 + ch], 1.0)
        projV(0, True)
        fw.barrier()
        checkpoint("Bctx")
        phaseA(NT)
        proj_fm(w_in, [1024 + 256 * i for i in range(4)],
                lambda ch, th: (KT[:, ch, 1024 + th * 512:1024 + (th + 1) * 512], t_KT[ch][2 + th]), 2,
                lambda ch: consts[:, C_BIN + 8 + ch:C_BIN + 9 + ch], 1.0)
        projV(8, False)
        proj_fm(w_in, [256 * i for i in range(4)],
                lambda ch, th: (QT[:, ch, th * 512:(th + 1) * 512], t_QT[ch][th]), 2,
                lambda ch: consts[:, C_BQS + ch:C_BQS + ch + 1], SCALE)
        fw.barrier()
        dump("hT", hT, [128, 16, 1024], t_hT, BF16)
        dump("KT", KT, [128, 8, 2048], [t for r in t_KT for t in r], BF16)
        dump("Vt", Vt, [128, 16, 1024], t_Vt, BF16)
        dump("QT", QT, [128, 8, 1024], [t for r in t_QT for t in r], BF16)
        checkpoint("B")

        scr2 = regf(204, 2)
        t_scr2 = Tile("scr2")
        for j in range(4):
            fw.op("pool", lambda e: e.memset(scr2, 1.0), writes=[t_scr2])
            fw.op("pool", lambda e: e.affine_select(out=scr2, in_=scr2, pattern=[[1, 512]], compare_op=ALU.is_gt,
                                                    fill=0.0, base=-128 * j, channel_multiplier=-1),
                  reads=[t_scr2], writes=[t_scr2])
            fw.op("dve", lambda e: e.tensor_copy(masks[:, j, :], scr2), reads=[t_scr2], writes=[t_masks])
        fw.barrier()

        oT = regb(172, 16).rearrange("p (a b) -> p a b", a=8)
        t_oT = [[Tile(f"oT{h}_{q}") for q in range(2)] for h in range(8)]

        class Chain:
            pass

        NCH = 4
        chains = []
        for ci in range(NCH):
            c = Chain()
            base = 188 + 5 * ci if ci < 3 else 52
            c.sp = regb(base, 1); c.t_sp = Tile(f"c{ci}sp")
            c.S = regf(base + 1, 2); c.t_S = Tile(f"c{ci}S")
            c.Sb = regb(base + 3, 1); c.t_Sb = Tile(f"c{ci}Sb")
            c.a = regb(base + 4, 1); c.t_a = Tile(f"c{ci}a")
            c.bA = ci * 2
            c.bO = ci * 2 + 1
            chains.append(c)
        ebuf = [regf(203, 2), regf(205, 2)]
        t_ebuf = [Tile("e0"), Tile("e1")]

        def attn_group(g, heads):
            nsteps = 8 + 4 * g + 4
            qsl = slice(g * 512, (g + 1) * 512)

            def QK(ci, kb):
                c, hd = chains[ci], heads[ci]
                fw.op("pe", lambda e: e.matmul(ps[c.bA][:], KT[:, hd, kb * 128:(kb + 1) * 128], QT[:, hd, qsl],
                                               start=True, stop=True),
                      reads=[t_KT[hd][kb // 4], t_QT[hd][g]], writes=[t_ps[c.bA]])

            for ci in range(NCH):
                QK(ci, nsteps - 1)
            for s in range(nsteps):
                kb = nsteps - 1 - s
                first, last = s == 0, s == nsteps - 1
                j = kb - (8 + 4 * g)
                diag = j >= 0
                for ci in range(NCH):
                    c = chains[ci]
                    ei = ci % 2
                    fw.op("act", lambda e: e.activation(ebuf[ei], ps[c.bA][:], AF.Exp),
                          reads=[t_ps[c.bA]], writes=[t_ebuf[ei]])
                    fw.op("act", lambda e: e.activation(c.sp, ebuf[ei], AF.Ln, bias=1.0),
                          reads=[t_ebuf[ei]], writes=[c.t_sp])
                    if diag:
                        fw.op("dve", lambda e: e.tensor_tensor(c.sp, c.sp, masks[:, j, :], ALU.mult),
                              reads=[c.t_sp, t_masks], writes=[c.t_sp])
                for ci in range(NCH):
                    c = chains[ci]
                    fw.op("pe", lambda e: e.matmul(ps[c.bA][:], negU, c.sp, start=False, stop=first,
                                                   skip_group_check=True),
                          reads=[c.t_sp, t_cmat], writes=[t_ps[c.bA]], inc=first)
                    if not first:
                        fw.op("pe", lambda e: e.matmul(ps[c.bA][:], negones, c.Sb, start=False, stop=True,
                                                       skip_group_check=True),
                              reads=[c.t_Sb, t_cmat], writes=[t_ps[c.bA]])
                for ci in range(NCH):
                    c = chains[ci]
                    fw.op("act", lambda e: e.activation(c.a, ps[c.bA][:], AF.Exp), reads=[t_ps[c.bA]], writes=[c.t_a])
                    if diag:
                        fw.op("dve", lambda e: e.tensor_tensor(c.a, c.a, masks[:, j, :], ALU.mult),
                              reads=[c.t_a, t_masks], writes=[c.t_a])
                for ci in range(NCH):
                    c, hd = chains[ci], heads[ci]
                    if not last:
                        QK(ci, kb - 1)
                    fw.op("pe", lambda e: e.matmul(ps[c.bO][:], Vt[:, kb, hd * 128:(hd + 1) * 128], c.a,
                                                   start=first, stop=last),
                          reads=[t_Vt[kb], c.t_a], writes=[t_ps[c.bO]], inc=last)
                for ci in range(NCH):
                    c, hd = chains[ci], heads[ci]
                    if not last:
                        if first:
                            fw.op("dve", lambda e: e.tensor_copy(c.S, c.sp), reads=[c.t_sp], writes=[c.t_S])
                        else:
                            fw.op("dve", lambda e: e.tensor_tensor(c.S, c.S, c.sp, ALU.add),
                                  reads=[c.t_S, c.t_sp], writes=[c.t_S])
                        fw.op("dve", lambda e: e.tensor_copy(c.Sb, c.S), reads=[c.t_S], writes=[c.t_Sb])
                    else:
                        fw.op("dve", lambda e: e.tensor_copy(oT[:, hd, qsl], ps[c.bO][:]),
                              reads=[t_ps[c.bO]], writes=[t_oT[hd][g]])

        for g in range(2):
            for hg in range(2):
                attn_group(g, [hg * 4 + i for i in range(4)])
        fw.barrier()
        dump("oT", oT, [128, 8, 1024], [t for r in t_oT for t in r], BF16)
        checkpoint("C")

        ycv = regf(92, 32).rearrange("p (a b) -> p a b", a=8)
        t_ycv = [[Tile(f"ycv{c}_{h}") for h in range(2)] for c in range(8)]
        swT = regb(124, 16).rearrange("p (a b) -> p a b", a=8)
        t_swT = [Tile(f"swT{c}") for c in range(8)]
        ubuf = [regb(140, 2.25)[:, 0:1054], regb(142.25, 2.25)[:, 0:1054]]
        t_u = [[Tile(f"u{i}_{h}") for h in range(3)] for i in range(2)]
        dg = regb(52, 7.75).rearrange("p (a b) -> p a b", a=31)
        t_dg = Tile("dg")
        ysq = [regf(149, 4), regf(153, 4)]
        t_ysq = [Tile("ysq0"), Tile("ysq1")]
        mean_sb = regf(157, 4); t_mean = Tile("mean")
        rstd_sb = regf(161, 4); t_rstd = Tile("rstd")
        var_sb = regf(165, 4); t_var = Tile("var")
        t1 = [regf(188, 4), regf(192, 4)]
        t_t1 = [Tile("t1_0"), Tile("t1_1")]
        acc1, acc2 = t1
        t_acc1, t_acc2 = t_t1
        sg = [regf(196, 2), regf(198, 2), regf(200, 2)]
        t_sg = [Tile("sg0"), Tile("sg1"), Tile("sg2")]
        ident3 = ident.unsqueeze(1).broadcast_to([128, 31, 128])

        def conv_dg(c):
            w3 = consts[:, C_WDW + c * 31: C_WDW + (c + 1) * 31].unsqueeze(2).broadcast_to([128, 31, 128])
            fw.op("dve", lambda e: e.tensor_tensor(dg, ident3, w3, ALU.mult),
                  reads=[t_cmat, t_consts], writes=[t_dg])

        def conv_chunk(c):
            ui = c % 2
            u = ubuf[ui]
            cb = [4, 5] if c % 2 == 0 else [6, 7]
            for th in range(2):
                rd = [t_u[ui][0], t_u[ui][1]] if th == 0 else [t_u[ui][1], t_u[ui][2]]
                for jtap in range(31):
                    fw.op("pe", lambda e: e.matmul(ps[cb[th]][:], dg[:, jtap, :],
                                                   u[:, jtap + th * 512: jtap + th * 512 + 512],
                                                   start=(jtap == 0), stop=(jtap == 30)),
                          reads=rd + [t_dg], writes=[t_ps[cb[th]]], inc=(jtap == 30))
                fw.op("act", lambda e: e.activation(ycv[:, c, th * 512:(th + 1) * 512], ps[cb[th]][:], AF.Identity,
                                                    bias=consts[:, C_BDW + c:C_BDW + c + 1]),
                      reads=[t_ps[cb[th]], t_consts], writes=[t_ycv[c][th]])
            qi = c % 2
            fw.op("act", lambda e: e.activation(ysq[qi], ycv[:, c, :], AF.Square),
                  reads=t_ycv[c], writes=[t_ysq[qi]])
            if c == 0:
                fw.op("dve", lambda e: e.tensor_copy(acc1, ycv[:, c, :]), reads=t_ycv[c], writes=[t_acc1])
                fw.op("dve", lambda e: e.tensor_copy(acc2, ysq[qi]), reads=[t_ysq[qi]], writes=[t_acc2])
            else:
                fw.op("dve", lambda e: e.tensor_tensor(acc1, acc1, ycv[:, c, :], ALU.add),
                      reads=t_ycv[c] + [t_acc1], writes=[t_acc1])
                fw.op("dve", lambda e: e.tensor_tensor(acc2, acc2, ysq[qi], ALU.add),
                      reads=[t_ysq[qi], t_acc2], writes=[t_acc2])

        for tpair in range(4):
            wa, t_wa = wload(w_in, 0, 16, 3072 + tpair * 256, 256)
            wb_, t_wb = wload(w_in, 0, 16, 4096 + tpair * 256, 256)
            for ec in range(2):
                c = tpair * 2 + ec
                ui = c % 2
                u = ubuf[ui]
                if c > 0:
                    conv_dg(c - 1)
                bh = 0
                for (wv_, t_wv, cs) in ((wa, t_wa, 0), (wb_, t_wb, 128)):
                    for kc in range(16):
                        fw.op("pe", lambda e: e.matmul(ps[bh][:, cs:cs + 128], wv_[:, kc, ec * 128:(ec + 1) * 128],
                                                       hTc_last[:, kc, :], start=(kc == 0), stop=(kc == 15)),
                              reads=t_wv + [t_hTc_last], writes=[t_ps[bh]], inc=(kc == 15))
                ba_col = consts[:, C_BIN + 24 + c:C_BIN + 25 + c]
                bb_col = consts[:, C_BIN + 32 + c:C_BIN + 33 + c]
                fw.op("act", lambda e: e.activation(sg[2][:, 0:30], ps[bh][:, 128 + 98:256], AF.Sigmoid, bias=bb_col),
                      reads=[t_ps[bh], t_consts], writes=[t_sg[2]])
                fw.op("dve", lambda e: e.scalar_tensor_tensor(u[:, 0:30], ps[bh][:, 98:128], ba_col, sg[2][:, 0:30],
                                                              ALU.add, ALU.mult),
                      reads=[t_ps[bh], t_consts, t_sg[2]], writes=[t_u[ui][0]])
                fw.op("dve", lambda e: e.tensor_scalar(u[:, 0:30], u[:, 0:30], flag, None, ALU.mult),
                      reads=[t_u[ui][0], t_consts], writes=[t_u[ui][0]])
                for th in range(2):
                    bA_, bB_ = (1, 2) if th == 0 else (3, 0)
                    for (wv_, t_wv, bb) in ((wa, t_wa, bA_), (wb_, t_wb, bB_)):
                        for kc in range(16):
                            fw.op("pe", lambda e: e.matmul(ps[bb][:], wv_[:, kc, ec * 128:(ec + 1) * 128],
                                                           hT[:, kc, th * 512:(th + 1) * 512],
                                                           start=(kc == 0), stop=(kc == 15)),
                                  reads=t_wv + t_hT[th * 4:(th + 1) * 4], writes=[t_ps[bb]], inc=(kc == 15))
                    fw.op("act", lambda e: e.activation(sg[th], ps[bB_][:], AF.Sigmoid, bias=bb_col),
                          reads=[t_ps[bB_], t_consts], writes=[t_sg[th]])
                    fw.op("dve", lambda e: e.scalar_tensor_tensor(u[:, 30 + th * 512: 30 + (th + 1) * 512], ps[bA_][:],
                                                                  ba_col, sg[th], ALU.add, ALU.mult),
                          reads=[t_ps[bA_], t_consts, t_sg[th]], writes=[t_u[ui][1 + th]])
                if c > 0:
                    conv_chunk(c - 1)
        conv_dg(7)
        conv_chunk(7)
        B_MEAN = [0, 1]
        B_MSQ = [2, 3]
        for th in range(2):
            sl = slice(th * 512, (th + 1) * 512)
            fw.op("pe", lambda e: e.matmul(ps[B_MEAN[th]][:], onesf, acc1[:, sl], start=True, stop=True),
                  reads=[t_cmat, t_acc1], writes=[t_ps[B_MEAN[th]]])
            fw.op("pe", lambda e: e.matmul(ps[B_MSQ[th]][:], onesf, acc2[:, sl], start=True, stop=True),
                  reads=[t_cmat, t_acc2], writes=[t_ps[B_MSQ[th]]])
        for th in range(2):
            sl = slice(th * 512, (th + 1) * 512)
            fw.op("act", lambda e: e.activation(mean_sb[:, sl], ps[B_MEAN[th]][:], AF.Copy),
                  reads=[t_ps[B_MEAN[th]]], writes=[t_mean])
            fw.op("dve", lambda e: e.tensor_tensor(var_sb[:, sl], mean_sb[:, sl], mean_sb[:, sl], ALU.mult),
                  reads=[t_mean], writes=[t_var])
            fw.op("dve", lambda e: e.tensor_tensor(var_sb[:, sl], ps[B_MSQ[th]][:], var_sb[:, sl], ALU.subtract),
                  reads=[t_ps[B_MSQ[th]], t_var], writes=[t_var])
            fw.op("act", lambda e: e.activation(rstd_sb[:, sl], var_sb[:, sl], AF.Sqrt, bias=epsc),
                  reads=[t_var, t_cmat], writes=[t_rstd])
            fw.op("dve", lambda e: e.reciprocal(rstd_sb[:, sl], rstd_sb[:, sl]), reads=[t_rstd], writes=[t_rstd])
        for c in range(8):
            qi = c % 2
            fw.op("dve", lambda e: e.tensor_tensor(t1[qi], ycv[:, c, :], mean_sb, ALU.subtract),
                  reads=t_ycv[c] + [t_mean], writes=[t_t1[qi]])
            fw.op("dve", lambda e: e.tensor_tensor(t1[qi], t1[qi], rstd_sb, ALU.mult),
                  reads=[t_t1[qi], t_rstd], writes=[t_t1[qi]])
            fw.op("act", lambda e: e.activation(swT[:, c, :], t1[qi], AF.Silu,
                                                bias=consts[:, C_BLN + c:C_BLN + c + 1],
                                                scale=consts[:, C_GLN + c:C_GLN + c + 1]),
                  reads=[t_t1[qi], t_consts], writes=[t_swT[c]])
        fw.barrier()
        dump("ycv", regf(92, 32), [128, 8192], [t for r in t_ycv for t in r])
        dump("swT", swT, [128, 8, 1024], t_swT, BF16)
        checkpoint("D")

        mergedT = regb(92, 32).rearrange("p (a b) -> p a b", a=16)
        t_mT = [Tile(f"mT{i}") for i in range(8)]
        sbcv_ap = [regb(188 + 8 * i, 8).rearrange("p (h a b) -> p h a b", h=2, a=8) for i in range(2)]
        t_sbcv = [[Tile(f"sbcv{i}a"), Tile(f"sbcv{i}b")] for i in range(2)]
        fw.fence(t_t1 + t_sg, t_sbcv[0] + t_sbcv[1])
        etmp = [[regf(140 + 8 * bi + 2 * k, 2) for k in range(4)] for bi in range(2)]
        t_etmp = [[Tile(f"et{bi}_{k}") for k in range(4)] for bi in range(2)]
        it = 0
        for eg2 in range(4):
            for sub in range(2):
                eg = eg2 * 2 + sub
                wg1, t_wg1 = wload(w_in, 0, 16, 5120 + eg * 256, 256)
                wg2, t_wg2 = wload(w_in, 0, 16, 7168 + eg * 256, 256)
                pv = eg % 2
                wsc = sbcv_ap[pv]
                t_wsc = t_sbcv[pv]
                for hh, wd in enumerate((w_sbo, w_cvo)):
                    fw.dma("pool", f"sbcv{pv}" + "ab"[hh], wsc[:, hh],
                           wd[0:1024, eg * 256:(eg + 1) * 256].rearrange("(a p) e -> p a e", p=128),
                           writes=[t_sbcv[pv][hh]])
                wsb, wcv = wsc[:, 0], wsc[:, 1]
                for ec in range(2):
                    ech = eg * 2 + ec
                    wcol = ec * 128
                    for th in range(2):
                        bi = it % 2
                        it += 1
                        bG1, bG2, bP1, bP2 = [bi * 4 + k for k in range(4)]
                        tsl = slice(th * 512, (th + 1) * 512)
                        for kc in range(16):
                            fw.op("pe", lambda e: e.matmul(ps[bG1][:], wg1[:, kc, ec * 128:(ec + 1) * 128], hT[:, kc, tsl],
                                                           start=(kc == 0), stop=(kc == 15)),
                                  reads=t_wg1 + t_hT[th * 4:(th + 1) * 4], writes=[t_ps[bG1]], inc=(kc == 15))
                        for kc in range(16):
                            fw.op("pe", lambda e: e.matmul(ps[bG2][:], wg2[:, kc, ec * 128:(ec + 1) * 128], hT[:, kc, tsl],
                                                           start=(kc == 0), stop=(kc == 15)),
                                  reads=t_wg2 + t_hT[th * 4:(th + 1) * 4], writes=[t_ps[bG2]], inc=(kc == 15))
                        for kc in range(8):
                            fw.op("pe", lambda e: e.matmul(ps[bP1][:], wsb[:, kc, wcol:wcol + 128], oT[:, kc, tsl],
                                                           start=(kc == 0), stop=(kc == 7)),
                                  reads=t_wsc + [t_oT[kc][th]], writes=[t_ps[bP1]], inc=(kc == 7))
                        for kc in range(8):
                            fw.op("pe", lambda e: e.matmul(ps[bP2][:], wcv[:, kc, wcol:wcol + 128], swT[:, kc, tsl],
                                                           start=(kc == 0), stop=(kc == 7)),
                                  reads=t_wsc + [t_swT[kc]], writes=[t_ps[bP2]], inc=(kc == 7))
                        s1, s2, m1, m2 = etmp[bi]
                        ts1, ts2, tm1, tm2 = t_etmp[bi]
                        fw.op("act", lambda e: e.activation(s1, ps[bG1][:], AF.Sigmoid,
                                                            bias=consts[:, C_BIN + 40 + ech:C_BIN + 41 + ech]),
                              reads=[t_ps[bG1], t_consts], writes=[ts1])
                        fw.op("act", lambda e: e.activation(s2, ps[bG2][:], AF.Sigmoid,
                                                            bias=consts[:, C_BIN + 56 + ech:C_BIN + 57 + ech]),
                              reads=[t_ps[bG2], t_consts], writes=[ts2])
                        fw.op("dve", lambda e: e.tensor_tensor(m1, ps[bP1][:], s1, ALU.mult),
                              reads=[t_ps[bP1], ts1], writes=[tm1])
                        fw.op("dve", lambda e: e.tensor_tensor(m2, ps[bP2][:], s2, ALU.mult),
                              reads=[t_ps[bP2], ts2], writes=[tm2])
                        fw.op("dve", lambda e: e.tensor_tensor(mergedT[:, ech, tsl], m1, m2, ALU.add),
                              reads=[tm1, tm2], writes=t_mT[th * 4:(th + 1) * 4])
        fw.barrier()
        dump("mT", mergedT, [128, 16, 1024], t_mT, BF16)
        checkpoint("E")

        x1 = regf(143, 64).rearrange("p (a b) -> p a b", a=8)
        t_x1 = [Tile(f"x1_{i}") for i in range(8)]
        xbF = [regf(60, 8), regf(68, 8)]
        t_xbF = [Tile("xbF0"), Tile("xbF1")]
        gslotF = regf(76, 8); t_gslotF = Tile("gslotF")
        junk = regb(84, 1); t_junk = Tile("junk")
        hbfA = regb(85, 4); t_hbfA = Tile("hbfA")
        hbfB = regb(108, 4); t_hbfB = Tile("hbfB")
        gslotG = regf(52, 8); t_gslotG = Tile("gslotG")
        gslotH = regf(100, 8); t_gslotH = Tile("gslotH")
        h2T = regb(124, 16).rearrange("p (a b) -> p a b", a=16)
        t_h2T = [Tile(f"h2T{i}") for i in range(4)]
        rtmp = [regf(8, 2), regf(10, 2)]
        t_rtmp = [Tile("rtmp0"), Tile("rtmp1")]
        ssqF, t_ssqF = stat(64)

        load_g(1, gslotF, t_gslotF)
        load_g(2, gslotG, t_gslotG)

        def F_cg(half, cg):
            wts = [wload(w_o, kt * 1024, 8, cg * 512, 512) for kt in range(2)]
            banks = [(cg * 4 + tbl) % 6 for tbl in range(4)]
            for kt in range(2):
                wv, t_w = wts[kt]
                for tbl in range(4):
                    tb = half * 4 + tbl
                    b = banks[tbl]
                    for kc in range(8):
                        fw.op("pe", lambda e: e.matmul(ps[b][:], mergedT[:, kt * 8 + kc, tb * 128:(tb + 1) * 128],
                                                       wv[:, kc, :], start=(kt == 0 and kc == 0),
                                                       stop=(kt == 1 and kc == 7), skip_group_check=True),
                              reads=t_w + [t_mT[tb]], writes=[t_ps[b]], inc=(kt == 1 and kc == 7))
            for tbl in range(4):
                tb = half * 4 + tbl
                b = banks[tbl]
                fw.op("dve", lambda e: e.tensor_copy(x1[:, tb, cg * 512:(cg + 1) * 512], ps[b][:]),
                      reads=[t_ps[b]], writes=[t_x1[tb]])
                fw.op("act", lambda e: e.activation(junk, x1[:, tb, cg * 512:(cg + 1) * 512], AF.Square,
                                                    accum_out=ssqF[:, tb * 8 + cg: tb * 8 + cg + 1]),
                      reads=[t_x1[tb]], writes=[t_junk, t_ssqF])

        def F_post(half):
            for tbl in range(4):
                tb = half * 4 + tbl
                i = tb % 2
                fw.dma("sp", f"xbF{i}", xbF[i], xin[NT + tb * 128: NT + (tb + 1) * 128, :], writes=[t_xbF[i]])
                ss, t_ss = stat()
                rs, t_rs = stat()
                fw.op("dve", lambda e: e.tensor_reduce(ss, ssqF[:, tb * 8:tb * 8 + 4], mybir.AxisListType.X, ALU.add),
                      reads=[t_ssqF], writes=[t_ss])
                rstd_from_ss(ss, t_ss, 1.0 / D, rs, t_rs)
                fw.op("dve", lambda e: e.scalar_tensor_tensor(x1[:, tb, :], x1[:, tb, :], rs, gslotF, ALU.mult, ALU.mult),
                      reads=[t_x1[tb], t_rs, t_gslotF], writes=[t_x1[tb]])
                fw.op("dve", lambda e: e.tensor_tensor(x1[:, tb, :], x1[:, tb, :], xbF[i], ALU.add),
                      reads=[t_x1[tb], t_xbF[i]], writes=[t_x1[tb]])

        for cg in range(4):
            F_cg(0, cg)
        F_post(0)
        for cg in range(4):
            F_cg(1, cg)
            if True:
                tbl = cg
                prenorm(x1[:, tbl, :], t_x1[tbl], hbfA, t_hbfA, gslotG, t_gslotG)
                transpose_block(hbfA, t_hbfA, h2T, t_h2T[tbl], tbl * 128, banks=(6, 7))
        F_post(1)
        h2Tb = regb(108, 16).rearrange("p (a b) -> p a b", a=16)
        t_h2Tb = [Tile(f"h2Tb{i}") for i in range(4)]
        fT = regb(92, 16).rearrange("p (a b) -> p a b", a=8)
        t_fT = [[Tile(f"fT{c}_{h}") for h in range(2)] for c in range(8)]
        fw.fence(t_mT, t_h2Tb + [t for r in t_fT for t in r])
        h2Th = [h2T, h2Tb]
        t_h2Th = [t_h2T, t_h2Tb]
        t_out = [Tile(f"out{i}") for i in range(8)]
        for tb in range(8):
            fw.dma("sp", f"outA{tb}", out_d[tb * 128:(tb + 1) * 128, :], x1[:, tb, :],
                   reads=[t_x1[tb]], writes=[t_out[tb]])
        dump("x1", regf(143, 64), [128, 8 * 2048], t_x1)
        dump("h2T", h2T, [128, 16, 512], t_h2T, BF16)
        checkpoint("F")

        y2, t_y2 = x1, t_x1
        gslotH = regf(60, 8); t_gslotH = Tile("gslotH")
        junk2 = regb(68, 4); t_junk2 = Tile("junk2")
        fw.fence(t_xbF + [t_gslotF, t_junk, t_hbfA], [t_gslotH, t_junk2])
        load_g(3, gslotH, t_gslotH)
        for e8 in range(8):
            def up_chunk(wv, t_w, ti, ec, th, b):
                ch = ti * 2 + ec
                for kc in range(16):
                    fw.op("pe", lambda e: e.matmul(ps[b][:], wv[:, kc, ec * 128:(ec + 1) * 128],
                                                   h2Th[th][:, kc, :], start=(kc == 0), stop=(kc == 15)),
                          reads=t_w + t_h2Th[th], writes=[t_ps[b]], inc=(kc == 15))
                fw.op("act", lambda e: e.activation(rtmp[th], ps[b][:], AF.Relu),
                      reads=[t_ps[b]], writes=[t_rtmp[th]])
                fw.op("dve", lambda e: e.tensor_tensor(fT[:, ch, th * 512:(th + 1) * 512], rtmp[th], rtmp[th],
                                                       ALU.mult),
                      reads=[t_rtmp[th]], writes=[t_fT[ch][th]])

            if e8 == 0:
                tiles = [wload(w_up, 0, 16, ti * 256, 256) for ti in range(4)]
                prenorm(x1[:, 4, :], t_x1[4], hbfA, t_hbfA, gslotG, t_gslotG)
                for ti in range(4):
                    for ec in range(2):
                        up_chunk(tiles[ti][0], tiles[ti][1], ti, ec, 0, 4 + ec)
                    transpose_block(hbfA, t_hbfA, h2Tb, t_h2Tb[ti], ti * 128, banks=(6, 7))
                    if ti < 3:
                        prenorm(x1[:, 5 + ti, :], t_x1[5 + ti], hbfA, t_hbfA, gslotG, t_gslotG)
                for ti in range(4):
                    for ec in range(2):
                        up_chunk(tiles[ti][0], tiles[ti][1], ti, ec, 1, 4 + (ti * 2 + ec) % 4)
            else:
                for ti in range(4):
                    wv, t_w = wload(w_up, 0, 16, e8 * 1024 + ti * 256, 256)
                    for ec in range(2):
                        ch = ti * 2 + ec
                        banks = [4 + (ch % 2) * 2, 5 + (ch % 2) * 2]
                        for kc in range(16):
                            for th in range(2):
                                b = banks[th]
                                fw.op("pe", lambda e: e.matmul(ps[b][:], wv[:, kc, ec * 128:(ec + 1) * 128],
                                                               h2Th[th][:, kc, :], start=(kc == 0), stop=(kc == 15)),
                                      reads=t_w + t_h2Th[th], writes=[t_ps[b]], inc=(kc == 15))
                        for th in range(2):
                            b = banks[th]
                            fw.op("act", lambda e: e.activation(rtmp[th], ps[b][:], AF.Relu),
                                  reads=[t_ps[b]], writes=[t_rtmp[th]])
                            fw.op("dve", lambda e: e.tensor_tensor(fT[:, ch, th * 512:(th + 1) * 512], rtmp[th],
                                                                   rtmp[th], ALU.mult),
                                  reads=[t_rtmp[th]], writes=[t_fT[ch][th]])
            for cg in range(4):
                wv, t_w = wload(w_dn, e8 * 1024, 8, cg * 512, 512)
                for tbh in range(2):
                    for tbl in range(4):
                        tb = tbh * 4 + tbl
                        b = tbl
                        for kc in range(8):
                            fw.op("pe", lambda e: e.matmul(ps[b][:], fT[:, kc, tb * 128:(tb + 1) * 128], wv[:, kc, :],
                                                           start=(kc == 0), stop=(kc == 7)),
                                  reads=t_w + [t_fT[kc][tbh]], writes=[t_ps[b]], inc=(kc == 7))
                    for tbl in range(4):
                        tb = tbh * 4 + tbl
                        dst = y2[:, tb, cg * 512:(cg + 1) * 512]
                        if e8 == 0:
                            fw.op("act", lambda e: e.activation(dst, ps[tbl][:], AF.Copy),
                                  reads=[t_ps[tbl]], writes=[t_y2[tb]])
                        else:
                            fw.op("dve", lambda e: e.tensor_tensor(dst, ps[tbl][:], dst, ALU.add),
                                  reads=[t_ps[tbl], t_y2[tb]], writes=[t_y2[tb]])
        for tb in range(8):
            ss, t_ss = stat()
            rs, t_rs = stat()
            fw.op("act", lambda e: e.activation(junk2, y2[:, tb, :], AF.Square, accum_out=ss),
                  reads=[t_y2[tb]], writes=[t_junk2, t_ss])
            rstd_from_ss(ss, t_ss, 1.0 / D, rs, t_rs)
            fw.op("dve", lambda e: e.scalar_tensor_tensor(y2[:, tb, :], y2[:, tb, :], rs, gslotH, ALU.mult, ALU.mult),
                  reads=[t_y2[tb], t_rs, t_gslotH], writes=[t_y2[tb]])
            fw.dma("pool", f"outB{tb}", out_d[tb * 128:(tb + 1) * 128, :], y2[:, tb, :],
                   reads=[t_y2[tb]], writes=[t_out[tb]], accum_op=ALU.add)
        t_y2 = t_out
        fw.wait_tiles("sp", t_y2)
        print("[marks]", fw.marks)
        print(f"[build] insts={fw.n_inst} waits={fw.n_wait} wtiles={wcount[0]} "
              f"counts={ {k: e.count for k, e in fw.engs.items()} }")
    return nc, dbg_out


def make_in_maps(x, g_pre_mix, w_in, b_in, w_dw, b_dw, g_conv_ln, b_conv_ln, w_sb_out, w_conv_out, w_o,
                 g_post_mix, g_pre_mlp, w_up, w_down, g_post_mlp):
    f = lambda a: np.ascontiguousarray(np.asarray(a, dtype=np.float32))
    x = f(x)
    b_in0 = f(b_in)[0]
    consts = np.zeros((128, 384), np.float32)
    consts[:, 0:72] = b_in0.reshape(72, 128).T
    wdw = f(w_dw)[0]
    consts[:, 72:320] = wdw.reshape(31, 8, 128).transpose(2, 1, 0).reshape(128, 248)
    consts[:, 320:328] = f(b_dw)[0].reshape(8, 128).T
    consts[:, 328:336] = f(g_conv_ln)[0].reshape(8, 128).T
    consts[:, 336:344] = f(b_conv_ln)[0].reshape(8, 128).T
    gv = np.stack([f(g_pre_mix)[0], f(g_post_mix)[0], f(g_pre_mlp)[0], f(g_post_mlp)[0]])
    gvec = np.ascontiguousarray(np.broadcast_to(gv[:, None, :], (4, 128, D)))
    bv = np.ascontiguousarray(np.broadcast_to(b_in0[None, 2048:3072], (128, 1024)))
    shared = {
        "gvec": gvec, "bv": bv, "w_in": f(w_in)[0], "w_sbo": f(w_sb_out)[0], "w_cvo": f(w_conv_out)[0],
        "w_o": f(w_o)[0], "w_up": f(w_up)[0], "w_dn": f(w_down)[0],
    }
    in_maps = []
    for c in range(8):
        b, half = c // 2, c % 2
        xi = np.zeros((2 * NT, D), np.float32)
        if half == 1:
            xi[:NT] = x[b, :NT]
        xi[NT:] = x[b, half * NT:(half + 1) * NT]
        cc = consts.copy()
        cc[:, 344] = float(half)
        m = dict(shared)
        m["xin"] = xi
        m["consts"] = cc
        in_maps.append(m)
    return in_maps


_NC_CACHE = {}


def kernel(**inputs):
    if "nc" not in _NC_CACHE:
        _NC_CACHE["nc"] = build(False)[0]
    nc = _NC_CACHE["nc"]
    in_maps = make_in_maps(**inputs)
    res = run_bass_kernel_spmd(nc, in_maps, core_ids=list(range(8)))
    out = np.zeros((4, 2048, D), np.float32)
    for c in range(8):
        b, half = c // 2, c % 2
        out[b, half * NT:(half + 1) * NT] = res.results[c]["out"]
    return out
```

```python
from contextlib import ExitStack
import math
import numpy as np
import concourse.bass as bass
import concourse.mybir as mybir
from concourse.bass_utils import run_bass_kernel_spmd

F32 = mybir.dt.float32
BF16 = mybir.dt.bfloat16
AF = mybir.ActivationFunctionType
ALU = mybir.AluOpType

D = 2048
NT = 1024
DIN = 9216
DFF = 8192
EPS = 1e-6
NSLOT = 5
SCALE = 1.0 / math.sqrt(128.0)
KIB = 256


class Tile:
    __slots__ = ("name", "w", "r", "excl")

    def __init__(self, name, excl=False):
        self.name = name
        self.w = None
        self.r = []
        self.excl = excl


class Eng:
    def __init__(self, name, handle, sem):
        self.name = name
        self.h = handle
        self.sem = sem
        self.count = 0
        self.known = {}


class FW:
    def __init__(self, nc, stack):
        self.nc = nc
        self.stack = stack
        self.sems = {}
        self.engs = {}
        for name, h in (("pe", nc.tensor), ("act", nc.scalar), ("dve", nc.vector),
                        ("pool", nc.gpsimd), ("sp", nc.sync)):
            sem = stack.enter_context(nc.semaphore("sem_" + name))
            self.sems["e:" + name] = sem
            self.engs[name] = Eng(name, h, sem)
        self.dma_vals = {}
        self.n_wait = 0
        self.n_inst = 0
        self.n_by = {}
        self.marks = []

    def _need(self, eng, tickets):
        need = {}
        for t in tickets:
            if t is None:
                continue
            k, v, _ = t
            if eng.known.get(k, 0) >= v:
                continue
            if need.get(k, 0) < v:
                need[k] = v
        for k, v in need.items():
            eng.h.wait_ge(self.sems[k], v)
            eng.known[k] = v
            self.n_wait += 1

    def op(self, engname, fn, reads=(), writes=(), inc=True):
        eng = self.engs[engname]
        tickets = []
        for t in reads:
            if t.w is not None:
                tickets.append(t.w)
            if t.excl:
                for r in t.r:
                    if r[2] != engname:
                        tickets.append(r)
        same_ok = engname == "pe"
        for t in writes:
            if t.w is not None and not (same_ok and t.w[2] == engname):
                tickets.append(t.w)
            for r in t.r:
                if not (same_ok and r[2] == engname):
                    tickets.append(r)
        self._need(eng, tickets)
        ins = fn(eng.h)
        self.n_inst += 1
        self.n_by[engname] = self.n_by.get(engname, 0) + 1
        key = "e:" + engname
        if inc:
            eng.count += 1
            ins.then_inc(eng.sem, 1)
            ticket = (key, eng.count, engname)
        else:
            ticket = (key, eng.count + 1, engname)
        for t in reads:
            t.r.append(ticket)
        for t in writes:
            t.w = ticket
            t.r = []
        return ins

    def dma(self, qname, semkey, out, in_, reads=(), writes=(), **kw):
        eng = self.engs[qname]
        k = "d:" + semkey
        if k not in self.sems:
            self.sems[k] = self.stack.enter_context(self.nc.semaphore("dsem_" + semkey))
            self.dma_vals[k] = 0
        tickets = []
        for t in reads:
            if t.w is not None:
                tickets.append(t.w)
        for t in writes:
            if t.w is not None:
                tickets.append(t.w)
            tickets.extend(t.r)
        self._need(eng, tickets)
        ins = eng.h.dma_start(out=out, in_=in_, **kw)
        self.dma_vals[k] += 16
        ins.then_inc(self.sems[k], 16)
        ticket = (k, self.dma_vals[k], "dma")
        for t in reads:
            t.r.append(ticket)
        for t in writes:
            t.w = ticket
            t.r = []
        self.n_inst += 1
        return ins

    def fence(self, src_tiles, dst_tiles):
        for d in dst_tiles:
            for t in src_tiles:
                if t.w is not None:
                    d.r.append(t.w)
                d.r.extend(t.r)

    def wait_tiles(self, engname, tiles):
        eng = self.engs[engname]
        tickets = []
        for t in tiles:
            if t.w is not None:
                tickets.append(t.w)
            tickets.extend(t.r)
        self._need(eng, tickets)

    def barrier(self, extra_tiles=()):
        tickets = []
        for n in ("pe", "act", "dve", "pool"):
            e = self.engs[n]
            if e.count > 0:
                tickets.append(("e:" + n, e.count, n))
        for t in extra_tiles:
            if t.w is not None:
                tickets.append(t.w)
            tickets.extend(t.r)
        for n in ("pe", "act", "dve", "sp"):
            self._need(self.engs[n], [t for t in tickets if t[2] != n])


class _Stop(Exception):
    pass


class _Suppress:
    def __enter__(self):
        return self

    def __exit__(self, et, ev, tb):
        return et is _Stop


def build(dbg=False, stop_after=None):
    nc = bass.Bass("TRN2", target_bir_lowering=False)
    dram = {}

    def din(name, shape):
        dram[name] = nc.dram_tensor(name, list(shape), F32, kind="ExternalInput").ap()
        return dram[name]

    xin = din("xin", [2 * NT, D])
    consts_d = din("consts", [128, 384])
    gvec_d = din("gvec", [4, 128, D])
    bv_d = din("bv", [128, 1024])
    w_in = din("w_in", [D, DIN])
    w_sbo = din("w_sbo", [1024, D])
    w_cvo = din("w_cvo", [1024, D])
    w_o = din("w_o", [D, D])
    w_up = din("w_up", [D, DFF])
    w_dn = din("w_dn", [DFF, D])
    out_d = nc.dram_tensor("out", [NT, D], F32, kind="ExternalOutput").ap()
    dbg_out = {}

    with _Suppress(), ExitStack() as st:
        fw = FW(nc, st)
        arena = nc.alloc_sbuf_tensor("arena", [128, 52992], F32)
        ps_all = nc.alloc_psum_tensor("ps_all", [128, 4096], F32)
        ps = [ps_all[:, i * 512:(i + 1) * 512] for i in range(8)]
        t_ps = [Tile(f"ps{i}", excl=True) for i in range(8)]

        def regf(off_kib, size_kib):
            a = int(round(off_kib * KIB))
            b = int(round((off_kib + size_kib) * KIB))
            return arena[:, a:b]

        def regb(off_kib, size_kib):
            return regf(off_kib, size_kib).bitcast(BF16)

        def dump(name, ap_sb, shape, tiles, dt=F32):
            if not dbg:
                return
            d = nc.dram_tensor("dbg_" + name, list(shape), dt, kind="ExternalOutput").ap()
            dbg_out[name] = d
            fw.dma("sp", "dbg", d, ap_sb, reads=tiles)
            fw.barrier(extra_tiles=tiles)

        def checkpoint(name):
            fw.marks.append((name, dict(fw.n_by)))
            if stop_after == name:
                fw.barrier()
                t_fin = Tile("fin")
                fw.dma("sp", "fin", out_d[0:128, :], regf(60, 8), reads=[t_fin])
                fw.wait_tiles("sp", [t_fin])
                raise _Stop()

        consts = regf(0, 1.5)
        t_consts = Tile("consts")
        ident = regb(1.5, 0.25)
        negU = regb(1.75, 0.25)
        negones = regb(2.0, 0.25)
        onesf = regf(2.25, 0.5)
        t_cmat = Tile("cmat")
        stats = regf(3, 1)
        epsc = stats[:, 255:256]
        hTc_last = regb(4, 4).rearrange("p (a b) -> p a b", a=16)
        t_hTc_last = Tile("hTc_last")
        masks = regb(8, 4).rearrange("p (a b) -> p a b", a=4)
        t_masks = Tile("masks")
        bvbc = regf(8, 4)
        t_bvbc = Tile("bvbc")
        bvf = None

        C_BIN, C_WDW, C_BDW, C_GLN, C_BLN, C_FLAG, C_BQS = 0, 72, 320, 328, 336, 344, 345
        flag = consts[:, C_FLAG:C_FLAG + 1]

        slot_ap = [regb(12 + 8 * i, 8) for i in range(NSLOT)]
        t_slot = [[Tile(f"slot{i}a"), Tile(f"slot{i}b")] for i in range(NSLOT)]
        wcount = [0]

        def wload(wd, r0, kc, c0, ncols):
            assert kc * ncols == 4096
            s = wcount[0] % NSLOT
            wcount[0] += 1
            view = slot_ap[s].rearrange("p (a b) -> p a b", a=kc)
            src = wd[r0:r0 + kc * 128, c0:c0 + ncols].rearrange("(a p) e -> p a e", p=128)
            fw.dma("pool", f"w{s}", view, src, writes=t_slot[s])
            return view, t_slot[s]

        def wload2(wdA, wdB, c0):
            s = wcount[0] % NSLOT
            wcount[0] += 1
            view = slot_ap[s].rearrange("p (h a b) -> p h a b", h=2, a=8)
            for hh, wd in enumerate((wdA, wdB)):
                src = wd[0:1024, c0:c0 + 256].rearrange("(a p) e -> p a e", p=128)
                fw.dma("pool", f"w{s}" + "ab"[hh], view[:, hh], src, writes=[t_slot[s][hh]])
            return view, t_slot[s]

        scol = [0]

        def stat(n=1):
            c = scol[0]
            scol[0] += n
            assert scol[0] <= 255
            return stats[:, c:c + n], Tile(f"stat{c}")

        fw.dma("sp", "consts", consts, consts_d[:, :], writes=[t_consts])
        scr = regf(188, 2)
        t_scr = Tile("scr")
        fw.op("pool", lambda e: e.memset(scr[:, 0:128], 1.0), writes=[t_scr])
        fw.op("pool", lambda e: e.affine_select(out=scr[:, 0:128], in_=scr[:, 0:128], pattern=[[-1, 128]],
                                                compare_op=ALU.is_equal, fill=0.0, base=0, channel_multiplier=1),
              reads=[t_scr], writes=[t_scr])
        fw.op("dve", lambda e: e.tensor_copy(ident, scr[:, 0:128]), reads=[t_scr], writes=[t_cmat])
        fw.op("pool", lambda e: e.memset(scr[:, 128:256], -1.0), writes=[t_scr])
        fw.op("pool", lambda e: e.affine_select(out=scr[:, 128:256], in_=scr[:, 128:256], pattern=[[-1, 128]],
                                                compare_op=ALU.is_ge, fill=0.0, base=0, channel_multiplier=1),
              reads=[t_scr], writes=[t_scr])
        fw.op("dve", lambda e: e.tensor_copy(negU, scr[:, 128:256]), reads=[t_scr], writes=[t_cmat])
        fw.op("pool", lambda e: e.memset(scr[:, 256:384], -1.0), writes=[t_scr])
        fw.op("dve", lambda e: e.tensor_copy(negones, scr[:, 256:384]), reads=[t_scr], writes=[t_cmat])
        fw.op("pool", lambda e: e.memset(onesf, 1.0 / 1024.0), writes=[t_cmat])
        fw.op("pool", lambda e: e.memset(epsc, EPS), writes=[t_cmat])
        fw.op("dve", lambda e: e.tensor_scalar(consts[:, C_BQS:C_BQS + 8], consts[:, C_BIN:C_BIN + 8], SCALE, None,
                                               ALU.mult), reads=[t_consts], writes=[t_consts])

        fw.barrier()

        checkpoint("const")
        hT = regb(60, 32).rearrange("p (a b) -> p a b", a=16)
        t_hT = [Tile(f"hT{i}") for i in range(8)]
        KT = regb(92, 32).rearrange("p (a b) -> p a b", a=8)
        t_KT = [[Tile(f"KT{h}_{q}") for q in range(4)] for h in range(8)]
        Vt = regb(124, 32).rearrange("p (a b) -> p a b", a=16)
        t_Vt = [Tile(f"Vt{i}") for i in range(16)]
        QT = regb(156, 16).rearrange("p (a b) -> p a b", a=8)
        t_QT = [[Tile(f"QT{h}_{q}") for q in range(2)] for h in range(8)]
        xbuf = [regf(172, 8), regf(180, 8)]
        t_xbuf = [Tile("xbuf0"), Tile("xbuf1")]
        hbf = [regb(188, 4), regb(192, 4)]
        t_hbf = [Tile("hbf0"), Tile("hbf1")]
        gslot = regf(196, 8)
        t_gslot = Tile("gslotA")

        bank_rr = [0]

        def next_bank():
            b = bank_rr[0] % 8
            bank_rr[0] += 1
            return b

        def rstd_from_ss(ss_ap, t_ss, n_inv, out_ap, t_out):
            fw.op("act", lambda e: e.activation(out_ap, ss_ap, AF.Sqrt, bias=epsc, scale=n_inv),
                  reads=[t_ss, t_cmat], writes=[t_out])
            fw.op("dve", lambda e: e.reciprocal(out_ap, out_ap), reads=[t_out], writes=[t_out])

        def prenorm(src_ap, t_src, hb, t_hb, g_ap, t_g):
            ss, t_ss = stat()
            rs, t_rs = stat()
            fw.op("act", lambda e: e.activation(hb, src_ap, AF.Square, accum_out=ss),
                  reads=[t_src], writes=[t_hb, t_ss])
            rstd_from_ss(ss, t_ss, 1.0 / D, rs, t_rs)
            fw.op("dve", lambda e: e.scalar_tensor_tensor(hb, src_ap, rs, g_ap, ALU.mult, ALU.mult),
                  reads=[t_src, t_rs, t_g], writes=[t_hb])

        def transpose_block(hb, t_hb, dstT, t_dst, col0, banks=None):
            for half in range(2):
                b = next_bank() if banks is None else banks[half]
                pT = ps[b][:].bitcast(BF16).rearrange("p (a b) -> p a b", a=8)
                for j in range(8):
                    kc = half * 8 + j
                    fw.op("pe", lambda e: e.transpose(pT[:, j, :], hb[:, kc * 128:(kc + 1) * 128], ident),
                          reads=[t_hb, t_cmat], writes=[t_ps[b]], inc=(j == 7))
                dst = dstT[:, half * 8:(half + 1) * 8, col0:col0 + 128]
                if half == 0:
                    fw.op("act", lambda e: e.activation(dst, pT, AF.Copy), reads=[t_ps[b]], writes=[t_dst])
                else:
                    fw.op("dve", lambda e: e.tensor_copy(dst, pT), reads=[t_ps[b]], writes=[t_dst])

        def prenorm_transpose(src_ap, t_src, i, g_ap, t_g, dstT, t_dst, col0):
            prenorm(src_ap, t_src, hbf[i], t_hbf[i], g_ap, t_g)
            transpose_block(hbf[i], t_hbf[i], dstT, t_dst, col0)

        def load_g(idx, slot, t_slot_):
            fw.dma("sp", "g" + t_slot_.name, slot, gvec_d[idx], writes=[t_slot_])

        load_g(0, gslot, t_gslot)
        fw.dma("sp", "bv", bvbc, bv_d[:, :], writes=[t_bvbc])
        bvf_ap = None

        def phaseA(row0):
            for blk in range(8):
                i = blk % 2
                fw.dma("sp", f"xb{i}", xbuf[i], xin[row0 + blk * 128: row0 + (blk + 1) * 128, :],
                       writes=[t_xbuf[i]])
                prenorm_transpose(xbuf[i], t_xbuf[i], i, gslot, t_gslot, hT, t_hT[blk], blk * 128)

        def proj_fm(wd, c0_list, dst_fn, nblk_tok, bias_col_fn, scale, func=AF.Identity):
            for ti, c0 in enumerate(c0_list):
                wv, t_w = wload(wd, 0, 16, c0, 256)
                for ec in range(2):
                    banks = [next_bank() for _ in range(nblk_tok)]
                    for kc in range(16):
                        for th in range(nblk_tok):
                            b = banks[th]
                            fw.op("pe", lambda e: e.matmul(ps[b][:], wv[:, kc, ec * 128:(ec + 1) * 128],
                                                           hT[:, kc, th * 512:(th + 1) * 512],
                                                           start=(kc == 0), stop=(kc == 15)),
                                  reads=t_w + t_hT[th * 4:(th + 1) * 4], writes=[t_ps[b]], inc=(kc == 15))
                    for th in range(nblk_tok):
                        b = banks[th]
                        dst, t_dst = dst_fn(ti * 2 + ec, th)
                        bias = bias_col_fn(ti * 2 + ec)
                        fw.op("act", lambda e: e.activation(dst, ps[b][:], func, bias=bias, scale=scale),
                              reads=[t_ps[b], t_consts], writes=[t_dst])

        def projV(blk0, is_ctx):
            for cg in range(2):
                wts = [wload(w_in, kt * 1024, 8, 2048 + cg * 512, 512) for kt in range(2)]
                for tbh in range(2):
                    banks = [next_bank() for _ in range(4)]
                    for kt in range(2):
                        wv, t_w = wts[kt]
                        for tbl in range(4):
                            tb = tbh * 4 + tbl
                            b = banks[tbl]
                            for kc in range(8):
                                fw.op("pe", lambda e: e.matmul(ps[b][:], hT[:, kt * 8 + kc, tb * 128:(tb + 1) * 128],
                                                               wv[:, kc, :], start=(kt == 0 and kc == 0),
                                                               stop=(kt == 1 and kc == 7), skip_group_check=True),
                                      reads=t_w + [t_hT[tb]], writes=[t_ps[b]], inc=(kt == 1 and kc == 7))
                    for tbl in range(4):
                        tb = tbh * 4 + tbl
                        b = banks[tbl]
                        dst = Vt[:, blk0 + tb, cg * 512:(cg + 1) * 512]
                        if is_ctx:
                            fw.op("dve", lambda e: e.scalar_tensor_tensor(dst, ps[b][:], flag,
                                                                          bvf_ap[:, cg * 512:(cg + 1) * 512],
                                                                          ALU.mult, ALU.add),
                                  reads=[t_ps[b], t_consts, t_bvf], writes=[t_Vt[blk0 + tb]])
                        else:
                            fw.op("dve", lambda e: e.tensor_tensor(dst, ps[b][:],
                                                                   bvbc[:, cg * 512:(cg + 1) * 512], ALU.add),
                                  reads=[t_ps[b], t_bvbc], writes=[t_Vt[blk0 + tb]])

        bvf_ap = regf(156, 4)
        t_bvf = Tile("bvf")
        fw.op("dve", lambda e: e.tensor_scalar(bvf_ap, bvbc, flag, None, ALU.mult),
              reads=[t_bvbc, t_consts], writes=[t_bvf])

        phaseA(0)
        fw.op("dve", lambda e: e.tensor_copy(hTc_last, hT[:, :, 896:1024]), reads=[t_hT[7]], writes=[t_hTc_last])
        proj_fm(w_in, [1024 + 256 * i for i in range(4)],
                lambda ch, th: (KT[:, ch, th * 512:(th + 1) * 512], t_KT[ch][th]), 2,
                lambda ch: consts[:, C_BIN + 8 + ch:C_BIN + 9 + ch], 1.0)
        projV(0, True)
        fw.barrier()
        checkpoint("Bctx")
        phaseA(NT)
        proj_fm(w_in, [1024 + 256 * i for i in range(4)],
                lambda ch, th: (KT[:, ch, 1024 + th * 512:1024 + (th + 1) * 512], t_KT[ch][2 + th]), 2,
                lambda ch: consts[:, C_BIN + 8 + ch:C_BIN + 9 + ch], 1.0)
        projV(8, False)
        proj_fm(w_in, [256 * i for i in range(4)],
                lambda ch, th: (QT[:, ch, th * 512:(th + 1) * 512], t_QT[ch][th]), 2,
                lambda ch: consts[:, C_BQS + ch:C_BQS + ch + 1], SCALE)
        fw.barrier()
        dump("hT", hT, [128, 16, 1024], t_hT, BF16)
        dump("KT", KT, [128, 8, 2048], [t for r in t_KT for t in r], BF16)
        dump("Vt", Vt, [128, 16, 1024], t_Vt, BF16)
        dump("QT", QT, [128, 8, 1024], [t for r in t_QT for t in r], BF16)
        checkpoint("B")

        scr2 = regf(204, 2)
        t_scr2 = Tile("scr2")
        for j in range(4):
            fw.op("pool", lambda e: e.memset(scr2, 1.0), writes=[t_scr2])
            fw.op("pool", lambda e: e.affine_select(out=scr2, in_=scr2, pattern=[[1, 512]], compare_op=ALU.is_gt,
                                                    fill=0.0, base=-128 * j, channel_multiplier=-1),
                  reads=[t_scr2], writes=[t_scr2])
            fw.op("dve", lambda e: e.tensor_copy(masks[:, j, :], scr2), reads=[t_scr2], writes=[t_masks])
        fw.barrier()

        oT = regb(172, 16).rearrange("p (a b) -> p a b", a=8)
        t_oT = [[Tile(f"oT{h}_{q}") for q in range(2)] for h in range(8)]

        NCH = 4
        sp_all = regb(188, 4).rearrange("p (a b) -> p a b", a=4)
        a_all = regb(192, 4).rearrange("p (a b) -> p a b", a=4)
        Sb_all = regb(196, 4).rearrange("p (a b) -> p a b", a=4)
        S_all = regf(52, 8).rearrange("p (a b) -> p a b", a=4)
        ebuf = regf(200, 4)
        t_e = Tile("ebuf")
        t_sp = [Tile("spP0"), Tile("spP1")]
        t_a = [Tile("aP0"), Tile("aP1")]
        t_S = [Tile("SP0"), Tile("SP1")]
        t_Sb = [Tile("SbP0"), Tile("SbP1")]

        def pair(ap3, pr):
            return ap3[:, 2 * pr:2 * pr + 2, :]

        def attn_group(g, heads):
            nsteps = 8 + 4 * g + 4
            qsl = slice(g * 512, (g + 1) * 512)

            def QK(ci, kb):
                hd = heads[ci]
                fw.op("pe", lambda e: e.matmul(ps[ci][:], KT[:, hd, kb * 128:(kb + 1) * 128], QT[:, hd, qsl],
                                               start=True, stop=True),
                      reads=[t_KT[hd][kb // 4], t_QT[hd][g]], writes=[t_ps[ci]])

            for ci in range(NCH):
                QK(ci, nsteps - 1)
            for s in range(nsteps):
                kb = nsteps - 1 - s
                first, last = s == 0, s == nsteps - 1
                j = kb - (8 + 4 * g)
                diag = j >= 0
                if diag:
                    m3 = masks[:, j, :].unsqueeze(1).broadcast_to([128, 2, 512])
                for pr in range(2):
                    zA = ps_all[:, 2 * pr * 512:(2 * pr + 2) * 512]
                    tz = [t_ps[2 * pr], t_ps[2 * pr + 1]]
                    spP = pair(sp_all, pr)
                    fw.op("act", lambda e: e.activation(ebuf, zA, AF.Exp), reads=tz, writes=[t_e])
                    fw.op("act", lambda e: e.activation(spP, ebuf.rearrange("p (a b) -> p a b", a=2), AF.Ln, bias=1.0),
                          reads=[t_e], writes=[t_sp[pr]])
                    if diag:
                        fw.op("dve", lambda e: e.tensor_tensor(spP, spP, m3, ALU.mult),
                              reads=[t_sp[pr], t_masks], writes=[t_sp[pr]])
                for ci in range(NCH):
                    pr = ci // 2
                    fw.op("pe", lambda e: e.matmul(ps[ci][:], negU, sp_all[:, ci, :], start=False, stop=first,
                                                   skip_group_check=True),
                          reads=[t_sp[pr], t_cmat], writes=[t_ps[ci]], inc=first)
                    if not first:
                        fw.op("pe", lambda e: e.matmul(ps[ci][:], negones, Sb_all[:, ci, :], start=False, stop=True,
                                                       skip_group_check=True),
                              reads=[t_Sb[pr], t_cmat], writes=[t_ps[ci]])
                for pr in range(2):
                    zA = ps_all[:, 2 * pr * 512:(2 * pr + 2) * 512]
                    tz = [t_ps[2 * pr], t_ps[2 * pr + 1]]
                    aP = pair(a_all, pr)
                    fw.op("act", lambda e: e.activation(aP, zA.rearrange("p (a b) -> p a b", a=2), AF.Exp),
                          reads=tz, writes=[t_a[pr]])
                    if diag:
                        fw.op("dve", lambda e: e.tensor_tensor(aP, aP, m3, ALU.mult),
                              reads=[t_a[pr], t_masks], writes=[t_a[pr]])
                for ci in range(NCH):
                    pr = ci // 2
                    hd = heads[ci]
                    bO = 4 + ci
                    if not last:
                        QK(ci, kb - 1)
                    fw.op("pe", lambda e: e.matmul(ps[bO][:], Vt[:, kb, hd * 128:(hd + 1) * 128], a_all[:, ci, :],
                                                   start=first, stop=last),
                          reads=[t_Vt[kb], t_a[pr]], writes=[t_ps[bO]], inc=last)
                if not last:
                    for pr in range(2):
                        SP_, spP, SbP = pair(S_all, pr), pair(sp_all, pr), pair(Sb_all, pr)
                        if first:
                            fw.op("dve", lambda e: e.tensor_copy(SP_, spP), reads=[t_sp[pr]], writes=[t_S[pr]])
                        else:
                            fw.op("dve", lambda e: e.tensor_tensor(SP_, SP_, spP, ALU.add),
                                  reads=[t_S[pr], t_sp[pr]], writes=[t_S[pr]])
                        fw.op("dve", lambda e: e.tensor_copy(SbP, SP_), reads=[t_S[pr]], writes=[t_Sb[pr]])
                else:
                    for ci in range(NCH):
                        hd = heads[ci]
                        fw.op("dve", lambda e: e.tensor_copy(oT[:, hd, qsl], ps[4 + ci][:]),
                              reads=[t_ps[4 + ci]], writes=[t_oT[hd][g]])

        for g in range(2):
            for hg in range(2):
                attn_group(g, [hg * 4 + i for i in range(4)])
        fw.barrier()
        dump("oT", oT, [128, 8, 1024], [t for r in t_oT for t in r], BF16)
        checkpoint("C")

        ycv = regf(92, 32).rearrange("p (a b) -> p a b", a=8)
        t_ycv = [[Tile(f"ycv{c}_{h}") for h in range(2)] for c in range(8)]
        swT = regb(124, 16).rearrange("p (a b) -> p a b", a=8)
        t_swT = [Tile(f"swT{c}") for c in range(8)]
        ubuf = [regb(140, 2.25)[:, 0:1054], regb(142.25, 2.25)[:, 0:1054]]
        t_u = [[Tile(f"u{i}_{h}") for h in range(3)] for i in range(2)]
        dg = regb(52, 7.75).rearrange("p (a b) -> p a b", a=31)
        t_dg = Tile("dg")
        ysq = [regf(149, 4), regf(153, 4)]
        t_ysq = [Tile("ysq0"), Tile("ysq1")]
        mean_sb = regf(157, 4); t_mean = Tile("mean")
        rstd_sb = regf(161, 4); t_rstd = Tile("rstd")
        var_sb = regf(165, 4); t_var = Tile("var")
        t1 = [regf(188, 4), regf(192, 4)]
        t_t1 = [Tile("t1_0"), Tile("t1_1")]
        acc1, acc2 = t1
        t_acc1, t_acc2 = t_t1
        sg = [regf(196, 2), regf(198, 2), regf(200, 2)]
        t_sg = [Tile("sg0"), Tile("sg1"), Tile("sg2")]
        ident3 = ident.unsqueeze(1).broadcast_to([128, 31, 128])

        def conv_dg(c):
            w3 = consts[:, C_WDW + c * 31: C_WDW + (c + 1) * 31].unsqueeze(2).broadcast_to([128, 31, 128])
            fw.op("dve", lambda e: e.tensor_tensor(dg, ident3, w3, ALU.mult),
                  reads=[t_cmat, t_consts], writes=[t_dg])

        def conv_chunk(c):
            ui = c % 2
            u = ubuf[ui]
            cb = [4, 5] if c % 2 == 0 else [6, 7]
            for th in range(2):
                rd = [t_u[ui][0], t_u[ui][1]] if th == 0 else [t_u[ui][1], t_u[ui][2]]
                for jtap in range(31):
                    fw.op("pe", lambda e: e.matmul(ps[cb[th]][:], dg[:, jtap, :],
                                                   u[:, jtap + th * 512: jtap + th * 512 + 512],
                                                   start=(jtap == 0), stop=(jtap == 30)),
                          reads=rd + [t_dg], writes=[t_ps[cb[th]]], inc=(jtap == 30))
                fw.op("act", lambda e: e.activation(ycv[:, c, th * 512:(th + 1) * 512], ps[cb[th]][:], AF.Identity,
                                                    bias=consts[:, C_BDW + c:C_BDW + c + 1]),
                      reads=[t_ps[cb[th]], t_consts], writes=[t_ycv[c][th]])
            qi = c % 2
            fw.op("act", lambda e: e.activation(ysq[qi], ycv[:, c, :], AF.Square),
                  reads=t_ycv[c], writes=[t_ysq[qi]])
            if c == 0:
                fw.op("dve", lambda e: e.tensor_copy(acc1, ycv[:, c, :]), reads=t_ycv[c], writes=[t_acc1])
                fw.op("dve", lambda e: e.tensor_copy(acc2, ysq[qi]), reads=[t_ysq[qi]], writes=[t_acc2])
            else:
                fw.op("dve", lambda e: e.tensor_tensor(acc1, acc1, ycv[:, c, :], ALU.add),
                      reads=t_ycv[c] + [t_acc1], writes=[t_acc1])
                fw.op("dve", lambda e: e.tensor_tensor(acc2, acc2, ysq[qi], ALU.add),
                      reads=[t_ysq[qi], t_acc2], writes=[t_acc2])

        for tpair in range(4):
            wa, t_wa = wload(w_in, 0, 16, 3072 + tpair * 256, 256)
            wb_, t_wb = wload(w_in, 0, 16, 4096 + tpair * 256, 256)
            for ec in range(2):
                c = tpair * 2 + ec
                ui = c % 2
                u = ubuf[ui]
                if c > 0:
                    conv_dg(c - 1)
                bh = 0
                for (wv_, t_wv, cs) in ((wa, t_wa, 0), (wb_, t_wb, 128)):
                    for kc in range(16):
                        fw.op("pe", lambda e: e.matmul(ps[bh][:, cs:cs + 128], wv_[:, kc, ec * 128:(ec + 1) * 128],
                                                       hTc_last[:, kc, :], start=(kc == 0), stop=(kc == 15)),
                              reads=t_wv + [t_hTc_last], writes=[t_ps[bh]], inc=(kc == 15))
                ba_col = consts[:, C_BIN + 24 + c:C_BIN + 25 + c]
                bb_col = consts[:, C_BIN + 32 + c:C_BIN + 33 + c]
                fw.op("act", lambda e: e.activation(sg[2][:, 0:30], ps[bh][:, 128 + 98:256], AF.Sigmoid, bias=bb_col),
                      reads=[t_ps[bh], t_consts], writes=[t_sg[2]])
                fw.op("dve", lambda e: e.scalar_tensor_tensor(u[:, 0:30], ps[bh][:, 98:128], ba_col, sg[2][:, 0:30],
                                                              ALU.add, ALU.mult),
                      reads=[t_ps[bh], t_consts, t_sg[2]], writes=[t_u[ui][0]])
                fw.op("dve", lambda e: e.tensor_scalar(u[:, 0:30], u[:, 0:30], flag, None, ALU.mult),
                      reads=[t_u[ui][0], t_consts], writes=[t_u[ui][0]])
                for th in range(2):
                    bA_, bB_ = (1, 2) if th == 0 else (3, 0)
                    for (wv_, t_wv, bb) in ((wa, t_wa, bA_), (wb_, t_wb, bB_)):
                        for kc in range(16):
                            fw.op("pe", lambda e: e.matmul(ps[bb][:], wv_[:, kc, ec * 128:(ec + 1) * 128],
                                                           hT[:, kc, th * 512:(th + 1) * 512],
                                                           start=(kc == 0), stop=(kc == 15)),
                                  reads=t_wv + t_hT[th * 4:(th + 1) * 4], writes=[t_ps[bb]], inc=(kc == 15))
                    fw.op("act", lambda e: e.activation(sg[th], ps[bB_][:], AF.Sigmoid, bias=bb_col),
                          reads=[t_ps[bB_], t_consts], writes=[t_sg[th]])
                    fw.op("dve", lambda e: e.scalar_tensor_tensor(u[:, 30 + th * 512: 30 + (th + 1) * 512], ps[bA_][:],
                                                                  ba_col, sg[th], ALU.add, ALU.mult),
                          reads=[t_ps[bA_], t_consts, t_sg[th]], writes=[t_u[ui][1 + th]])
                if c > 0:
                    conv_chunk(c - 1)
        conv_dg(7)
        conv_chunk(7)
        B_MEAN = [0, 1]
        B_MSQ = [2, 3]
        for th in range(2):
            sl = slice(th * 512, (th + 1) * 512)
            fw.op("pe", lambda e: e.matmul(ps[B_MEAN[th]][:], onesf, acc1[:, sl], start=True, stop=True),
                  reads=[t_cmat, t_acc1], writes=[t_ps[B_MEAN[th]]])
            fw.op("pe", lambda e: e.matmul(ps[B_MSQ[th]][:], onesf, acc2[:, sl], start=True, stop=True),
                  reads=[t_cmat, t_acc2], writes=[t_ps[B_MSQ[th]]])
        for th in range(2):
            sl = slice(th * 512, (th + 1) * 512)
            fw.op("act", lambda e: e.activation(mean_sb[:, sl], ps[B_MEAN[th]][:], AF.Copy),
                  reads=[t_ps[B_MEAN[th]]], writes=[t_mean])
            fw.op("dve", lambda e: e.tensor_tensor(var_sb[:, sl], mean_sb[:, sl], mean_sb[:, sl], ALU.mult),
                  reads=[t_mean], writes=[t_var])
            fw.op("dve", lambda e: e.tensor_tensor(var_sb[:, sl], ps[B_MSQ[th]][:], var_sb[:, sl], ALU.subtract),
                  reads=[t_ps[B_MSQ[th]], t_var], writes=[t_var])
            fw.op("act", lambda e: e.activation(rstd_sb[:, sl], var_sb[:, sl], AF.Sqrt, bias=epsc),
                  reads=[t_var, t_cmat], writes=[t_rstd])
            fw.op("dve", lambda e: e.reciprocal(rstd_sb[:, sl], rstd_sb[:, sl]), reads=[t_rstd], writes=[t_rstd])
        for c in range(8):
            qi = c % 2
            fw.op("dve", lambda e: e.tensor_tensor(t1[qi], ycv[:, c, :], mean_sb, ALU.subtract),
                  reads=t_ycv[c] + [t_mean], writes=[t_t1[qi]])
            fw.op("dve", lambda e: e.tensor_tensor(t1[qi], t1[qi], rstd_sb, ALU.mult),
                  reads=[t_t1[qi], t_rstd], writes=[t_t1[qi]])
            fw.op("act", lambda e: e.activation(swT[:, c, :], t1[qi], AF.Silu,
                                                bias=consts[:, C_BLN + c:C_BLN + c + 1],
                                                scale=consts[:, C_GLN + c:C_GLN + c + 1]),
                  reads=[t_t1[qi], t_consts], writes=[t_swT[c]])
        fw.barrier()
        dump("ycv", regf(92, 32), [128, 8192], [t for r in t_ycv for t in r])
        dump("swT", swT, [128, 8, 1024], t_swT, BF16)
        checkpoint("D")

        mergedT = regb(92, 32).rearrange("p (a b) -> p a b", a=16)
        t_mT = [Tile(f"mT{i}") for i in range(8)]
        sbcv_ap = [regb(188 + 8 * i, 8).rearrange("p (h a b) -> p h a b", h=2, a=8) for i in range(2)]
        t_sbcv = [[Tile(f"sbcv{i}a"), Tile(f"sbcv{i}b")] for i in range(2)]
        fw.fence(t_t1 + t_sg, t_sbcv[0] + t_sbcv[1])
        etmp = [[regf(140 + 8 * bi + 2 * k, 2) for k in range(4)] for bi in range(2)]
        t_etmp = [[Tile(f"et{bi}_{k}") for k in range(4)] for bi in range(2)]
        it = 0
        for eg2 in range(4):
            for sub in range(2):
                eg = eg2 * 2 + sub
                wg1, t_wg1 = wload(w_in, 0, 16, 5120 + eg * 256, 256)
                wg2, t_wg2 = wload(w_in, 0, 16, 7168 + eg * 256, 256)
                pv = eg % 2
                wsc = sbcv_ap[pv]
                t_wsc = t_sbcv[pv]
                for hh, wd in enumerate((w_sbo, w_cvo)):
                    fw.dma("pool", f"sbcv{pv}" + "ab"[hh], wsc[:, hh],
                           wd[0:1024, eg * 256:(eg + 1) * 256].rearrange("(a p) e -> p a e", p=128),
                           writes=[t_sbcv[pv][hh]])
                wsb, wcv = wsc[:, 0], wsc[:, 1]
                for ec in range(2):
                    ech = eg * 2 + ec
                    wcol = ec * 128
                    for th in range(2):
                        bi = it % 2
                        it += 1
                        bG1, bG2, bP1, bP2 = [bi * 4 + k for k in range(4)]
                        tsl = slice(th * 512, (th + 1) * 512)
                        for kc in range(16):
                            fw.op("pe", lambda e: e.matmul(ps[bG1][:], wg1[:, kc, ec * 128:(ec + 1) * 128], hT[:, kc, tsl],
                                                           start=(kc == 0), stop=(kc == 15)),
                                  reads=t_wg1 + t_hT[th * 4:(th + 1) * 4], writes=[t_ps[bG1]], inc=(kc == 15))
                        for kc in range(16):
                            fw.op("pe", lambda e: e.matmul(ps[bG2][:], wg2[:, kc, ec * 128:(ec + 1) * 128], hT[:, kc, tsl],
                                                           start=(kc == 0), stop=(kc == 15)),
                                  reads=t_wg2 + t_hT[th * 4:(th + 1) * 4], writes=[t_ps[bG2]], inc=(kc == 15))
                        for kc in range(8):
                            fw.op("pe", lambda e: e.matmul(ps[bP1][:], wsb[:, kc, wcol:wcol + 128], oT[:, kc, tsl],
                                                           start=(kc == 0), stop=(kc == 7)),
                                  reads=t_wsc + [t_oT[kc][th]], writes=[t_ps[bP1]], inc=(kc == 7))
                        for kc in range(8):
                            fw.op("pe", lambda e: e.matmul(ps[bP2][:], wcv[:, kc, wcol:wcol + 128], swT[:, kc, tsl],
                                                           start=(kc == 0), stop=(kc == 7)),
                                  reads=t_wsc + [t_swT[kc]], writes=[t_ps[bP2]], inc=(kc == 7))
                        s1, s2, m1, m2 = etmp[bi]
                        ts1, ts2, tm1, tm2 = t_etmp[bi]
                        fw.op("act", lambda e: e.activation(s1, ps[bG1][:], AF.Sigmoid,
                                                            bias=consts[:, C_BIN + 40 + ech:C_BIN + 41 + ech]),
                              reads=[t_ps[bG1], t_consts], writes=[ts1])
                        fw.op("act", lambda e: e.activation(s2, ps[bG2][:], AF.Sigmoid,
                                                            bias=consts[:, C_BIN + 56 + ech:C_BIN + 57 + ech]),
                              reads=[t_ps[bG2], t_consts], writes=[ts2])
                        fw.op("dve", lambda e: e.tensor_tensor(m1, ps[bP1][:], s1, ALU.mult),
                              reads=[t_ps[bP1], ts1], writes=[tm1])
                        fw.op("dve", lambda e: e.tensor_tensor(m2, ps[bP2][:], s2, ALU.mult),
                              reads=[t_ps[bP2], ts2], writes=[tm2])
                        fw.op("dve", lambda e: e.tensor_tensor(mergedT[:, ech, tsl], m1, m2, ALU.add),
                              reads=[tm1, tm2], writes=t_mT[th * 4:(th + 1) * 4])
        fw.barrier()
        dump("mT", mergedT, [128, 16, 1024], t_mT, BF16)
        checkpoint("E")

        x1 = regf(143, 64).rearrange("p (a b) -> p a b", a=8)
        t_x1 = [Tile(f"x1_{i}") for i in range(8)]
        xbF = [regf(60, 8), regf(68, 8)]
        t_xbF = [Tile("xbF0"), Tile("xbF1")]
        gslotF = regf(76, 8); t_gslotF = Tile("gslotF")
        junk = regb(84, 1); t_junk = Tile("junk")
        hbfA = regb(85, 4); t_hbfA = Tile("hbfA")
        hbfB = regb(108, 4); t_hbfB = Tile("hbfB")
        gslotG = regf(52, 8); t_gslotG = Tile("gslotG")
        gslotH = regf(100, 8); t_gslotH = Tile("gslotH")
        h2T = regb(124, 16).rearrange("p (a b) -> p a b", a=16)
        t_h2T = [Tile(f"h2T{i}") for i in range(4)]
        rtmp = [regf(8, 2), regf(10, 2)]
        t_rtmp = [Tile("rtmp0"), Tile("rtmp1")]
        ssqF, t_ssqF = stat(64)

        load_g(1, gslotF, t_gslotF)
        load_g(2, gslotG, t_gslotG)

        def F_cg(half, cg):
            wts = [wload(w_o, kt * 1024, 8, cg * 512, 512) for kt in range(2)]
            banks = [(cg * 4 + tbl) % 6 for tbl in range(4)]
            for kt in range(2):
                wv, t_w = wts[kt]
                for tbl in range(4):
                    tb = half * 4 + tbl
                    b = banks[tbl]
                    for kc in range(8):
                        fw.op("pe", lambda e: e.matmul(ps[b][:], mergedT[:, kt * 8 + kc, tb * 128:(tb + 1) * 128],
                                                       wv[:, kc, :], start=(kt == 0 and kc == 0),
                                                       stop=(kt == 1 and kc == 7), skip_group_check=True),
                              reads=t_w + [t_mT[tb]], writes=[t_ps[b]], inc=(kt == 1 and kc == 7))
            for tbl in range(4):
                tb = half * 4 + tbl
                b = banks[tbl]
                fw.op("dve", lambda e: e.tensor_copy(x1[:, tb, cg * 512:(cg + 1) * 512], ps[b][:]),
                      reads=[t_ps[b]], writes=[t_x1[tb]])
                fw.op("act", lambda e: e.activation(junk, x1[:, tb, cg * 512:(cg + 1) * 512], AF.Square,
                                                    accum_out=ssqF[:, tb * 8 + cg: tb * 8 + cg + 1]),
                      reads=[t_x1[tb]], writes=[t_junk, t_ssqF])

        def F_post(half):
            for tbl in range(4):
                tb = half * 4 + tbl
                i = tb % 2
                fw.dma("sp", f"xbF{i}", xbF[i], xin[NT + tb * 128: NT + (tb + 1) * 128, :], writes=[t_xbF[i]])
                ss, t_ss = stat()
                rs, t_rs = stat()
                fw.op("dve", lambda e: e.tensor_reduce(ss, ssqF[:, tb * 8:tb * 8 + 4], mybir.AxisListType.X, ALU.add),
                      reads=[t_ssqF], writes=[t_ss])
                rstd_from_ss(ss, t_ss, 1.0 / D, rs, t_rs)
                fw.op("dve", lambda e: e.scalar_tensor_tensor(x1[:, tb, :], x1[:, tb, :], rs, gslotF, ALU.mult, ALU.mult),
                      reads=[t_x1[tb], t_rs, t_gslotF], writes=[t_x1[tb]])
                fw.op("dve", lambda e: e.tensor_tensor(x1[:, tb, :], x1[:, tb, :], xbF[i], ALU.add),
                      reads=[t_x1[tb], t_xbF[i]], writes=[t_x1[tb]])

        for cg in range(4):
            F_cg(0, cg)
        F_post(0)
        for cg in range(4):
            tbl = cg
            prenorm(x1[:, tbl, :], t_x1[tbl], hbfA, t_hbfA, gslotG, t_gslotG)
            F_cg(1, cg)
            transpose_block(hbfA, t_hbfA, h2T, t_h2T[tbl], tbl * 128, banks=(6, 7))
        F_post(1)
        h2Tb = regb(108, 16).rearrange("p (a b) -> p a b", a=16)
        t_h2Tb = [Tile(f"h2Tb{i}") for i in range(4)]
        fT = regb(92, 16).rearrange("p (a b) -> p a b", a=8)
        t_fT = [[Tile(f"fT{c}_{h}") for h in range(2)] for c in range(8)]
        fw.fence(t_mT, t_h2Tb + [t for r in t_fT for t in r])
        h2Th = [h2T, h2Tb]
        t_h2Th = [t_h2T, t_h2Tb]
        t_out = [Tile(f"out{i}") for i in range(8)]
        for tb in range(8):
            fw.dma("sp", f"outA{tb}", out_d[tb * 128:(tb + 1) * 128, :], x1[:, tb, :],
                   reads=[t_x1[tb]], writes=[t_out[tb]])
        dump("x1", regf(143, 64), [128, 8 * 2048], t_x1)
        dump("h2T", h2T, [128, 16, 512], t_h2T, BF16)
        checkpoint("F")

        y2, t_y2 = x1, t_x1
        gslotH = regf(60, 8); t_gslotH = Tile("gslotH")
        junk2 = regb(68, 4); t_junk2 = Tile("junk2")
        fw.fence(t_xbF + [t_gslotF, t_junk, t_hbfA], [t_gslotH, t_junk2])
        load_g(3, gslotH, t_gslotH)
        for e8 in range(8):
            def up_chunk(wv, t_w, ti, ec, th, b):
                ch = ti * 2 + ec
                for kc in range(16):
                    fw.op("pe", lambda e: e.matmul(ps[b][:], wv[:, kc, ec * 128:(ec + 1) * 128],
                                                   h2Th[th][:, kc, :], start=(kc == 0), stop=(kc == 15)),
                          reads=t_w + t_h2Th[th], writes=[t_ps[b]], inc=(kc == 15))
                fw.op("act", lambda e: e.activation(rtmp[th], ps[b][:], AF.Relu),
                      reads=[t_ps[b]], writes=[t_rtmp[th]])
                fw.op("dve", lambda e: e.tensor_tensor(fT[:, ch, th * 512:(th + 1) * 512], rtmp[th], rtmp[th],
                                                       ALU.mult),
                      reads=[t_rtmp[th]], writes=[t_fT[ch][th]])

            if e8 == 0:
                tiles = [wload(w_up, 0, 16, ti * 256, 256) for ti in range(4)]
                prenorm(x1[:, 4, :], t_x1[4], hbfA, t_hbfA, gslotG, t_gslotG)
                for ti in range(4):
                    for ec in range(2):
                        up_chunk(tiles[ti][0], tiles[ti][1], ti, ec, 0, 4 + ec)
                    transpose_block(hbfA, t_hbfA, h2Tb, t_h2Tb[ti], ti * 128, banks=(6, 7))
                    if ti < 3:
                        prenorm(x1[:, 5 + ti, :], t_x1[5 + ti], hbfA, t_hbfA, gslotG, t_gslotG)
                for ti in range(4):
                    for ec in range(2):
                        up_chunk(tiles[ti][0], tiles[ti][1], ti, ec, 1, 4 + (ti * 2 + ec) % 4)
            else:
                for ti in range(4):
                    wv, t_w = wload(w_up, 0, 16, e8 * 1024 + ti * 256, 256)
                    for ec in range(2):
                        ch = ti * 2 + ec
                        banks = [4 + (ch % 2) * 2, 5 + (ch % 2) * 2]
                        for kc in range(16):
                            for th in range(2):
                                b = banks[th]
                                fw.op("pe", lambda e: e.matmul(ps[b][:], wv[:, kc, ec * 128:(ec + 1) * 128],
                                                               h2Th[th][:, kc, :], start=(kc == 0), stop=(kc == 15)),
                                      reads=t_w + t_h2Th[th], writes=[t_ps[b]], inc=(kc == 15))
                        for th in range(2):
                            b = banks[th]
                            fw.op("act", lambda e: e.activation(rtmp[th], ps[b][:], AF.Relu),
                                  reads=[t_ps[b]], writes=[t_rtmp[th]])
                            fw.op("dve", lambda e: e.tensor_tensor(fT[:, ch, th * 512:(th + 1) * 512], rtmp[th],
                                                                   rtmp[th], ALU.mult),
                                  reads=[t_rtmp[th]], writes=[t_fT[ch][th]])
            def down_group(wv, t_w, cg, tbh):
                for tbl in range(4):
                    tb = tbh * 4 + tbl
                    b = tbl
                    for kc in range(8):
                        fw.op("pe", lambda e: e.matmul(ps[b][:], fT[:, kc, tb * 128:(tb + 1) * 128], wv[:, kc, :],
                                                       start=(kc == 0), stop=(kc == 7)),
                              reads=t_w + [t_fT[kc][tbh]], writes=[t_ps[b]], inc=(kc == 7))
                for tbl in range(4):
                    tb = tbh * 4 + tbl
                    dst = y2[:, tb, cg * 512:(cg + 1) * 512]
                    if e8 == 0:
                        fw.op("act", lambda e: e.activation(dst, ps[tbl][:], AF.Copy),
                              reads=[t_ps[tbl]], writes=[t_y2[tb]])
                    else:
                        fw.op("dve", lambda e: e.tensor_tensor(dst, ps[tbl][:], dst, ALU.add),
                              reads=[t_ps[tbl], t_y2[tb]], writes=[t_y2[tb]])

            def post_out(tb):
                ss, t_ss = stat()
                rs, t_rs = stat()
                fw.op("act", lambda e: e.activation(junk2, y2[:, tb, :], AF.Square, accum_out=ss),
                      reads=[t_y2[tb]], writes=[t_junk2, t_ss])
                rstd_from_ss(ss, t_ss, 1.0 / D, rs, t_rs)
                fw.op("dve", lambda e: e.scalar_tensor_tensor(y2[:, tb, :], y2[:, tb, :], rs, gslotH, ALU.mult, ALU.mult),
                      reads=[t_y2[tb], t_rs, t_gslotH], writes=[t_y2[tb]])
                fw.dma("pool", f"outB{tb}", out_d[tb * 128:(tb + 1) * 128, :], y2[:, tb, :],
                       reads=[t_y2[tb]], writes=[t_out[tb]], accum_op=ALU.add)

            if e8 < 7:
                for cg in range(4):
                    wv, t_w = wload(w_dn, e8 * 1024, 8, cg * 512, 512)
                    for tbh in range(2):
                        down_group(wv, t_w, cg, tbh)
            else:
                wts = [wload(w_dn, e8 * 1024, 8, cg * 512, 512) for cg in range(4)]
                for tbh in range(2):
                    for cg in range(4):
                        down_group(wts[cg][0], wts[cg][1], cg, tbh)
                    for tbl in range(4):
                        post_out(tbh * 4 + tbl)
        t_y2 = t_out
        fw.wait_tiles("sp", t_y2)
        print("[marks]", fw.marks)
        print(f"[build] insts={fw.n_inst} waits={fw.n_wait} wtiles={wcount[0]} "
              f"counts={ {k: e.count for k, e in fw.engs.items()} }")
    return nc, dbg_out


def make_in_maps(x, g_pre_mix, w_in, b_in, w_dw, b_dw, g_conv_ln, b_conv_ln, w_sb_out, w_conv_out, w_o,
                 g_post_mix, g_pre_mlp, w_up, w_down, g_post_mlp):
    f = lambda a: np.ascontiguousarray(np.asarray(a, dtype=np.float32))
    x = f(x)
    b_in0 = f(b_in)[0]
    consts = np.zeros((128, 384), np.float32)
    consts[:, 0:72] = b_in0.reshape(72, 128).T
    wdw = f(w_dw)[0]
    consts[:, 72:320] = wdw.reshape(31, 8, 128).transpose(2, 1, 0).reshape(128, 248)
    consts[:, 320:328] = f(b_dw)[0].reshape(8, 128).T
    consts[:, 328:336] = f(g_conv_ln)[0].reshape(8, 128).T
    consts[:, 336:344] = f(b_conv_ln)[0].reshape(8, 128).T
    gv = np.stack([f(g_pre_mix)[0], f(g_post_mix)[0], f(g_pre_mlp)[0], f(g_post_mlp)[0]])
    gvec = np.ascontiguousarray(np.broadcast_to(gv[:, None, :], (4, 128, D)))
    bv = np.ascontiguousarray(np.broadcast_to(b_in0[None, 2048:3072], (128, 1024)))
    shared = {
        "gvec": gvec, "bv": bv, "w_in": f(w_in)[0], "w_sbo": f(w_sb_out)[0], "w_cvo": f(w_conv_out)[0],
        "w_o": f(w_o)[0], "w_up": f(w_up)[0], "w_dn": f(w_down)[0],
    }
    in_maps = []
    for c in range(8):
        b, half = c // 2, c % 2
        xi = np.zeros((2 * NT, D), np.float32)
        if half == 1:
            xi[:NT] = x[b, :NT]
        xi[NT:] = x[b, half * NT:(half + 1) * NT]
        cc = consts.copy()
        cc[:, 344] = float(half)
        m = dict(shared)
        m["xin"] = xi
        m["consts"] = cc
        in_maps.append(m)
    return in_maps


_NC_CACHE = {}


def kernel(**inputs):
    if "nc" not in _NC_CACHE:
        _NC_CACHE["nc"] = build(False)[0]
    nc = _NC_CACHE["nc"]
    in_maps = make_in_maps(**inputs)
    res = run_bass_kernel_spmd(nc, in_maps, core_ids=list(range(8)))
    out = np.zeros((4, 2048, D), np.float32)
    for c in range(8):
        b, half = c // 2, c % 2
        out[b, half * NT:(half + 1) * NT] = res.results[c]["out"]
    return out
```

```python
from contextlib import ExitStack
import math
import numpy as np
import concourse.bass as bass
import concourse.mybir as mybir
from concourse.bass_utils import run_bass_kernel_spmd

F32 = mybir.dt.float32
BF16 = mybir.dt.bfloat16
AF = mybir.ActivationFunctionType
ALU = mybir.AluOpType

D = 2048
NT = 1024
DIN = 9216
DFF = 8192
EPS = 1e-6
NSLOT = 5
SCALE = 1.0 / math.sqrt(128.0)
KIB = 256


class Tile:
    __slots__ = ("name", "w", "r", "excl")

    def __init__(self, name, excl=False):
        self.name = name
        self.w = None
        self.r = []
        self.excl = excl


class Eng:
    def __init__(self, name, handle, sem):
        self.name = name
        self.h = handle
        self.sem = sem
        self.count = 0
        self.known = {}


class FW:
    def __init__(self, nc, stack):
        self.nc = nc
        self.stack = stack
        self.sems = {}
        self.engs = {}
        for name, h in (("pe", nc.tensor), ("act", nc.scalar), ("dve", nc.vector),
                        ("pool", nc.gpsimd), ("sp", nc.sync)):
            sem = stack.enter_context(nc.semaphore("sem_" + name))
            self.sems["e:" + name] = sem
            self.engs[name] = Eng(name, h, sem)
        self.dma_vals = {}
        self.n_wait = 0
        self.n_inst = 0
        self.n_by = {}
        self.marks = []

    def _need(self, eng, tickets):
        need = {}
        for t in tickets:
            if t is None:
                continue
            k, v, _ = t
            if eng.known.get(k, 0) >= v:
                continue
            if need.get(k, 0) < v:
                need[k] = v
        for k, v in need.items():
            eng.h.wait_ge(self.sems[k], v)
            eng.known[k] = v
            self.n_wait += 1

    def op(self, engname, fn, reads=(), writes=(), inc=True):
        eng = self.engs[engname]
        tickets = []
        for t in reads:
            if t.w is not None:
                tickets.append(t.w)
            if t.excl:
                for r in t.r:
                    if r[2] != engname:
                        tickets.append(r)
        same_ok = engname == "pe"
        for t in writes:
            if t.w is not None and not (same_ok and t.w[2] == engname):
                tickets.append(t.w)
            for r in t.r:
                if not (same_ok and r[2] == engname):
                    tickets.append(r)
        self._need(eng, tickets)
        ins = fn(eng.h)
        self.n_inst += 1
        self.n_by[engname] = self.n_by.get(engname, 0) + 1
        key = "e:" + engname
        if inc:
            eng.count += 1
            ins.then_inc(eng.sem, 1)
            ticket = (key, eng.count, engname)
        else:
            ticket = (key, eng.count + 1, engname)
        for t in reads:
            t.r.append(ticket)
        for t in writes:
            t.w = ticket
            t.r = []
        return ins

    def dma(self, qname, semkey, out, in_, reads=(), writes=(), **kw):
        eng = self.engs[qname]
        k = "d:" + semkey
        if k not in self.sems:
            self.sems[k] = self.stack.enter_context(self.nc.semaphore("dsem_" + semkey))
            self.dma_vals[k] = 0
        tickets = []
        for t in reads:
            if t.w is not None:
                tickets.append(t.w)
        for t in writes:
            if t.w is not None:
                tickets.append(t.w)
            tickets.extend(t.r)
        self._need(eng, tickets)
        ins = eng.h.dma_start(out=out, in_=in_, **kw)
        self.dma_vals[k] += 16
        ins.then_inc(self.sems[k], 16)
        ticket = (k, self.dma_vals[k], "dma")
        for t in reads:
            t.r.append(ticket)
        for t in writes:
            t.w = ticket
            t.r = []
        self.n_inst += 1
        return ins

    def fence(self, src_tiles, dst_tiles):
        for d in dst_tiles:
            for t in src_tiles:
                if t.w is not None:
                    d.r.append(t.w)
                d.r.extend(t.r)

    def wait_tiles(self, engname, tiles):
        eng = self.engs[engname]
        tickets = []
        for t in tiles:
            if t.w is not None:
                tickets.append(t.w)
            tickets.extend(t.r)
        self._need(eng, tickets)

    def barrier(self, extra_tiles=()):
        tickets = []
        for n in ("pe", "act", "dve", "pool"):
            e = self.engs[n]
            if e.count > 0:
                tickets.append(("e:" + n, e.count, n))
        for t in extra_tiles:
            if t.w is not None:
                tickets.append(t.w)
            tickets.extend(t.r)
        for n in ("pe", "act", "dve", "sp"):
            self._need(self.engs[n], [t for t in tickets if t[2] != n])


class _Stop(Exception):
    pass


class _Suppress:
    def __enter__(self):
        return self

    def __exit__(self, et, ev, tb):
        return et is _Stop


def build(dbg=False, stop_after=None):
    nc = bass.Bass("TRN2", target_bir_lowering=False)
    dram = {}

    def din(name, shape):
        dram[name] = nc.dram_tensor(name, list(shape), F32, kind="ExternalInput").ap()
        return dram[name]

    xin = din("xin", [2 * NT, D])
    consts_d = din("consts", [128, 384])
    gvec_d = din("gvec", [4, 128, D])
    bv_d = din("bv", [128, 1024])
    w_in = din("w_in", [D, DIN])
    w_sbo = din("w_sbo", [1024, D])
    w_cvo = din("w_cvo", [1024, D])
    w_o = din("w_o", [D, D])
    w_up = din("w_up", [D, DFF])
    w_dn = din("w_dn", [DFF, D])
    out_d = nc.dram_tensor("out", [NT, D], F32, kind="ExternalOutput").ap()
    dbg_out = {}

    with _Suppress(), ExitStack() as st:
        fw = FW(nc, st)
        arena = nc.alloc_sbuf_tensor("arena", [128, 52992], F32)
        ps_all = nc.alloc_psum_tensor("ps_all", [128, 4096], F32)
        ps = [ps_all[:, i * 512:(i + 1) * 512] for i in range(8)]
        t_ps = [Tile(f"ps{i}", excl=True) for i in range(8)]

        def regf(off_kib, size_kib):
            a = int(round(off_kib * KIB))
            b = int(round((off_kib + size_kib) * KIB))
            return arena[:, a:b]

        def regb(off_kib, size_kib):
            return regf(off_kib, size_kib).bitcast(BF16)

        def dump(name, ap_sb, shape, tiles, dt=F32):
            if not dbg:
                return
            d = nc.dram_tensor("dbg_" + name, list(shape), dt, kind="ExternalOutput").ap()
            dbg_out[name] = d
            fw.dma("sp", "dbg", d, ap_sb, reads=tiles)
            fw.barrier(extra_tiles=tiles)

        def checkpoint(name):
            fw.marks.append((name, dict(fw.n_by)))
            if stop_after == name:
                fw.barrier()
                t_fin = Tile("fin")
                fw.dma("sp", "fin", out_d[0:128, :], regf(60, 8), reads=[t_fin])
                fw.wait_tiles("sp", [t_fin])
                raise _Stop()

        consts = regf(0, 1.5)
        t_consts = Tile("consts")
        ident = regb(1.5, 0.25)
        negU = regb(1.75, 0.25)
        negones = regb(2.0, 0.25)
        onesf = regf(2.25, 0.5)
        t_cmat = Tile("cmat")
        stats = regf(3, 1)
        epsc = stats[:, 255:256]
        hTc_last = regb(4, 4).rearrange("p (a b) -> p a b", a=16)
        t_hTc_last = Tile("hTc_last")
        masks = regb(8, 4).rearrange("p (a b) -> p a b", a=4)
        t_masks = Tile("masks")
        bvbc = regf(8, 4)
        t_bvbc = Tile("bvbc")
        bvf = None

        C_BIN, C_WDW, C_BDW, C_GLN, C_BLN, C_FLAG, C_BQS = 0, 72, 320, 328, 336, 344, 345
        flag = consts[:, C_FLAG:C_FLAG + 1]

        slot_ap = [regb(12 + 8 * i, 8) for i in range(NSLOT)]
        t_slot = [[Tile(f"slot{i}a"), Tile(f"slot{i}b")] for i in range(NSLOT)]
        wcount = [0]

        def wload(wd, r0, kc, c0, ncols):
            assert kc * ncols == 4096
            s = wcount[0] % NSLOT
            wcount[0] += 1
            view = slot_ap[s].rearrange("p (a b) -> p a b", a=kc)
            src = wd[r0:r0 + kc * 128, c0:c0 + ncols].rearrange("(a p) e -> p a e", p=128)
            fw.dma("pool", f"w{s}", view, src, writes=t_slot[s])
            return view, t_slot[s]

        def wload2(wdA, wdB, c0):
            s = wcount[0] % NSLOT
            wcount[0] += 1
            view = slot_ap[s].rearrange("p (h a b) -> p h a b", h=2, a=8)
            for hh, wd in enumerate((wdA, wdB)):
                src = wd[0:1024, c0:c0 + 256].rearrange("(a p) e -> p a e", p=128)
                fw.dma("pool", f"w{s}" + "ab"[hh], view[:, hh], src, writes=[t_slot[s][hh]])
            return view, t_slot[s]

        scol = [0]

        def stat(n=1):
            c = scol[0]
            scol[0] += n
            assert scol[0] <= 255
            return stats[:, c:c + n], Tile(f"stat{c}")

        fw.dma("sp", "consts", consts, consts_d[:, :], writes=[t_consts])
        scr = regf(188, 2)
        t_scr = Tile("scr")
        fw.op("pool", lambda e: e.memset(scr[:, 0:128], 1.0), writes=[t_scr])
        fw.op("pool", lambda e: e.affine_select(out=scr[:, 0:128], in_=scr[:, 0:128], pattern=[[-1, 128]],
                                                compare_op=ALU.is_equal, fill=0.0, base=0, channel_multiplier=1),
              reads=[t_scr], writes=[t_scr])
        fw.op("dve", lambda e: e.tensor_copy(ident, scr[:, 0:128]), reads=[t_scr], writes=[t_cmat])
        fw.op("pool", lambda e: e.memset(scr[:, 128:256], -1.0), writes=[t_scr])
        fw.op("pool", lambda e: e.affine_select(out=scr[:, 128:256], in_=scr[:, 128:256], pattern=[[-1, 128]],
                                                compare_op=ALU.is_ge, fill=0.0, base=0, channel_multiplier=1),
              reads=[t_scr], writes=[t_scr])
        fw.op("dve", lambda e: e.tensor_copy(negU, scr[:, 128:256]), reads=[t_scr], writes=[t_cmat])
        fw.op("pool", lambda e: e.memset(scr[:, 256:384], -1.0), writes=[t_scr])
        fw.op("dve", lambda e: e.tensor_copy(negones, scr[:, 256:384]), reads=[t_scr], writes=[t_cmat])
        fw.op("pool", lambda e: e.memset(onesf, 1.0 / 1024.0), writes=[t_cmat])
        fw.op("pool", lambda e: e.memset(epsc, EPS), writes=[t_cmat])
        fw.op("dve", lambda e: e.tensor_scalar(consts[:, C_BQS:C_BQS + 8], consts[:, C_BIN:C_BIN + 8], SCALE, None,
                                               ALU.mult), reads=[t_consts], writes=[t_consts])

        fw.barrier()

        checkpoint("const")
        hT = regb(60, 32).rearrange("p (a b) -> p a b", a=16)
        t_hT = [Tile(f"hT{i}") for i in range(8)]
        KT = regb(92, 32).rearrange("p (a b) -> p a b", a=8)
        t_KT = [[Tile(f"KT{h}_{q}") for q in range(4)] for h in range(8)]
        Vt = regb(124, 32).rearrange("p (a b) -> p a b", a=16)
        t_Vt = [Tile(f"Vt{i}") for i in range(16)]
        QT = regb(156, 16).rearrange("p (a b) -> p a b", a=8)
        t_QT = [[Tile(f"QT{h}_{q}") for q in range(2)] for h in range(8)]
        xbuf = [regf(172, 8), regf(180, 8), regf(52, 8)]
        t_xbuf = [Tile("xbuf0"), Tile("xbuf1"), Tile("xbuf2")]
        hbf = [regb(188, 4), regb(192, 4), regb(160, 4)]
        t_hbf = [Tile("hbf0"), Tile("hbf1"), Tile("hbf2")]
        gslot = regf(196, 8)
        t_gslot = Tile("gslotA")

        bank_rr = [0]

        def next_bank():
            b = bank_rr[0] % 8
            bank_rr[0] += 1
            return b

        def rstd_from_ss(ss_ap, t_ss, n_inv, out_ap, t_out):
            fw.op("act", lambda e: e.activation(out_ap, ss_ap, AF.Sqrt, bias=epsc, scale=n_inv),
                  reads=[t_ss, t_cmat], writes=[t_out])
            fw.op("dve", lambda e: e.reciprocal(out_ap, out_ap), reads=[t_out], writes=[t_out])

        def prenorm(src_ap, t_src, hb, t_hb, g_ap, t_g):
            ss, t_ss = stat()
            rs, t_rs = stat()
            fw.op("act", lambda e: e.activation(hb, src_ap, AF.Square, accum_out=ss),
                  reads=[t_src], writes=[t_hb, t_ss])
            rstd_from_ss(ss, t_ss, 1.0 / D, rs, t_rs)
            fw.op("dve", lambda e: e.scalar_tensor_tensor(hb, src_ap, rs, g_ap, ALU.mult, ALU.mult),
                  reads=[t_src, t_rs, t_g], writes=[t_hb])

        def transpose_block(hb, t_hb, dstT, t_dst, col0, banks=None):
            for half in range(2):
                b = next_bank() if banks is None else banks[half]
                pT = ps[b][:].bitcast(BF16).rearrange("p (a b) -> p a b", a=8)
                for j in range(8):
                    kc = half * 8 + j
                    fw.op("pe", lambda e: e.transpose(pT[:, j, :], hb[:, kc * 128:(kc + 1) * 128], ident),
                          reads=[t_hb, t_cmat], writes=[t_ps[b]], inc=(j == 7))
                dst = dstT[:, half * 8:(half + 1) * 8, col0:col0 + 128]
                if half == 0:
                    fw.op("act", lambda e: e.activation(dst, pT, AF.Copy), reads=[t_ps[b]], writes=[t_dst])
                else:
                    fw.op("dve", lambda e: e.tensor_copy(dst, pT), reads=[t_ps[b]], writes=[t_dst])

        def prenorm_transpose(src_ap, t_src, i, g_ap, t_g, dstT, t_dst, col0):
            prenorm(src_ap, t_src, hbf[i], t_hbf[i], g_ap, t_g)
            transpose_block(hbf[i], t_hbf[i], dstT, t_dst, col0)

        def load_g(idx, slot, t_slot_):
            fw.dma("sp", "g" + t_slot_.name, slot, gvec_d[idx], writes=[t_slot_])

        load_g(0, gslot, t_gslot)
        fw.dma("sp", "bv", bvbc, bv_d[:, :], writes=[t_bvbc])
        bvf_ap = None

        def phaseA(row0):
            for blk in range(8):
                i = blk % 3
                fw.dma("sp", f"xb{i}", xbuf[i], xin[row0 + blk * 128: row0 + (blk + 1) * 128, :],
                       writes=[t_xbuf[i]])
                prenorm_transpose(xbuf[i], t_xbuf[i], i, gslot, t_gslot, hT, t_hT[blk], blk * 128)

        def proj_fm(wd, c0_list, dst_fn, nblk_tok, bias_col_fn, scale, func=AF.Identity):
            for ti, c0 in enumerate(c0_list):
                wv, t_w = wload(wd, 0, 16, c0, 256)
                for ec in range(2):
                    banks = [next_bank() for _ in range(nblk_tok)]
                    for kc in range(16):
                        for th in range(nblk_tok):
                            b = banks[th]
                            fw.op("pe", lambda e: e.matmul(ps[b][:], wv[:, kc, ec * 128:(ec + 1) * 128],
                                                           hT[:, kc, th * 512:(th + 1) * 512],
                                                           start=(kc == 0), stop=(kc == 15)),
                                  reads=t_w + t_hT[th * 4:(th + 1) * 4], writes=[t_ps[b]], inc=(kc == 15))
                    for th in range(nblk_tok):
                        b = banks[th]
                        dst, t_dst = dst_fn(ti * 2 + ec, th)
                        bias = bias_col_fn(ti * 2 + ec)
                        fw.op("act", lambda e: e.activation(dst, ps[b][:], func, bias=bias, scale=scale),
                              reads=[t_ps[b], t_consts], writes=[t_dst])

        def projV(blk0, is_ctx):
            for cg in range(2):
                wts = [wload(w_in, kt * 1024, 8, 2048 + cg * 512, 512) for kt in range(2)]
                for tbh in range(2):
                    banks = [next_bank() for _ in range(4)]
                    for kt in range(2):
                        wv, t_w = wts[kt]
                        for tbl in range(4):
                            tb = tbh * 4 + tbl
                            b = banks[tbl]
                            for kc in range(8):
                                fw.op("pe", lambda e: e.matmul(ps[b][:], hT[:, kt * 8 + kc, tb * 128:(tb + 1) * 128],
                                                               wv[:, kc, :], start=(kt == 0 and kc == 0),
                                                               stop=(kt == 1 and kc == 7), skip_group_check=True),
                                      reads=t_w + [t_hT[tb]], writes=[t_ps[b]], inc=(kt == 1 and kc == 7))
                    for tbl in range(4):
                        tb = tbh * 4 + tbl
                        b = banks[tbl]
                        dst = Vt[:, blk0 + tb, cg * 512:(cg + 1) * 512]
                        if is_ctx:
                            fw.op("dve", lambda e: e.scalar_tensor_tensor(dst, ps[b][:], flag,
                                                                          bvf_ap[:, cg * 512:(cg + 1) * 512],
                                                                          ALU.mult, ALU.add),
                                  reads=[t_ps[b], t_consts, t_bvf], writes=[t_Vt[blk0 + tb]])
                        else:
                            fw.op("dve", lambda e: e.tensor_tensor(dst, ps[b][:],
                                                                   bvbc[:, cg * 512:(cg + 1) * 512], ALU.add),
                                  reads=[t_ps[b], t_bvbc], writes=[t_Vt[blk0 + tb]])

        bvf_ap = regf(156, 4)
        t_bvf = Tile("bvf")
        fw.op("dve", lambda e: e.tensor_scalar(bvf_ap, bvbc, flag, None, ALU.mult),
              reads=[t_bvbc, t_consts], writes=[t_bvf])

        phaseA(0)
        fw.op("dve", lambda e: e.tensor_copy(hTc_last, hT[:, :, 896:1024]), reads=[t_hT[7]], writes=[t_hTc_last])
        proj_fm(w_in, [1024 + 256 * i for i in range(4)],
                lambda ch, th: (KT[:, ch, th * 512:(th + 1) * 512], t_KT[ch][th]), 2,
                lambda ch: consts[:, C_BIN + 8 + ch:C_BIN + 9 + ch], 1.0)
        projV(0, True)
        fw.barrier()
        checkpoint("Bctx")
        phaseA(NT)
        proj_fm(w_in, [1024 + 256 * i for i in range(4)],
                lambda ch, th: (KT[:, ch, 1024 + th * 512:1024 + (th + 1) * 512], t_KT[ch][2 + th]), 2,
                lambda ch: consts[:, C_BIN + 8 + ch:C_BIN + 9 + ch], 1.0)
        projV(8, False)
        fw.fence([t_hbf[2]], [t for r in t_QT for t in r])
        proj_fm(w_in, [256 * i for i in range(4)],
                lambda ch, th: (QT[:, ch, th * 512:(th + 1) * 512], t_QT[ch][th]), 2,
                lambda ch: consts[:, C_BQS + ch:C_BQS + ch + 1], SCALE)
        fw.barrier()
        dump("hT", hT, [128, 16, 1024], t_hT, BF16)
        dump("KT", KT, [128, 8, 2048], [t for r in t_KT for t in r], BF16)
        dump("Vt", Vt, [128, 16, 1024], t_Vt, BF16)
        dump("QT", QT, [128, 8, 1024], [t for r in t_QT for t in r], BF16)
        checkpoint("B")

        scr2 = regf(204, 2)
        t_scr2 = Tile("scr2")
        for j in range(4):
            fw.op("pool", lambda e: e.memset(scr2, 1.0), writes=[t_scr2])
            fw.op("pool", lambda e: e.affine_select(out=scr2, in_=scr2, pattern=[[1, 512]], compare_op=ALU.is_gt,
                                                    fill=0.0, base=-128 * j, channel_multiplier=-1),
                  reads=[t_scr2], writes=[t_scr2])
            fw.op("dve", lambda e: e.tensor_copy(masks[:, j, :], scr2), reads=[t_scr2], writes=[t_masks])
        fw.barrier()

        oT = regb(172, 16).rearrange("p (a b) -> p a b", a=8)
        t_oT = [[Tile(f"oT{h}_{q}") for q in range(2)] for h in range(8)]

        NCH = 4
        sp_all = regb(188, 4).rearrange("p (a b) -> p a b", a=4)
        a_all = regb(192, 4).rearrange("p (a b) -> p a b", a=4)
        Sb_all = regb(196, 4).rearrange("p (a b) -> p a b", a=4)
        S_all = regf(52, 8).rearrange("p (a b) -> p a b", a=4)
        ebuf = regf(200, 4)
        t_e = Tile("ebuf")
        t_sp = [Tile("spP0"), Tile("spP1")]
        t_a = [Tile("aP0"), Tile("aP1")]
        t_S = [Tile("SP0"), Tile("SP1")]
        t_Sb = [Tile("SbP0"), Tile("SbP1")]

        def pair(ap3, pr):
            return ap3[:, 2 * pr:2 * pr + 2, :]

        def attn_group(g, heads):
            nsteps = 8 + 4 * g + 4
            qsl = slice(g * 512, (g + 1) * 512)

            def QK(ci, kb):
                hd = heads[ci]
                fw.op("pe", lambda e: e.matmul(ps[ci][:], KT[:, hd, kb * 128:(kb + 1) * 128], QT[:, hd, qsl],
                                               start=True, stop=True),
                      reads=[t_KT[hd][kb // 4], t_QT[hd][g]], writes=[t_ps[ci]])

            for ci in range(NCH):
                QK(ci, nsteps - 1)
            for s in range(nsteps):
                kb = nsteps - 1 - s
                first, last = s == 0, s == nsteps - 1
                j = kb - (8 + 4 * g)
                diag = j >= 0
                if diag:
                    m3 = masks[:, j, :].unsqueeze(1).broadcast_to([128, 2, 512])
                for pr in range(2):
                    zA = ps_all[:, 2 * pr * 512:(2 * pr + 2) * 512]
                    tz = [t_ps[2 * pr], t_ps[2 * pr + 1]]
                    spP = pair(sp_all, pr)
                    fw.op("act", lambda e: e.activation(ebuf, zA, AF.Exp), reads=tz, writes=[t_e])
                    fw.op("act", lambda e: e.activation(spP, ebuf.rearrange("p (a b) -> p a b", a=2), AF.Ln, bias=1.0),
                          reads=[t_e], writes=[t_sp[pr]])
                    if diag:
                        fw.op("dve", lambda e: e.tensor_tensor(spP, spP, m3, ALU.mult),
                              reads=[t_sp[pr], t_masks], writes=[t_sp[pr]])
                for ci in range(NCH):
                    pr = ci // 2
                    fw.op("pe", lambda e: e.matmul(ps[ci][:], negU, sp_all[:, ci, :], start=False, stop=first,
                                                   skip_group_check=True),
                          reads=[t_sp[pr], t_cmat], writes=[t_ps[ci]], inc=first)
                    if not first:
                        fw.op("pe", lambda e: e.matmul(ps[ci][:], negones, Sb_all[:, ci, :], start=False, stop=True,
                                                       skip_group_check=True),
                              reads=[t_Sb[pr], t_cmat], writes=[t_ps[ci]])
                for pr in range(2):
                    zA = ps_all[:, 2 * pr * 512:(2 * pr + 2) * 512]
                    tz = [t_ps[2 * pr], t_ps[2 * pr + 1]]
                    aP = pair(a_all, pr)
                    fw.op("act", lambda e: e.activation(aP, zA.rearrange("p (a b) -> p a b", a=2), AF.Exp),
                          reads=tz, writes=[t_a[pr]])
                    if diag:
                        fw.op("dve", lambda e: e.tensor_tensor(aP, aP, m3, ALU.mult),
                              reads=[t_a[pr], t_masks], writes=[t_a[pr]])
                for ci in range(NCH):
                    pr = ci // 2
                    hd = heads[ci]
                    bO = 4 + ci
                    if not last:
                        QK(ci, kb - 1)
                    fw.op("pe", lambda e: e.matmul(ps[bO][:], Vt[:, kb, hd * 128:(hd + 1) * 128], a_all[:, ci, :],
                                                   start=first, stop=last),
                          reads=[t_Vt[kb], t_a[pr]], writes=[t_ps[bO]], inc=last)
                if not last:
                    for pr in range(2):
                        SP_, spP, SbP = pair(S_all, pr), pair(sp_all, pr), pair(Sb_all, pr)
                        if first:
                            fw.op("dve", lambda e: e.tensor_copy(SP_, spP), reads=[t_sp[pr]], writes=[t_S[pr]])
                        else:
                            fw.op("dve", lambda e: e.tensor_tensor(SP_, SP_, spP, ALU.add),
                                  reads=[t_S[pr], t_sp[pr]], writes=[t_S[pr]])
                        fw.op("dve", lambda e: e.tensor_copy(SbP, SP_), reads=[t_S[pr]], writes=[t_Sb[pr]])
                else:
                    for ci in range(NCH):
                        hd = heads[ci]
                        fw.op("dve", lambda e: e.tensor_copy(oT[:, hd, qsl], ps[4 + ci][:]),
                              reads=[t_ps[4 + ci]], writes=[t_oT[hd][g]])

        for g in range(2):
            for hg in range(2):
                attn_group(g, [hg * 4 + i for i in range(4)])
        fw.barrier()
        dump("oT", oT, [128, 8, 1024], [t for r in t_oT for t in r], BF16)
        checkpoint("C")

        ycv = regf(92, 32).rearrange("p (a b) -> p a b", a=8)
        t_ycv = [[Tile(f"ycv{c}_{h}") for h in range(2)] for c in range(8)]
        swT = regb(124, 16).rearrange("p (a b) -> p a b", a=8)
        t_swT = [Tile(f"swT{c}") for c in range(8)]
        ubuf = [regb(140, 2.25)[:, 0:1054], regb(142.25, 2.25)[:, 0:1054]]
        t_u = [[Tile(f"u{i}_{h}") for h in range(3)] for i in range(2)]
        dg = regb(52, 7.75).rearrange("p (a b) -> p a b", a=31)
        t_dg = Tile("dg")
        ysq = [regf(149, 4), regf(153, 4)]
        t_ysq = [Tile("ysq0"), Tile("ysq1")]
        mean_sb = regf(157, 4); t_mean = Tile("mean")
        rstd_sb = regf(161, 4); t_rstd = Tile("rstd")
        var_sb = regf(165, 4); t_var = Tile("var")
        t1 = [regf(188, 4), regf(192, 4)]
        t_t1 = [Tile("t1_0"), Tile("t1_1")]
        acc1, acc2 = t1
        t_acc1, t_acc2 = t_t1
        sg = [regf(196, 2), regf(198, 2), regf(200, 2)]
        t_sg = [Tile("sg0"), Tile("sg1"), Tile("sg2")]
        ident3 = ident.unsqueeze(1).broadcast_to([128, 31, 128])

        def conv_dg(c):
            w3 = consts[:, C_WDW + c * 31: C_WDW + (c + 1) * 31].unsqueeze(2).broadcast_to([128, 31, 128])
            fw.op("dve", lambda e: e.tensor_tensor(dg, ident3, w3, ALU.mult),
                  reads=[t_cmat, t_consts], writes=[t_dg])

        def conv_chunk(c):
            ui = c % 2
            u = ubuf[ui]
            cb = [4, 5] if c % 2 == 0 else [6, 7]
            for th in range(2):
                rd = [t_u[ui][0], t_u[ui][1]] if th == 0 else [t_u[ui][1], t_u[ui][2]]
                for jtap in range(31):
                    fw.op("pe", lambda e: e.matmul(ps[cb[th]][:], dg[:, jtap, :],
                                                   u[:, jtap + th * 512: jtap + th * 512 + 512],
                                                   start=(jtap == 0), stop=(jtap == 30)),
                          reads=rd + [t_dg], writes=[t_ps[cb[th]]], inc=(jtap == 30))
                fw.op("act", lambda e: e.activation(ycv[:, c, th * 512:(th + 1) * 512], ps[cb[th]][:], AF.Identity,
                                                    bias=consts[:, C_BDW + c:C_BDW + c + 1]),
                      reads=[t_ps[cb[th]], t_consts], writes=[t_ycv[c][th]])
            qi = c % 2
            fw.op("act", lambda e: e.activation(ysq[qi], ycv[:, c, :], AF.Square),
                  reads=t_ycv[c], writes=[t_ysq[qi]])
            if c == 0:
                fw.op("dve", lambda e: e.tensor_copy(acc1, ycv[:, c, :]), reads=t_ycv[c], writes=[t_acc1])
                fw.op("dve", lambda e: e.tensor_copy(acc2, ysq[qi]), reads=[t_ysq[qi]], writes=[t_acc2])
            else:
                fw.op("dve", lambda e: e.tensor_tensor(acc1, acc1, ycv[:, c, :], ALU.add),
                      reads=t_ycv[c] + [t_acc1], writes=[t_acc1])
                fw.op("dve", lambda e: e.tensor_tensor(acc2, acc2, ysq[qi], ALU.add),
                      reads=[t_ysq[qi], t_acc2], writes=[t_acc2])

        for tpair in range(4):
            wa, t_wa = wload(w_in, 0, 16, 3072 + tpair * 256, 256)
            wb_, t_wb = wload(w_in, 0, 16, 4096 + tpair * 256, 256)
            for ec in range(2):
                c = tpair * 2 + ec
                ui = c % 2
                u = ubuf[ui]
                if c > 0:
                    conv_dg(c - 1)
                bh = 0
                for (wv_, t_wv, cs) in ((wa, t_wa, 0), (wb_, t_wb, 128)):
                    for kc in range(16):
                        fw.op("pe", lambda e: e.matmul(ps[bh][:, cs:cs + 128], wv_[:, kc, ec * 128:(ec + 1) * 128],
                                                       hTc_last[:, kc, :], start=(kc == 0), stop=(kc == 15)),
                              reads=t_wv + [t_hTc_last], writes=[t_ps[bh]], inc=(kc == 15))
                ba_col = consts[:, C_BIN + 24 + c:C_BIN + 25 + c]
                bb_col = consts[:, C_BIN + 32 + c:C_BIN + 33 + c]
                fw.op("act", lambda e: e.activation(sg[2][:, 0:30], ps[bh][:, 128 + 98:256], AF.Sigmoid, bias=bb_col),
                      reads=[t_ps[bh], t_consts], writes=[t_sg[2]])
                fw.op("dve", lambda e: e.scalar_tensor_tensor(u[:, 0:30], ps[bh][:, 98:128], ba_col, sg[2][:, 0:30],
                                                              ALU.add, ALU.mult),
                      reads=[t_ps[bh], t_consts, t_sg[2]], writes=[t_u[ui][0]])
                fw.op("dve", lambda e: e.tensor_scalar(u[:, 0:30], u[:, 0:30], flag, None, ALU.mult),
                      reads=[t_u[ui][0], t_consts], writes=[t_u[ui][0]])
                for th in range(2):
                    bA_, bB_ = (1, 2) if th == 0 else (3, 0)
                    for (wv_, t_wv, bb) in ((wa, t_wa, bA_), (wb_, t_wb, bB_)):
                        for kc in range(16):
                            fw.op("pe", lambda e: e.matmul(ps[bb][:], wv_[:, kc, ec * 128:(ec + 1) * 128],
                                                           hT[:, kc, th * 512:(th + 1) * 512],
                                                           start=(kc == 0), stop=(kc == 15)),
                                  reads=t_wv + t_hT[th * 4:(th + 1) * 4], writes=[t_ps[bb]], inc=(kc == 15))
                    fw.op("act", lambda e: e.activation(sg[th], ps[bB_][:], AF.Sigmoid, bias=bb_col),
                          reads=[t_ps[bB_], t_consts], writes=[t_sg[th]])
                    fw.op("dve", lambda e: e.scalar_tensor_tensor(u[:, 30 + th * 512: 30 + (th + 1) * 512], ps[bA_][:],
                                                                  ba_col, sg[th], ALU.add, ALU.mult),
                          reads=[t_ps[bA_], t_consts, t_sg[th]], writes=[t_u[ui][1 + th]])
                if c > 0:
                    conv_chunk(c - 1)
        conv_dg(7)
        conv_chunk(7)
        B_MEAN = [0, 1]
        B_MSQ = [2, 3]
        for th in range(2):
            sl = slice(th * 512, (th + 1) * 512)
            fw.op("pe", lambda e: e.matmul(ps[B_MEAN[th]][:], onesf, acc1[:, sl], start=True, stop=True),
                  reads=[t_cmat, t_acc1], writes=[t_ps[B_MEAN[th]]])
            fw.op("pe", lambda e: e.matmul(ps[B_MSQ[th]][:], onesf, acc2[:, sl], start=True, stop=True),
                  reads=[t_cmat, t_acc2], writes=[t_ps[B_MSQ[th]]])
        for th in range(2):
            sl = slice(th * 512, (th + 1) * 512)
            fw.op("act", lambda e: e.activation(mean_sb[:, sl], ps[B_MEAN[th]][:], AF.Copy),
                  reads=[t_ps[B_MEAN[th]]], writes=[t_mean])
            fw.op("dve", lambda e: e.tensor_tensor(var_sb[:, sl], mean_sb[:, sl], mean_sb[:, sl], ALU.mult),
                  reads=[t_mean], writes=[t_var])
            fw.op("dve", lambda e: e.tensor_tensor(var_sb[:, sl], ps[B_MSQ[th]][:], var_sb[:, sl], ALU.subtract),
                  reads=[t_ps[B_MSQ[th]], t_var], writes=[t_var])
            fw.op("act", lambda e: e.activation(rstd_sb[:, sl], var_sb[:, sl], AF.Sqrt, bias=epsc),
                  reads=[t_var, t_cmat], writes=[t_rstd])
            fw.op("dve", lambda e: e.reciprocal(rstd_sb[:, sl], rstd_sb[:, sl]), reads=[t_rstd], writes=[t_rstd])
        for c in range(8):
            qi = c % 2
            fw.op("dve", lambda e: e.tensor_tensor(t1[qi], ycv[:, c, :], mean_sb, ALU.subtract),
                  reads=t_ycv[c] + [t_mean], writes=[t_t1[qi]])
            fw.op("dve", lambda e: e.tensor_tensor(t1[qi], t1[qi], rstd_sb, ALU.mult),
                  reads=[t_t1[qi], t_rstd], writes=[t_t1[qi]])
            fw.op("act", lambda e: e.activation(swT[:, c, :], t1[qi], AF.Silu,
                                                bias=consts[:, C_BLN + c:C_BLN + c + 1],
                                                scale=consts[:, C_GLN + c:C_GLN + c + 1]),
                  reads=[t_t1[qi], t_consts], writes=[t_swT[c]])
        fw.barrier()
        dump("ycv", regf(92, 32), [128, 8192], [t for r in t_ycv for t in r])
        dump("swT", swT, [128, 8, 1024], t_swT, BF16)
        checkpoint("D")

        mergedT = regb(92, 32).rearrange("p (a b) -> p a b", a=16)
        t_mT = [Tile(f"mT{i}") for i in range(8)]
        sbcv_ap = [regb(188 + 8 * i, 8).rearrange("p (h a b) -> p h a b", h=2, a=8) for i in range(2)]
        t_sbcv = [[Tile(f"sbcv{i}a"), Tile(f"sbcv{i}b")] for i in range(2)]
        fw.fence(t_t1 + t_sg, t_sbcv[0] + t_sbcv[1])
        etmp = [[regf(140 + 8 * bi + 2 * k, 2) for k in range(4)] for bi in range(2)]
        t_etmp = [[Tile(f"et{bi}_{k}") for k in range(4)] for bi in range(2)]
        it = 0
        for eg2 in range(4):
            for sub in range(2):
                eg = eg2 * 2 + sub
                wg1, t_wg1 = wload(w_in, 0, 16, 5120 + eg * 256, 256)
                wg2, t_wg2 = wload(w_in, 0, 16, 7168 + eg * 256, 256)
                pv = eg % 2
                wsc = sbcv_ap[pv]
                t_wsc = t_sbcv[pv]
                for hh, wd in enumerate((w_sbo, w_cvo)):
                    fw.dma("pool", f"sbcv{pv}" + "ab"[hh], wsc[:, hh],
                           wd[0:1024, eg * 256:(eg + 1) * 256].rearrange("(a p) e -> p a e", p=128),
                           writes=[t_sbcv[pv][hh]])
                wsb, wcv = wsc[:, 0], wsc[:, 1]
                for ec in range(2):
                    ech = eg * 2 + ec
                    wcol = ec * 128
                    for th in range(2):
                        bi = it % 2
                        it += 1
                        bG1, bG2, bP1, bP2 = [bi * 4 + k for k in range(4)]
                        tsl = slice(th * 512, (th + 1) * 512)
                        for kc in range(16):
                            fw.op("pe", lambda e: e.matmul(ps[bG1][:], wg1[:, kc, ec * 128:(ec + 1) * 128], hT[:, kc, tsl],
                                                           start=(kc == 0), stop=(kc == 15)),
                                  reads=t_wg1 + t_hT[th * 4:(th + 1) * 4], writes=[t_ps[bG1]], inc=(kc == 15))
                        for kc in range(16):
                            fw.op("pe", lambda e: e.matmul(ps[bG2][:], wg2[:, kc, ec * 128:(ec + 1) * 128], hT[:, kc, tsl],
                                                           start=(kc == 0), stop=(kc == 15)),
                                  reads=t_wg2 + t_hT[th * 4:(th + 1) * 4], writes=[t_ps[bG2]], inc=(kc == 15))
                        for kc in range(8):
                            fw.op("pe", lambda e: e.matmul(ps[bP1][:], wsb[:, kc, wcol:wcol + 128], oT[:, kc, tsl],
                                                           start=(kc == 0), stop=(kc == 7)),
                                  reads=t_wsc + [t_oT[kc][th]], writes=[t_ps[bP1]], inc=(kc == 7))
                        for kc in range(8):
                            fw.op("pe", lambda e: e.matmul(ps[bP2][:], wcv[:, kc, wcol:wcol + 128], swT[:, kc, tsl],
                                                           start=(kc == 0), stop=(kc == 7)),
                                  reads=t_wsc + [t_swT[kc]], writes=[t_ps[bP2]], inc=(kc == 7))
                        s1, s2, m1, m2 = etmp[bi]
                        ts1, ts2, tm1, tm2 = t_etmp[bi]
                        fw.op("act", lambda e: e.activation(s1, ps[bG1][:], AF.Sigmoid,
                                                            bias=consts[:, C_BIN + 40 + ech:C_BIN + 41 + ech]),
                              reads=[t_ps[bG1], t_consts], writes=[ts1])
                        fw.op("act", lambda e: e.activation(s2, ps[bG2][:], AF.Sigmoid,
                                                            bias=consts[:, C_BIN + 56 + ech:C_BIN + 57 + ech]),
                              reads=[t_ps[bG2], t_consts], writes=[ts2])
                        fw.op("dve", lambda e: e.tensor_tensor(m1, ps[bP1][:], s1, ALU.mult),
                              reads=[t_ps[bP1], ts1], writes=[tm1])
                        fw.op("dve", lambda e: e.tensor_tensor(m2, ps[bP2][:], s2, ALU.mult),
                              reads=[t_ps[bP2], ts2], writes=[tm2])
                        fw.op("dve", lambda e: e.tensor_tensor(mergedT[:, ech, tsl], m1, m2, ALU.add),
                              reads=[tm1, tm2], writes=t_mT[th * 4:(th + 1) * 4])
        fw.barrier()
        dump("mT", mergedT, [128, 16, 1024], t_mT, BF16)
        checkpoint("E")

        x1 = regf(143, 64).rearrange("p (a b) -> p a b", a=8)
        t_x1 = [Tile(f"x1_{i}") for i in range(8)]
        xbF = [regf(60, 8), regf(68, 8)]
        t_xbF = [Tile("xbF0"), Tile("xbF1")]
        gslotF = regf(76, 8); t_gslotF = Tile("gslotF")
        junk = regb(84, 1); t_junk = Tile("junk")
        hbfA = regb(85, 4); t_hbfA = Tile("hbfA")
        hbfB = regb(108, 4); t_hbfB = Tile("hbfB")
        gslotG = regf(52, 8); t_gslotG = Tile("gslotG")
        gslotH = regf(100, 8); t_gslotH = Tile("gslotH")
        h2T = regb(124, 16).rearrange("p (a b) -> p a b", a=16)
        t_h2T = [Tile(f"h2T{i}") for i in range(4)]
        rtmp = [regf(8, 2), regf(10, 2)]
        t_rtmp = [Tile("rtmp0"), Tile("rtmp1")]
        ssqF, t_ssqF = stat(64)

        load_g(1, gslotF, t_gslotF)
        load_g(2, gslotG, t_gslotG)

        def F_cg(half, cg):
            wts = [wload(w_o, kt * 1024, 8, cg * 512, 512) for kt in range(2)]
            banks = [(cg * 4 + tbl) % 6 for tbl in range(4)]
            for kt in range(2):
                wv, t_w = wts[kt]
                for tbl in range(4):
                    tb = half * 4 + tbl
                    b = banks[tbl]
                    for kc in range(8):
                        fw.op("pe", lambda e: e.matmul(ps[b][:], mergedT[:, kt * 8 + kc, tb * 128:(tb + 1) * 128],
                                                       wv[:, kc, :], start=(kt == 0 and kc == 0),
                                                       stop=(kt == 1 and kc == 7), skip_group_check=True),
                              reads=t_w + [t_mT[tb]], writes=[t_ps[b]], inc=(kt == 1 and kc == 7))
            for tbl in range(4):
                tb = half * 4 + tbl
                b = banks[tbl]
                fw.op("dve", lambda e: e.tensor_copy(x1[:, tb, cg * 512:(cg + 1) * 512], ps[b][:]),
                      reads=[t_ps[b]], writes=[t_x1[tb]])
                fw.op("act", lambda e: e.activation(junk, x1[:, tb, cg * 512:(cg + 1) * 512], AF.Square,
                                                    accum_out=ssqF[:, tb * 8 + cg: tb * 8 + cg + 1]),
                      reads=[t_x1[tb]], writes=[t_junk, t_ssqF])

        def F_post(half):
            for tbl in range(4):
                tb = half * 4 + tbl
                i = tb % 2
                fw.dma("sp", f"xbF{i}", xbF[i], xin[NT + tb * 128: NT + (tb + 1) * 128, :], writes=[t_xbF[i]])
                ss, t_ss = stat()
                rs, t_rs = stat()
                fw.op("dve", lambda e: e.tensor_reduce(ss, ssqF[:, tb * 8:tb * 8 + 4], mybir.AxisListType.X, ALU.add),
                      reads=[t_ssqF], writes=[t_ss])
                rstd_from_ss(ss, t_ss, 1.0 / D, rs, t_rs)
                fw.op("dve", lambda e: e.scalar_tensor_tensor(x1[:, tb, :], x1[:, tb, :], rs, gslotF, ALU.mult, ALU.mult),
                      reads=[t_x1[tb], t_rs, t_gslotF], writes=[t_x1[tb]])
                fw.op("dve", lambda e: e.tensor_tensor(x1[:, tb, :], x1[:, tb, :], xbF[i], ALU.add),
                      reads=[t_x1[tb], t_xbF[i]], writes=[t_x1[tb]])

        for cg in range(4):
            F_cg(0, cg)
        F_post(0)
        for cg in range(4):
            tbl = cg
            prenorm(x1[:, tbl, :], t_x1[tbl], hbfA, t_hbfA, gslotG, t_gslotG)
            F_cg(1, cg)
            transpose_block(hbfA, t_hbfA, h2T, t_h2T[tbl], tbl * 128, banks=(6, 7))
        F_post(1)
        h2Tb = regb(108, 16).rearrange("p (a b) -> p a b", a=16)
        t_h2Tb = [Tile(f"h2Tb{i}") for i in range(4)]
        fT = regb(92, 16).rearrange("p (a b) -> p a b", a=8)
        t_fT = [[Tile(f"fT{c}_{h}") for h in range(2)] for c in range(8)]
        fw.fence(t_mT, t_h2Tb + [t for r in t_fT for t in r])
        h2Th = [h2T, h2Tb]
        t_h2Th = [t_h2T, t_h2Tb]
        t_out = [Tile(f"out{i}") for i in range(8)]
        for tb in range(8):
            fw.dma("sp", f"outA{tb}", out_d[tb * 128:(tb + 1) * 128, :], x1[:, tb, :],
                   reads=[t_x1[tb]], writes=[t_out[tb]])
        dump("x1", regf(143, 64), [128, 8 * 2048], t_x1)
        dump("h2T", h2T, [128, 16, 512], t_h2T, BF16)
        checkpoint("F")

        y2, t_y2 = x1, t_x1
        gslotH = regf(60, 8); t_gslotH = Tile("gslotH")
        junk2 = regb(68, 4); t_junk2 = Tile("junk2")
        fw.fence(t_xbF + [t_gslotF, t_junk, t_hbfA], [t_gslotH, t_junk2])
        load_g(3, gslotH, t_gslotH)
        for e8 in range(8):
            def up_chunk(wv, t_w, ti, ec, th, b):
                ch = ti * 2 + ec
                for kc in range(16):
                    fw.op("pe", lambda e: e.matmul(ps[b][:], wv[:, kc, ec * 128:(ec + 1) * 128],
                                                   h2Th[th][:, kc, :], start=(kc == 0), stop=(kc == 15)),
                          reads=t_w + t_h2Th[th], writes=[t_ps[b]], inc=(kc == 15))
                fw.op("act", lambda e: e.activation(rtmp[th], ps[b][:], AF.Relu),
                      reads=[t_ps[b]], writes=[t_rtmp[th]])
                fw.op("dve", lambda e: e.tensor_tensor(fT[:, ch, th * 512:(th + 1) * 512], rtmp[th], rtmp[th],
                                                       ALU.mult),
                      reads=[t_rtmp[th]], writes=[t_fT[ch][th]])

            if e8 == 0:
                tiles = [wload(w_up, 0, 16, ti * 256, 256) for ti in range(4)]
                prenorm(x1[:, 4, :], t_x1[4], hbfA, t_hbfA, gslotG, t_gslotG)
                for ti in range(4):
                    for ec in range(2):
                        up_chunk(tiles[ti][0], tiles[ti][1], ti, ec, 0, 4 + ec)
                    transpose_block(hbfA, t_hbfA, h2Tb, t_h2Tb[ti], ti * 128, banks=(6, 7))
                    if ti < 3:
                        prenorm(x1[:, 5 + ti, :], t_x1[5 + ti], hbfA, t_hbfA, gslotG, t_gslotG)
                for ti in range(4):
                    for ec in range(2):
                        up_chunk(tiles[ti][0], tiles[ti][1], ti, ec, 1, 4 + (ti * 2 + ec) % 4)
            else:
                for ti in range(4):
                    wv, t_w = wload(w_up, 0, 16, e8 * 1024 + ti * 256, 256)
                    for ec in range(2):
                        ch = ti * 2 + ec
                        banks = [4 + (ch % 2) * 2, 5 + (ch % 2) * 2]
                        for kc in range(16):
                            for th in range(2):
                                b = banks[th]
                                fw.op("pe", lambda e: e.matmul(ps[b][:], wv[:, kc, ec * 128:(ec + 1) * 128],
                                                               h2Th[th][:, kc, :], start=(kc == 0), stop=(kc == 15)),
                                      reads=t_w + t_h2Th[th], writes=[t_ps[b]], inc=(kc == 15))
                        for th in range(2):
                            b = banks[th]
                            fw.op("act", lambda e: e.activation(rtmp[th], ps[b][:], AF.Relu),
                                  reads=[t_ps[b]], writes=[t_rtmp[th]])
                            fw.op("dve", lambda e: e.tensor_tensor(fT[:, ch, th * 512:(th + 1) * 512], rtmp[th],
                                                                   rtmp[th], ALU.mult),
                                  reads=[t_rtmp[th]], writes=[t_fT[ch][th]])
            def down_group(wv, t_w, cg, tbh):
                for tbl in range(4):
                    tb = tbh * 4 + tbl
                    b = tbl
                    for kc in range(8):
                        fw.op("pe", lambda e: e.matmul(ps[b][:], fT[:, kc, tb * 128:(tb + 1) * 128], wv[:, kc, :],
                                                       start=(kc == 0), stop=(kc == 7)),
                              reads=t_w + [t_fT[kc][tbh]], writes=[t_ps[b]], inc=(kc == 7))
                for tbl in range(4):
                    tb = tbh * 4 + tbl
                    dst = y2[:, tb, cg * 512:(cg + 1) * 512]
                    if e8 == 0:
                        fw.op("act", lambda e: e.activation(dst, ps[tbl][:], AF.Copy),
                              reads=[t_ps[tbl]], writes=[t_y2[tb]])
                    else:
                        fw.op("dve", lambda e: e.tensor_tensor(dst, ps[tbl][:], dst, ALU.add),
                              reads=[t_ps[tbl], t_y2[tb]], writes=[t_y2[tb]])

            def post_out(tb):
                ss, t_ss = stat()
                rs, t_rs = stat()
                fw.op("act", lambda e: e.activation(junk2, y2[:, tb, :], AF.Square, accum_out=ss),
                      reads=[t_y2[tb]], writes=[t_junk2, t_ss])
                rstd_from_ss(ss, t_ss, 1.0 / D, rs, t_rs)
                fw.op("dve", lambda e: e.scalar_tensor_tensor(y2[:, tb, :], y2[:, tb, :], rs, gslotH, ALU.mult, ALU.mult),
                      reads=[t_y2[tb], t_rs, t_gslotH], writes=[t_y2[tb]])
                fw.dma("pool", f"outB{tb}", out_d[tb * 128:(tb + 1) * 128, :], y2[:, tb, :],
                       reads=[t_y2[tb]], writes=[t_out[tb]], accum_op=ALU.add)

            if e8 < 7:
                for cg in range(4):
                    wv, t_w = wload(w_dn, e8 * 1024, 8, cg * 512, 512)
                    for tbh in range(2):
                        down_group(wv, t_w, cg, tbh)
            else:
                wts = [wload(w_dn, e8 * 1024, 8, cg * 512, 512) for cg in range(4)]
                for tbh in range(2):
                    for cg in range(4):
                        down_group(wts[cg][0], wts[cg][1], cg, tbh)
                    for tbl in range(4):
                        post_out(tbh * 4 + tbl)
        t_y2 = t_out
        fw.wait_tiles("sp", t_y2)
        print("[marks]", fw.marks)
        print(f"[build] insts={fw.n_inst} waits={fw.n_wait} wtiles={wcount[0]} "
              f"counts={ {k: e.count for k, e in fw.engs.items()} }")
    return nc, dbg_out


def make_in_maps(x, g_pre_mix, w_in, b_in, w_dw, b_dw, g_conv_ln, b_conv_ln, w_sb_out, w_conv_out, w_o,
                 g_post_mix, g_pre_mlp, w_up, w_down, g_post_mlp):
    f = lambda a: np.ascontiguousarray(np.asarray(a, dtype=np.float32))
    x = f(x)
    b_in0 = f(b_in)[0]
    consts = np.zeros((128, 384), np.float32)
    consts[:, 0:72] = b_in0.reshape(72, 128).T
    wdw = f(w_dw)[0]
    consts[:, 72:320] = wdw.reshape(31, 8, 128).transpose(2, 1, 0).reshape(128, 248)
    consts[:, 320:328] = f(b_dw)[0].reshape(8, 128).T
    consts[:, 328:336] = f(g_conv_ln)[0].reshape(8, 128).T
    consts[:, 336:344] = f(b_conv_ln)[0].reshape(8, 128).T
    gv = np.stack([f(g_pre_mix)[0], f(g_post_mix)[0], f(g_pre_mlp)[0], f(g_post_mlp)[0]])
    gvec = np.ascontiguousarray(np.broadcast_to(gv[:, None, :], (4, 128, D)))
    bv = np.ascontiguousarray(np.broadcast_to(b_in0[None, 2048:3072], (128, 1024)))
    shared = {
        "gvec": gvec, "bv": bv, "w_in": f(w_in)[0], "w_sbo": f(w_sb_out)[0], "w_cvo": f(w_conv_out)[0],
        "w_o": f(w_o)[0], "w_up": f(w_up)[0], "w_dn": f(w_down)[0],
    }
    in_maps = []
    for c in range(8):
        b, half = c // 2, c % 2
        xi = np.zeros((2 * NT, D), np.float32)
        if half == 1:
            xi[:NT] = x[b, :NT]
        xi[NT:] = x[b, half * NT:(half + 1) * NT]
        cc = consts.copy()
        cc[:, 344] = float(half)
        m = dict(shared)
        m["xin"] = xi
        m["consts"] = cc
        in_maps.append(m)
    return in_maps


_NC_CACHE = {}


def kernel(**inputs):
    if "nc" not in _NC_CACHE:
        _NC_CACHE["nc"] = build(False)[0]
    nc = _NC_CACHE["nc"]
    in_maps = make_in_maps(**inputs)
    res = run_bass_kernel_spmd(nc, in_maps, core_ids=list(range(8)))
    out = np.zeros((4, 2048, D), np.float32)
    for c in range(8):
        b, half = c // 2, c % 2
        out[b, half * NT:(half + 1) * NT] = res.results[c]["out"]
    return out
```
